# Optimizing a Trainium2 kernel written in Bass

```python
import math
import jax, jax.numpy as jnp
from jax import lax
import numpy as np

D_MODEL = 1024
BATCH = 4
SEQ = 8192
DEPTH = 4
DEC_BATCH = 8
DEC_SEQ = 2048
PAST_LEN = 128

GRID_W = 64
Q_BLOCK = 128
EPS = 1e-6
D_FF = 2816
D_PLE = 256
ROPE_BASE = 10000.0

A_HEADS = 8
A_KV_HEADS = 2
A_HEAD_DIM = 64
B_HEADS = 8
B_NOPE = 64
B_ROPE = 32
B_VDIM = 64
B_Q_RANK = 256
B_KV_RANK = 128
C_HEADS = 8
C_HEAD_DIM = 64

N_EVEN = (DEPTH + 1) // 2
N_ODD = DEPTH // 2

AB_SPLITS = (A_HEADS * A_HEAD_DIM, A_KV_HEADS * A_HEAD_DIM, A_KV_HEADS * A_HEAD_DIM,
             B_Q_RANK, B_KV_RANK, B_ROPE)
AB_IN = sum(AB_SPLITS)
AB_OUT = A_HEADS * A_HEAD_DIM + B_HEADS * B_VDIM
C_IN = 3 * C_HEADS * 2 * C_HEAD_DIM
C_OUT = C_HEADS * 2 * C_HEAD_DIM

kernel_name = "hybrid_gqa_mla_diffattn_macaron_encoder"


def rmsnorm(x, g):
    xf = x.astype(jnp.float32)
    y = xf * lax.rsqrt(jnp.mean(xf * xf, axis=-1, keepdims=True) + EPS)
    return (y * g.astype(jnp.float32)).astype(x.dtype)


def swiglu(x, wg, wu, wd):
    return (jax.nn.silu(x @ wg) * (x @ wu)) @ wd


def rope(x, pos):
    d = x.shape[-1]
    half = d // 2
    inv = ROPE_BASE ** (-jnp.arange(half, dtype=jnp.float32) * (2.0 / d))
    ang = pos.astype(jnp.float32)[:, None] * inv[None, :]
    cos, sin = jnp.cos(ang), jnp.sin(ang)
    xf = x.astype(jnp.float32)
    x1, x2 = xf[..., :half], xf[..., half:]
    return jnp.concatenate([x1 * cos - x2 * sin, x2 * cos + x1 * sin], axis=-1).astype(x.dtype)


def axial_rope(x, row, col):
    h = x.shape[-1] // 2
    return jnp.concatenate([rope(x[..., :h], row), rope(x[..., h:], col)], axis=-1)


def to_heads(t, n_heads):
    b, s, _ = t.shape
    return t.reshape(b, s, n_heads, -1).transpose(0, 2, 1, 3)


def from_heads(t):
    b, h, s, d = t.shape
    return t.transpose(0, 2, 1, 3).reshape(b, s, h * d)


def sweep_query_blocks(block_fn, n_q):
    starts = jnp.arange(n_q // Q_BLOCK, dtype=jnp.int32) * Q_BLOCK
    out = lax.map(block_fn, starts)
    nb, b, h, qb, dv = out.shape
    return jnp.moveaxis(out, 0, 2).reshape(b, h, nb * qb, dv)


def alibi_slopes():
    return 2.0 ** (-8.0 * jnp.arange(1, C_HEADS + 1, dtype=jnp.float32) / C_HEADS)


def mixer_gqa_mla(h, w_in, a_qn, a_kn, b_qn, b_wuq, b_kvn, b_wukv, w_out, row, col, tpos):
    b, s, _ = h.shape
    z = h @ w_in
    cuts = []
    acc = 0
    for w in AB_SPLITS[:-1]:
        acc += w
        cuts.append(acc)
    za_q, za_k, za_v, zb_cq, zb_ckv, zb_kr = jnp.split(z, cuts, axis=-1)

    qa = axial_rope(rmsnorm(to_heads(za_q, A_HEADS), a_qn), row, col)
    ka = axial_rope(rmsnorm(to_heads(za_k, A_KV_HEADS), a_kn), row, col)
    va = to_heads(za_v, A_KV_HEADS)
    qa = qa.reshape(b, A_KV_HEADS, A_HEADS // A_KV_HEADS, s, A_HEAD_DIM)
    scale_a = A_HEAD_DIM ** -0.5

    def blk_a(start):
        qblk = lax.dynamic_slice_in_dim(qa, start, Q_BLOCK, axis=3)
        sc = jnp.einsum('bkgqd,bksd->bkgqs', qblk, ka).astype(jnp.float32) * scale_a
        pr = jax.nn.softmax(sc, axis=-1).astype(va.dtype)
        o = jnp.einsum('bkgqs,bksd->bkgqd', pr, va)
        return o.reshape(b, A_HEADS, Q_BLOCK, A_HEAD_DIM)

    oa = sweep_query_blocks(blk_a, s)

    qb_all = to_heads(rmsnorm(zb_cq, b_qn) @ b_wuq, B_HEADS)
    q_nope = qb_all[..., :B_NOPE]
    q_rot = rope(qb_all[..., B_NOPE:], tpos)
    kv = to_heads(rmsnorm(zb_ckv, b_kvn) @ b_wukv, B_HEADS)
    k_nope = kv[..., :B_NOPE]
    vb = kv[..., B_NOPE:]
    k_rot = rope(zb_kr, tpos)
    scale_b = (B_NOPE + B_ROPE) ** -0.5

    def blk_b(start):
        qn = lax.dynamic_slice_in_dim(q_nope, start, Q_BLOCK, axis=2)
        qr = lax.dynamic_slice_in_dim(q_rot, start, Q_BLOCK, axis=2)
        sc = (jnp.einsum('bhqd,bhsd->bhqs', qn, k_nope)
              + jnp.einsum('bhqr,bsr->bhqs', qr, k_rot)).astype(jnp.float32) * scale_b
        pr = jax.nn.softmax(sc, axis=-1).astype(vb.dtype)
        return jnp.einsum('bhqs,bhsd->bhqd', pr, vb)

    ob = sweep_query_blocks(blk_b, s)

    o = jnp.concatenate([from_heads(oa), from_heads(ob)], axis=-1)
    return o @ w_out


def mixer_diff(h, w_in, lq1, lk1, lq2, lk2, sub_g, w_out, lambda_init, slopes):
    b, s, _ = h.shape
    q, k, v = jnp.split(h @ w_in, 3, axis=-1)
    q = to_heads(q, C_HEADS)
    k = to_heads(k, C_HEADS)
    v = to_heads(v, C_HEADS)
    q1, q2 = q[..., :C_HEAD_DIM], q[..., C_HEAD_DIM:]
    k1, k2 = k[..., :C_HEAD_DIM], k[..., C_HEAD_DIM:]
    lam = (jnp.exp(jnp.sum(lq1.astype(jnp.float32) * lk1.astype(jnp.float32)))
           - jnp.exp(jnp.sum(lq2.astype(jnp.float32) * lk2.astype(jnp.float32)))
           + lambda_init)
    kpos = jnp.arange(s, dtype=jnp.float32)
    scale = C_HEAD_DIM ** -0.5

    def blk(start):
        q1b = lax.dynamic_slice_in_dim(q1, start, Q_BLOCK, axis=2)
        q2b = lax.dynamic_slice_in_dim(q2, start, Q_BLOCK, axis=2)
        qpos = start.astype(jnp.float32) + jnp.arange(Q_BLOCK, dtype=jnp.float32)
        bias = -slopes[:, None, None] * jnp.abs(qpos[:, None] - kpos[None, :])[None]
        s1 = jnp.einsum('bhqd,bhsd->bhqs', q1b, k1).astype(jnp.float32) * scale + bias
        s2 = jnp.einsum('bhqd,bhsd->bhqs', q2b, k2).astype(jnp.float32) * scale + bias
        a = jax.nn.softmax(s1, axis=-1) - lam * jax.nn.softmax(s2, axis=-1)
        return jnp.einsum('bhqs,bhsd->bhqd', a.astype(v.dtype), v)

    o = sweep_query_blocks(blk, s)
    o = rmsnorm(o, sub_g) * (1.0 - lambda_init)
    return from_heads(o) @ w_out


def trunk(x, p, ffn1_norm, ffn1_wg, ffn1_wu, ffn1_wd, mix_norm,
          ab_w_in, a_q_norm, a_k_norm, b_q_norm, b_w_uq, b_kv_norm, b_w_ukv, ab_w_out,
          c_w_in, c_lambda_q1, c_lambda_k1, c_lambda_q2, c_lambda_k2, c_sub_norm, c_w_out,
          ffn2_norm, ffn2_wg, ffn2_wu, ffn2_wd, ple_norm, ple_w_gate, ple_w_proj, final_norm):
    s = x.shape[1]
    rows = s // GRID_W
    row = jnp.repeat(jnp.arange(rows, dtype=jnp.int32), GRID_W)
    col = jnp.tile(jnp.arange(GRID_W, dtype=jnp.int32), rows)
    tpos = jnp.arange(s, dtype=jnp.int32)
    slopes = alibi_slopes()
    for i in range(DEPTH):
        x = x + 0.5 * swiglu(rmsnorm(x, ffn1_norm[i]), ffn1_wg[i], ffn1_wu[i], ffn1_wd[i])
        h = rmsnorm(x, mix_norm[i])
        if i % 2 == 0:
            e = i // 2
            x = x + mixer_gqa_mla(h, ab_w_in[e], a_q_norm[e], a_k_norm[e], b_q_norm[e], b_w_uq[e],
                                  b_kv_norm[e], b_w_ukv[e], ab_w_out[e], row, col, tpos)
        else:
            o = i // 2
            lambda_init = 0.8 - 0.6 * math.exp(-0.3 * i)
            x = x + mixer_diff(h, c_w_in[o], c_lambda_q1[o], c_lambda_k1[o], c_lambda_q2[o],
                               c_lambda_k2[o], c_sub_norm[o], c_w_out[o], lambda_init, slopes)
        x = x + 0.5 * swiglu(rmsnorm(x, ffn2_norm[i]), ffn2_wg[i], ffn2_wu[i], ffn2_wd[i])
        gate = jax.nn.sigmoid(rmsnorm(x, ple_norm[i]) @ ple_w_gate[i])
        x = x + gate * (p[i] @ ple_w_proj[i])
    return rmsnorm(x, final_norm)


def setup_inputs(seed: int = 0) -> dict:
    key = jax.random.key(seed)
    ks = iter(jax.random.split(key, 64))
    f32 = jnp.float32

    def w(shape, fan_in):
        return jax.random.normal(next(ks), shape, f32) * (fan_in ** -0.5)

    def gain(shape):
        return 1.0 + 0.1 * jax.random.normal(next(ks), shape, f32)

    def small(shape):
        return 0.1 * jax.random.normal(next(ks), shape, f32)

    return {
        "x_prompt": jax.random.normal(next(ks), (BATCH, SEQ, D_MODEL), f32),
        "x_sample": jax.random.normal(next(ks), (DEC_BATCH, DEC_SEQ, D_MODEL), f32),
        "p_prompt": jax.random.normal(next(ks), (DEPTH, BATCH, SEQ, D_PLE), f32),
        "p_sample": jax.random.normal(next(ks), (DEPTH, DEC_BATCH, DEC_SEQ, D_PLE), f32),
        "ffn1_norm": gain((DEPTH, D_MODEL)),
        "ffn1_wg": w((DEPTH, D_MODEL, D_FF), D_MODEL),
        "ffn1_wu": w((DEPTH, D_MODEL, D_FF), D_MODEL),
        "ffn1_wd": w((DEPTH, D_FF, D_MODEL), D_FF),
        "mix_norm": gain((DEPTH, D_MODEL)),
        "ab_w_in": w((N_EVEN, D_MODEL, AB_IN), D_MODEL),
        "a_q_norm": gain((N_EVEN, A_HEAD_DIM)),
        "a_k_norm": gain((N_EVEN, A_HEAD_DIM)),
        "b_q_norm": gain((N_EVEN, B_Q_RANK)),
        "b_w_uq": w((N_EVEN, B_Q_RANK, B_HEADS * (B_NOPE + B_ROPE)), B_Q_RANK),
        "b_kv_norm": gain((N_EVEN, B_KV_RANK)),
        "b_w_ukv": w((N_EVEN, B_KV_RANK, B_HEADS * (B_NOPE + B_VDIM)), B_KV_RANK),
        "ab_w_out": w((N_EVEN, AB_OUT, D_MODEL), AB_OUT),
        "c_w_in": w((N_ODD, D_MODEL, C_IN), D_MODEL),
        "c_lambda_q1": small((N_ODD, C_HEAD_DIM)),
        "c_lambda_k1": small((N_ODD, C_HEAD_DIM)),
        "c_lambda_q2": small((N_ODD, C_HEAD_DIM)),
        "c_lambda_k2": small((N_ODD, C_HEAD_DIM)),
        "c_sub_norm": gain((N_ODD, 2 * C_HEAD_DIM)),
        "c_w_out": w((N_ODD, C_OUT, D_MODEL), C_OUT),
        "ffn2_norm": gain((DEPTH, D_MODEL)),
        "ffn2_wg": w((DEPTH, D_MODEL, D_FF), D_MODEL),
        "ffn2_wu": w((DEPTH, D_MODEL, D_FF), D_MODEL),
        "ffn2_wd": w((DEPTH, D_FF, D_MODEL), D_FF),
        "ple_norm": gain((DEPTH, D_MODEL)),
        "ple_w_gate": w((DEPTH, D_MODEL, D_MODEL), D_MODEL),
        "ple_w_proj": w((DEPTH, D_PLE, D_MODEL), D_PLE),
        "final_norm": gain((D_MODEL,)),
    }


def reference(x_prompt, x_sample, p_prompt, p_sample, ffn1_norm, ffn1_wg, ffn1_wu, ffn1_wd,
              mix_norm, ab_w_in, a_q_norm, a_k_norm, b_q_norm, b_w_uq, b_kv_norm, b_w_ukv,
              ab_w_out, c_w_in, c_lambda_q1, c_lambda_k1, c_lambda_q2, c_lambda_k2, c_sub_norm,
              c_w_out, ffn2_norm, ffn2_wg, ffn2_wu, ffn2_wd, ple_norm, ple_w_gate, ple_w_proj,
              final_norm):
    y_prompt = trunk(x_prompt, p_prompt, ffn1_norm, ffn1_wg, ffn1_wu, ffn1_wd, mix_norm,
                     ab_w_in, a_q_norm, a_k_norm, b_q_norm, b_w_uq, b_kv_norm, b_w_ukv, ab_w_out,
                     c_w_in, c_lambda_q1, c_lambda_k1, c_lambda_q2, c_lambda_k2, c_sub_norm, c_w_out,
                     ffn2_norm, ffn2_wg, ffn2_wu, ffn2_wd, ple_norm, ple_w_gate, ple_w_proj, final_norm)
    y_sample = trunk(x_sample, p_sample, ffn1_norm, ffn1_wg, ffn1_wu, ffn1_wd, mix_norm,
                     ab_w_in, a_q_norm, a_k_norm, b_q_norm, b_w_uq, b_kv_norm, b_w_ukv, ab_w_out,
                     c_w_in, c_lambda_q1, c_lambda_k1, c_lambda_q2, c_lambda_k2, c_sub_norm, c_w_out,
                     ffn2_norm, ffn2_wg, ffn2_wu, ffn2_wd, ple_norm, ple_w_gate, ple_w_proj, final_norm)
    return (y_prompt, y_sample)
```

```python
import math
from contextlib import ExitStack

import numpy as np
import concourse.bass as bass
import concourse.mybir as mybir
from concourse.bass_utils import run_bass_kernel_spmd

F32 = mybir.dt.float32
BF16 = mybir.dt.bfloat16
AF = mybir.ActivationFunctionType
ALU = mybir.AluOpType

DEPTH = 4
D = 1024
DFF = 2816
NT = 8192
TT = 512
NTILE = NT // TT
EPS = 1e-6
NEG = -30000.0
N_UNITS = 6

DBG = {"passes": 5, "attn": True}


class DSem:
    def __init__(self, handle):
        self.handle = handle
        self.total = 0
        self.last = None


class Op:
    __slots__ = ("eng", "fn", "deps", "is_dma", "sem", "val", "needs_inc", "n")

    def __init__(self, eng, fn, deps, is_dma=False, sem=None, n=1):
        self.eng = eng
        self.fn = fn
        self.deps = deps
        self.is_dma = is_dma
        self.sem = sem
        self.val = 0
        self.needs_inc = False
        self.n = n


class Buf:
    def __init__(self, t, dsem=None):
        self.t = t
        self.w = None
        self.r = []
        self.dsem = dsem

    def __getitem__(self, k):
        return self.t[k]


class Prog:
    ENGS = ("pe", "act", "dve", "pool", "sp")

    def __init__(self, nc, esems, dsem_handles):
        self.nc = nc
        self.ops = {e: [] for e in self.ENGS}
        self.esem = esems
        self.free_dsems = [DSem(h) for h in dsem_handles]
        self.all_dsems = list(self.free_dsems)
        self.last_barrier = None

    def new_dsem(self):
        return self.free_dsems.pop()

    def release_dsems(self, ds):
        self.free_dsems.extend(ds)

    def _deps(self, eng, reads, writes, extra):
        deps = []
        for b in reads:
            if b.w is not None:
                deps.append(b.w)
        for b in writes:
            for r in b.r:
                if r.eng != eng or r.is_dma:
                    deps.append(r)
            if b.w is not None and (b.w.eng != eng or b.w.is_dma):
                deps.append(b.w)
        deps.extend(d for d in extra if d is not None)
        if self.last_barrier is not None:
            deps.append(self.last_barrier)
        return deps

    def add(self, eng, fn, reads=(), writes=(), extra=()):
        op = Op(eng, fn, self._deps(eng, reads, writes, extra))
        self.ops[eng].append(op)
        for b in reads:
            b.r.append(op)
        for b in writes:
            b.w = op
            b.r = []
        return op

    def dma(self, eng, fn, sem, reads=(), writes=(), extra=(), n=1):
        deps = self._deps(eng, reads, writes, extra)
        if sem.last is not None:
            deps.append(sem.last)
        op = Op(eng, fn, deps, is_dma=True, sem=sem, n=n)
        sem.total += 16 * n
        op.val = sem.total
        sem.last = op
        self.ops[eng].append(op)
        for b in reads:
            b.r.append(op)
        for b in writes:
            b.w = op
            b.r = []
        return op

    def barrier(self, bsem, scratch_src, scratch_dst):
        deps = []
        for e in self.ENGS:
            for op in reversed(self.ops[e]):
                if not op.is_dma:
                    deps.append(op)
                    break
        for ds in self.all_dsems:
            if ds.last is not None:
                deps.append(ds.last)
        self.last_barrier = None
        op = self.dma("sp", lambda e: e.dma_start(out=scratch_dst, in_=scratch_src), bsem, extra=deps)
        self.last_barrier = op
        return op

    def emit(self, block):
        for e in self.ENGS:
            for op in self.ops[e]:
                for d in op.deps:
                    if not d.is_dma:
                        d.needs_inc = True
        for e in self.ENGS:
            c = 0
            for op in self.ops[e]:
                if not op.is_dma and op.needs_inc:
                    c += 1
                    op.val = c
        esem = self.esem

        def run(engname, eng):
            waited = {}
            for op in self.ops[engname]:
                need = {}
                for d in op.deps:
                    s = d.sem.handle if d.is_dma else esem[d.eng]
                    key = id(s)
                    if waited.get(key, 0) < d.val and need.get(key, (None, 0))[1] < d.val:
                        need[key] = (s, d.val)
                for key, (s, v) in need.items():
                    eng.wait_ge(s, v)
                    waited[key] = v
                ins = op.fn(eng)
                if op.is_dma:
                    if not isinstance(ins, (list, tuple)):
                        ins = [ins]
                    assert len(ins) == op.n, (len(ins), op.n)
                    for i_ in ins:
                        i_.then_inc(op.sem.handle, 16)
                elif op.needs_inc:
                    ins.then_inc(esem[engname], 1)

        block.tensor(lambda eng: run("pe", eng))
        block.scalar(lambda eng: run("act", eng))
        block.vector(lambda eng: run("dve", eng))
        block.gpsimd(lambda eng: run("pool", eng))
        block.sync(lambda eng: run("sp", eng))


def _gcols():
    cols = {}
    c = 0
    for i in range(DEPTH):
        for nm in ("ffn1", "mix", "ffn2", "ple"):
            cols[(nm, i)] = c
            c += 8
    cols["final"] = c
    c += 8
    for e in range(2):
        for nm, w in (("aq", 1), ("aq_sw", 1), ("ak", 1), ("ak_sw", 1), ("bq", 2), ("bkv", 1)):
            cols[(nm, e)] = c
            c += w
    for o in range(2):
        cols[("sub", o)] = c
        c += 1
    return cols, c


GCOL, NG = _gcols()

ABX = {"q": 0, "q_sw": 512, "k": 1024, "k_sw": 1152, "cq": 1280, "ckv": 1536, "kr": 1664, "kr_sw": 1696,
       "v": 1728}
ABX_N = 1856


def _swap32(n):
    idx = np.arange(n)
    j = idx % 32
    return np.where(j < 16, idx + 16, idx - 16)


def build_program():
    nc = bass.Bass("TRN2", target_bir_lowering=False)

    def dram(name, shape, dt, kind="Internal"):
        if name in DBG.get("dump", ()):
            kind = "ExternalOutput"
        return nc.dram_tensor(name, list(shape), dt, kind=kind).ap()

    x_in = dram("x", [NT, D], F32, "ExternalInput")
    p_in = dram("p", [DEPTH, NT, 256], F32, "ExternalInput")
    y_out = dram("y", [NT, D], F32, "ExternalOutput")
    wsrc = {}
    wshape = {
        "f1g": (4, D, DFF), "f1u": (4, D, DFF), "f1d": (4, DFF, D),
        "f2g": (4, D, DFF), "f2u": (4, D, DFF), "f2d": (4, DFF, D),
        "abin": (2, D, ABX_N), "wuq": (2, 256, 1024), "wukv": (2, 128, 1024), "about": (2, D, D),
        "cin": (2, D, 3072), "cout": (2, D, D), "pleg": (4, D, D), "plep": (4, 256, D),
    }
    for k, shp in wshape.items():
        wsrc[k] = dram(k, shp, F32, "ExternalInput")
    gall_in = dram("gall", [128, NG], F32, "ExternalInput")
    lamv_in = dram("lamv", [1, 2 * 4 * 64], F32, "ExternalInput")
    ropeA_in = dram("ropeA", [2, 128, NT], F32, "ExternalInput")
    ropeB_in = dram("ropeB", [2, 128, NT], F32, "ExternalInput")
    maskb_in = dram("maskb", [128, 1024], F32, "ExternalInput")
    cdist_in = dram("cdist", [128, 1024], F32, "ExternalInput")
    dtl_in = dram("dtiles", [128, 6, 512], F32, "ExternalInput")

    wt = {}
    for k, (L, K, N) in wshape.items():
        kp = min(K, 128)
        wt[k] = dram("wt_" + k, [L, kp, (K // kp) * N], BF16)
    xs = dram("xs", [8, 128, NT], F32)
    QA = dram("QA", [512, NT], BF16)
    KA = dram("KA", [128, NT], BF16)
    VA = dram("VA", [NT, 130], BF16)
    QB = dram("QB", [8, 96, NT], BF16)
    KB = dram("KB", [8, 64, NT], BF16)
    KRB = dram("KRB", [32, NT], BF16)
    VB = dram("VB", [NT, 520], BF16)
    QC = dram("QC", [1024, NT], BF16)
    KC = dram("KC", [1024, NT], BF16)
    VC = dram("VC", [NT, 1024], BF16)
    OT = dram("OT", [1024, NT], BF16)
    bar_a = dram("bar_a", [1, 64], F32)
    bar_b = dram("bar_b", [1, 64], F32)

    with ExitStack() as top:
        esems = {e: top.enter_context(nc.semaphore("es_" + e)) for e in Prog.ENGS}
        dhandles = [top.enter_context(nc.semaphore("ds%d" % i)) for i in range(60)]
        block = top.enter_context(nc.Block())
        P = Prog(nc, esems, dhandles)
        bsem = P.new_dsem()

        uid = {"n": 0}

        def sb(stack, name, shape, dt, dma=False):
            uid["n"] += 1
            t = stack.enter_context(nc.sbuf_tensor("s%d_%s" % (uid["n"], name), list(shape), dt))
            return Buf(t, P.new_dsem() if dma else None)

        def ps(stack, name):
            uid["n"] += 1
            return Buf(stack.enter_context(nc.psum_tensor("p%d_%s" % (uid["n"], name), [128, 512], F32)))

        def ACT(out, in_, func, reads, writes, **kw):
            return P.add("act", lambda e: e.activation(out=out, in_=in_, func=func, **kw), reads, writes)

        def TTo(eng, out, a, b, op, reads, writes):
            return P.add(eng, lambda e: e.tensor_tensor(out, a, b, op), reads, writes)

        def STT(eng, out, in0, scalar, in1, op0, op1, reads, writes):
            return P.add(eng, lambda e: e.scalar_tensor_tensor(out, in0, scalar, in1, op0=op0, op1=op1), reads, writes)

        def TS(eng, out, in0, s1, s2, op0, op1, reads, writes):
            return P.add(eng, lambda e: e.tensor_scalar(out, in0, s1, s2, op0=op0, op1=op1), reads, writes)

        def TSS(eng, out, in0, s, op, reads, writes):
            return P.add(eng, lambda e: e.tensor_single_scalar(out, in0, s, op), reads, writes)

        def CP(eng, out, in_, reads, writes):
            return P.add(eng, lambda e: e.tensor_copy(out, in_), reads, writes)

        def RSTD(dst, src_ps, inv_n):
            ACT(dst[:], src_ps[:], AF.Ln, [src_ps, epsb], [dst], bias=epsb[:, 0:1], scale=inv_n)
            ACT(dst[:], dst[:], AF.Exp, [dst], [dst], scale=-0.5)

        def RCP(out, in_, reads, writes):
            return P.add("dve", lambda e: e.reciprocal(out, in_), reads, writes)

        def MSET(ap, val, writes):
            return P.add("pool", lambda e: e.memset(ap, val), (), writes)

        def DMA(eng, out, in_, sem, reads=(), writes=(), extra=()):
            return P.dma(eng, lambda e: e.dma_start(out=out, in_=in_), sem, reads, writes, extra)

        def DMAN(eng, pairs, sem, reads=(), writes=(), extra=()):
            pairs = list(pairs)
            return P.dma(eng, lambda e: [e.dma_start(out=a, in_=b) for a, b in pairs], sem, reads, writes, extra,
                         n=len(pairs))

        def MM(out_ap, pairs, reads, writes, first=True, last=True):
            pairs = list(pairs)

            def fn(e):
                ins = None
                n = len(pairs)
                for i, (a, b) in enumerate(pairs):
                    ins = e.matmul(out_ap, a, b, start=(first and i == 0), stop=(last and i == n - 1))
                return ins
            return P.add("pe", fn, reads, writes)

        def TRN(pairs, reads, writes):
            pairs = list(pairs)

            def fn(e):
                ins = None
                for o_, i_ in pairs:
                    ins = e.transpose(o_, i_, ident.t[:])
                return ins
            return P.add("pe", fn, list(reads) + [ident], writes)

        gall = sb(top, "gall", [128, NG], F32, dma=True)
        ones_bf = sb(top, "ones_bf", [128, 128], BF16)
        ones_f = sb(top, "ones_f", [128, 128], F32)
        bd64 = sb(top, "bd64", [128, 128], BF16)
        ident = sb(top, "ident", [128, 128], F32)
        nlam = sb(top, "nlam", [128, 2], F32)
        subg = sb(top, "subg", [128, 2], F32)

        epsb = sb(top, "epsb", [128, 1], F32)
        MSET(epsb[:], EPS, [epsb])
        DMA("sp", gall[:], gall_in, gall.dsem, writes=[gall])
        MSET(ones_bf[:], 1.0, [ones_bf])
        MSET(ones_f[:], 1.0, [ones_f])
        MSET(bd64[:], 0.0, [bd64])
        MSET(bd64[0:64, 0:64], 1.0, [bd64])
        MSET(bd64[64:128, 64:128], 1.0, [bd64])
        P.add("pool", lambda e: e.iota(ident[:], pattern=[[1, 128]], base=0, channel_multiplier=-1,
                                       allow_small_or_imprecise_dtypes=True), writes=[ident])
        TSS("dve", ident[:], ident[:], 0.0, ALU.is_equal, [ident], [ident])

        wsem = [P.new_dsem() for _ in range(4)]
        wi = 0
        order = ["f1g", "f1u", "f1d", "abin", "wuq", "wukv", "about", "cin", "cout", "f2g", "f2u", "f2d",
                 "pleg", "plep"]
        for l in range(4):
            for k in order:
                L, K, N = wshape[k]
                if l >= L:
                    continue
                kp = min(K, 128)
                src = wsrc[k][l].rearrange("(k p) n -> p k n", p=kp)
                dst = wt[k][l].rearrange("p (k n) -> p k n", n=N)
                nk = K // kp
                step = max(1, nk // 4) if nk >= 8 else nk
                for k0 in range(0, nk, step):
                    k1 = min(nk, k0 + step)
                    DMA("pool", dst[:, k0:k1, :], src[:, k0:k1, :], wsem[wi % 4])
                    wi += 1

        with ExitStack() as st0:
            lv = sb(st0, "lv", [1, 512], F32, dma=True)
            lp = sb(st0, "lp", [1, 256], F32)
            ls = sb(st0, "ls", [1, 4], F32)
            le = sb(st0, "le", [1, 4], F32)
            ln2 = sb(st0, "ln2", [1, 2], F32)
            pst = ps(st0, "ps_pro")
            DMA("sp", lv[:], lamv_in, lv.dsem, writes=[lv])
            lvv = lv.t[:].rearrange("p (o f d) -> p o f d", o=2, f=4)
            lpv = lp.t[:].rearrange("p (o f d) -> p o f d", o=2, f=2)
            for o in range(2):
                for f in range(2):
                    TTo("dve", lpv[:, o, f, :], lvv[:, o, 2 * f, :], lvv[:, o, 2 * f + 1, :], ALU.mult, [lv], [lp])
            lp3 = lp.t[:].rearrange("p (g d) -> p g d", d=64)
            P.add("dve", lambda e: e.reduce_sum(ls[:, 0:4], lp3, axis=mybir.AxisListType.X), reads=[lp], writes=[ls])
            ACT(le[:], ls[:], AF.Exp, [ls], [le])
            lev = le.t[:].rearrange("p (o f) -> p o f", f=2)
            for o in range(2):
                li = 0.8 - 0.6 * math.exp(-0.3 * (2 * o + 1))
                STT("dve", ln2[:, o:o + 1], lev[:, o, 1:2], -li, lev[:, o, 0:1], ALU.add, ALU.subtract, [le], [ln2])
            MM(pst.t[:, 0:2], [(ones_f[0:1, :], ln2[0:1, :])], [ln2, ones_f], [pst])
            CP("dve", nlam[:], pst.t[:, 0:2], [pst], [nlam])
            for o in range(2):
                li = 0.8 - 0.6 * math.exp(-0.3 * (2 * o + 1))
                c0 = GCOL[("sub", o)]
                TSS("dve", subg[:, o:o + 1], gall[:, c0:c0 + 1], 1.0 - li, ALU.mult, [gall], [subg])
            P.barrier(bsem, bar_a, bar_b)

        def rowlocal_pass(ipass):
            with ExitStack() as st:
                RING = 5
                ring = [sb(st, "ring%d" % i, [128, 5632], BF16, dma=True) for i in range(RING)]
                xbs = [sb(st, "xb%d" % i, [128, 8, TT], F32, dma=True) for i in range(2)]
                xb = xbs[0]
                hn = sb(st, "hn", [128, 8, TT], BF16)
                act = sb(st, "act", [128, 22, TT], BF16)
                sq = sb(st, "sq", [128, 8, TT], BF16)
                rstd = sb(st, "rstd", [128, TT], F32)
                tmpf = [sb(st, "tmpf%d" % i, [128, TT], F32) for i in range(6)]
                tab = [sb(st, "tab%d" % i, [128, TT], F32, dma=True) for i in range(4)]
                otb = sb(st, "otb", [128, 8, TT], BF16, dma=True)
                xin = [sb(st, "xin%d" % i, [128, D], F32, dma=True) for i in range(2)]
                ptk = sb(st, "ptk", [128, 4, 256], F32, dma=True)
                pT = sb(st, "pT", [128, 2, TT], BF16)
                NSTG = 8
                stg = [sb(st, "stg%d" % i, [128, TT], BF16, dma=True) for i in range(NSTG)]
                vstg = [sb(st, "vstg%d" % i, [128, 520], BF16, dma=True) for i in range(2)]
                cqn_b = [sb(st, "cqn%d" % i, [128, TT], BF16) for i in range(2)]
                ckvn_b = sb(st, "ckvn", [128, TT], BF16)
                pG = [ps(st, "pG%d" % i) for i in range(2)]
                pU = [ps(st, "pU%d" % i) for i in range(2)]
                pD = [ps(st, "pD%d" % i) for i in range(2)]
                pN = ps(st, "pN")
                pTp = ps(st, "pTp")
                cnt = {"ring": 0, "stg": 0, "vstg": 0, "G": 0, "D": 0, "tmp": 0, "xin": 0}

                for v in vstg:
                    MSET(v[:], 1.0, [v])

                def slab(wname, l, K, N, c0, ncols):
                    b = ring[cnt["ring"] % RING]
                    cnt["ring"] += 1
                    kp = min(K, 128)
                    nk = K // kp
                    src = wt[wname][l].rearrange("p (k n) -> p k n", n=N)[:, :, c0:c0 + ncols]
                    view = b.t[0:kp, 0:nk * ncols].rearrange("p (k n) -> p k n", n=ncols)
                    DMA("sp", view, src, b.dsem, writes=[b])
                    return b, view

                def next_tmp():
                    t = tmpf[cnt["tmp"] % len(tmpf)]
                    cnt["tmp"] += 1
                    return t

                def next_stg():
                    s_ = stg[cnt["stg"] % NSTG]
                    cnt["stg"] += 1
                    return s_

                def next_D():
                    d_ = pD[cnt["D"] % 2]
                    cnt["D"] += 1
                    return d_

                def rstd_from(ps_buf, inv_n):
                    RSTD(rstd, ps_buf, inv_n)

                def rmsnorm_stats(src_aps, src_bufs, inv_n):
                    n = len(src_aps)
                    for c, (ap, bb) in enumerate(zip(src_aps, src_bufs)):
                        if c % 2 == 0:
                            ACT(sq[:, c, :], ap, AF.Square, [bb], [sq])
                        else:
                            TTo("dve", sq[:, c, :], ap, ap, ALU.mult, [bb], [sq])
                    MM(pN[:], [(ones_bf[:], sq[:, c, :]) for c in range(n)], [sq, ones_bf], [pN])
                    rstd_from(pN, inv_n)

                def norm_x(gcol):
                    rmsnorm_stats([xb[:, c, :] for c in range(8)], [xb] * 8, 1.0 / D)
                    for c in range(8):
                        STT("dve", hn[:, c, :], xb[:, c, :], gall[:, gcol + c:gcol + c + 1], rstd[:], ALU.mult, ALU.mult,
                            [xb, rstd, gall], [hn])

                def ffn(wg, wu, wd, l, gcol):
                    norm_x(gcol)
                    for s0 in range(0, DFF, 512):
                        ncols = min(512, DFF - s0)
                        bg, vg = slab(wg, l, D, DFF, s0, ncols)
                        bu, vu = slab(wu, l, D, DFF, s0, ncols)
                        for j in range(ncols // 128):
                            f = (s0 // 128) + j
                            G = pG[cnt["G"] % 2]
                            U = pU[cnt["G"] % 2]
                            cnt["G"] += 1
                            MM(G[:], [(vg[:, k, j * 128:(j + 1) * 128], hn[:, k, :]) for k in range(8)], [bg, hn], [G])
                            MM(U[:], [(vu[:, k, j * 128:(j + 1) * 128], hn[:, k, :]) for k in range(8)], [bu, hn], [U])
                            t = next_tmp()
                            ACT(t[:], G[:], AF.Silu, [G], [t])
                            TTo("dve", act[:, f, :], t[:], U[:], ALU.mult, [t, U], [act])
                    for m0 in range(0, D, 256):
                        bd_, vd = slab(wd, l, DFF, D, m0, 256)
                        for j in range(2):
                            m = m0 // 128 + j
                            Dp = next_D()
                            MM(Dp[:], [(vd[:, k, j * 128:(j + 1) * 128], act[:, k, :]) for k in range(22)], [bd_, act], [Dp])
                            STT("dve", xb[:, m, :], Dp[:], 0.5, xb[:, m, :], ALU.mult, ALU.add, [Dp, xb], [xb])

                def proj(wname, l, K, N, c0, ncols, rhs_aps, rhs_bufs, consume):
                    nk = len(rhs_aps)
                    done = 0
                    while done < ncols:
                        sc = min(512, ncols - done)
                        b, v = slab(wname, l, K, N, c0 + done, sc)
                        for j in range((sc + 127) // 128):
                            mcols = min(128, sc - j * 128)
                            Dp = next_D()
                            MM(Dp.t[0:mcols, :], [(v[:, k, j * 128:j * 128 + mcols], rhs_aps[k]) for k in range(nk)],
                               [b] + list(rhs_bufs), [Dp])
                            consume((done // 128) + j, Dp, mcols)
                        done += sc

                def proj_tok(wname, l, K, N, c0, ncols, lhs_fn, lhs_bufs, nk, consume):
                    b, v = slab(wname, l, K, N, c0, ncols)
                    for s in range(4):
                        Dp = next_D()
                        MM(Dp.t[:, 0:ncols], [(lhs_fn(k, s), v[:, k, :]) for k in range(nk)], [b] + list(lhs_bufs), [Dp])
                        consume(s, Dp)

                def evac_to(dst_buf, rows=128):
                    def cons(j, Dp, mcols):
                        ACT(dst_buf.t[0:rows, :], Dp.t[0:rows, :], AF.Identity, [Dp], [dst_buf])
                    return cons

                hn_aps = [hn[:, k, :] for k in range(8)]

                for ti in range(DBG.get("ntile", NTILE)):
                    t0 = ti * TT
                    tsl = slice(t0, t0 + TT)
                    xb = xbs[ti % 2]
                    if ipass == 0:
                        for s in range(4):
                            xi = xin[cnt["xin"] % 2]
                            cnt["xin"] += 1
                            DMA("sp", xi[:], x_in[t0 + s * 128:t0 + (s + 1) * 128, :], xi.dsem, writes=[xi])
                            for c0 in range(0, 8, 4):
                                TRN([(pTp.t[:, cc * 128:(cc + 1) * 128], xi.t[:, (c0 + cc) * 128:(c0 + cc + 1) * 128])
                                     for cc in range(4)], [xi], [pTp])
                                CP("dve", xb.t[:, c0:c0 + 4, s * 128:(s + 1) * 128],
                                   pTp.t[:].rearrange("p (c t) -> p c t", t=128), [pTp], [xb])
                    else:
                        DMA("sp", xb[:], xs[:, :, tsl].rearrange("c p t -> p c t"), xb.dsem, writes=[xb])

                    if ipass > 0:
                        lp_ = ipass - 1
                        DMA("sp", otb[:], OT[:, tsl].rearrange("(c p) t -> p c t", p=128), otb.dsem, writes=[otb])
                        wo = "about" if lp_ % 2 == 0 else "cout"

                        def cons_out(m, Dp, mcols):
                            TTo("dve", xb[:, m, :], Dp[:], xb[:, m, :], ALU.add, [Dp, xb], [xb])
                        proj(wo, lp_ // 2, D, D, 0, D, [otb[:, k, :] for k in range(8)], [otb], cons_out)
                        ffn("f2g", "f2u", "f2d", lp_, GCOL[("ffn2", lp_)])
                        DMA("sp", ptk[:], p_in[lp_, t0:t0 + TT, :].rearrange("(s p) d -> p s d", p=128), ptk.dsem,
                            writes=[ptk])
                        for dc in range(2):
                            TRN([(pTp.t[:, s * 128:(s + 1) * 128], ptk.t[:, s, dc * 128:(dc + 1) * 128]) for s in range(4)],
                                [ptk], [pTp])
                            CP("dve", pT[:, dc, :], pTp[:], [pTp], [pT])
                        norm_x(GCOL[("ple", lp_)])
                        for m0 in range(0, D, 512):
                            bg_, vg_ = slab("pleg", lp_, D, D, m0, 512)
                            bp_, vp_ = slab("plep", lp_, 256, D, m0, 512)
                            for j in range(4):
                                m = m0 // 128 + j
                                G = pG[cnt["G"] % 2]
                                U = pU[cnt["G"] % 2]
                                cnt["G"] += 1
                                MM(G[:], [(vg_[:, k, j * 128:(j + 1) * 128], hn[:, k, :]) for k in range(8)], [bg_, hn], [G])
                                MM(U[:], [(vp_[:, k, j * 128:(j + 1) * 128], pT[:, k, :]) for k in range(2)], [bp_, pT], [U])
                                t = next_tmp()
                                ACT(t[:], G[:], AF.Sigmoid, [G], [t])
                                t2 = next_tmp()
                                TTo("dve", t2[:], t[:], U[:], ALU.mult, [t, U], [t2])
                                TTo("dve", xb[:, m, :], t2[:], xb[:, m, :], ALU.add, [t2, xb], [xb])

                    if ipass < DEPTH:
                        li = ipass
                        ffn("f1g", "f1u", "f1d", li, GCOL[("ffn1", li)])
                        norm_x(GCOL[("mix", li)])
                        if li % 2 == 0:
                            e_ = li // 2
                            for i_, (src_, row) in enumerate(((ropeA_in, 0), (ropeA_in, 1), (ropeB_in, 0), (ropeB_in, 1))):
                                DMA("sp", tab[i_][:], src_[row, :, tsl], tab[i_].dsem, writes=[tab[i_]])
                            for (nm, nch, gq, dst) in (("q", 4, "aq", QA), ("k", 1, "ak", KA)):
                                for c in range(nch):
                                    z = next_tmp()
                                    zw = next_tmp()
                                    proj("abin", e_, D, ABX_N, ABX[nm] + c * 128, 128, hn_aps, [hn], evac_to(z))
                                    proj("abin", e_, D, ABX_N, ABX[nm + "_sw"] + c * 128, 128, hn_aps, [hn], evac_to(zw))
                                    ACT(sq[:, 0, :], z[:], AF.Square, [z], [sq])
                                    MM(pN[:], [(bd64[:], sq[:, 0, :])], [sq, bd64], [pN])
                                    rstd_from(pN, 1.0 / 64)
                                    g0 = GCOL[(gq, e_)]
                                    g1 = GCOL[(gq + "_sw", e_)]
                                    STT("dve", z[:], z[:], gall[:, g0:g0 + 1], tab[0][:], ALU.mult, ALU.mult,
                                        [z, gall, tab[0]], [z])
                                    STT("dve", zw[:], zw[:], gall[:, g1:g1 + 1], tab[1][:], ALU.mult, ALU.mult,
                                        [zw, gall, tab[1]], [zw])
                                    TTo("dve", z[:], z[:], zw[:], ALU.add, [z, zw], [z])
                                    s_ = next_stg()
                                    TTo("dve", s_[:], z[:], rstd[:], ALU.mult, [z, rstd], [s_])
                                    DMA("pool", dst[c * 128:(c + 1) * 128, tsl], s_[:], s_.dsem, reads=[s_])
                            vs = vstg[cnt["vstg"] % 2]
                            cnt["vstg"] += 1

                            def consVA(s, Dp, vs=vs, t0=t0):
                                CP("dve", vs.t[:, 0:130].rearrange("p (g d) -> p g d", d=65)[:, :, 0:64],
                                   Dp.t[:, 0:128].rearrange("p (g d) -> p g d", d=64), [Dp], [vs])
                                DMA("pool", VA[t0 + s * 128:t0 + (s + 1) * 128, :], vs.t[:, 0:130], vs.dsem, reads=[vs])
                            proj_tok("abin", e_, D, ABX_N, ABX["v"], 128,
                                     lambda k, s: hn[:, k, s * 128:(s + 1) * 128], [hn], 8, consVA)
                            cq = [next_tmp(), next_tmp()]
                            for c in range(2):
                                proj("abin", e_, D, ABX_N, ABX["cq"] + c * 128, 128, hn_aps, [hn], evac_to(cq[c]))
                            rmsnorm_stats([cq[0][:], cq[1][:]], cq, 1.0 / 256)
                            cqn = cqn_b
                            for c in range(2):
                                gq_ = GCOL[("bq", e_)] + c
                                STT("dve", cqn[c][:], cq[c][:], gall[:, gq_:gq_ + 1], rstd[:], ALU.mult, ALU.mult,
                                    [cq[c], gall, rstd], [cqn[c]])
                            ckv = next_tmp()
                            proj("abin", e_, D, ABX_N, ABX["ckv"], 128, hn_aps, [hn], evac_to(ckv))
                            rmsnorm_stats([ckv[:]], [ckv], 1.0 / 128)
                            ckvn = ckvn_b
                            gk_ = GCOL[("bkv", e_)]
                            STT("dve", ckvn[:], ckv[:], gall[:, gk_:gk_ + 1], rstd[:], ALU.mult, ALU.mult,
                                [ckv, gall, rstd], [ckvn])
                            kr = next_tmp()
                            krw = next_tmp()
                            proj("abin", e_, D, ABX_N, ABX["kr"], 32, hn_aps, [hn], evac_to(kr, 32))
                            proj("abin", e_, D, ABX_N, ABX["kr_sw"], 32, hn_aps, [hn], evac_to(krw, 32))
                            TTo("dve", kr[0:32, :], kr[0:32, :], tab[2][0:32, :], ALU.mult, [kr, tab[2]], [kr])
                            TTo("dve", krw[0:32, :], krw[0:32, :], tab[3][0:32, :], ALU.mult, [krw, tab[3]], [krw])
                            s_ = next_stg()
                            TTo("dve", s_[0:32, :], kr[0:32, :], krw[0:32, :], ALU.add, [kr, krw], [s_])
                            DMA("pool", KRB[:, tsl], s_[0:32, :], s_.dsem, reads=[s_])

                            def cons_qn(j, Dp, mcols, tsl=tsl):
                                s_ = next_stg()
                                CP("dve", s_[:], Dp[:], [Dp], [s_])
                                DMAN("pool", [(QB[2 * j + hh_, 0:64, tsl], s_.t[hh_ * 64:(hh_ + 1) * 64, :]) for hh_ in range(2)],
                                     s_.dsem, reads=[s_])
                            proj("wuq", e_, 256, 1024, 0, 512, [cqn[0][:], cqn[1][:]], cqn, cons_qn)
                            for c in range(2):
                                z = next_tmp()
                                zw = next_tmp()
                                proj("wuq", e_, 256, 1024, 512 + c * 128, 128, [cqn[0][:], cqn[1][:]], cqn, evac_to(z))
                                proj("wuq", e_, 256, 1024, 768 + c * 128, 128, [cqn[0][:], cqn[1][:]], cqn, evac_to(zw))
                                TTo("dve", z[:], z[:], tab[2][:], ALU.mult, [z, tab[2]], [z])
                                TTo("dve", zw[:], zw[:], tab[3][:], ALU.mult, [zw, tab[3]], [zw])
                                s_ = next_stg()
                                TTo("dve", s_[:], z[:], zw[:], ALU.add, [z, zw], [s_])
                                DMAN("pool", [(QB[4 * c + hh_, 64:96, tsl], s_.t[hh_ * 32:(hh_ + 1) * 32, :]) for hh_ in range(4)],
                                     s_.dsem, reads=[s_])

                            def cons_kn(j, Dp, mcols, tsl=tsl):
                                s_ = next_stg()
                                CP("dve", s_[:], Dp[:], [Dp], [s_])
                                DMAN("pool", [(KB[2 * j + hh_, :, tsl], s_.t[hh_ * 64:(hh_ + 1) * 64, :]) for hh_ in range(2)],
                                     s_.dsem, reads=[s_])
                            proj("wukv", e_, 128, 1024, 0, 512, [ckvn[:]], [ckvn], cons_kn)
                            vs = vstg[cnt["vstg"] % 2]
                            cnt["vstg"] += 1

                            def consVB(s, Dp, vs=vs, t0=t0):
                                CP("dve", vs.t[:, 0:520].rearrange("p (g d) -> p g d", d=65)[:, :, 0:64],
                                   Dp.t[:, 0:512].rearrange("p (g d) -> p g d", d=64), [Dp], [vs])
                                DMA("pool", VB[t0 + s * 128:t0 + (s + 1) * 128, :], vs.t[:, 0:520], vs.dsem, reads=[vs])
                            proj_tok("wukv", e_, 128, 1024, 512, 512,
                                     lambda k, s: ckvn[:, s * 128:(s + 1) * 128], [ckvn], 1, consVB)
                        else:
                            o_ = li // 2
                            for which, dst in ((0, QC), (1024, KC)):
                                def cons_qk(j, Dp, mcols, dst=dst, tsl=tsl):
                                    s_ = next_stg()
                                    CP("dve", s_[:], Dp[:], [Dp], [s_])
                                    DMA("pool", dst[j * 128:(j + 1) * 128, tsl], s_[:], s_.dsem, reads=[s_])
                                proj("cin", o_, D, 3072, which, 1024, hn_aps, [hn], cons_qk)
                            for half in range(2):
                                def consVC(s, Dp, half=half, t0=t0):
                                    s_ = next_stg()
                                    CP("dve", s_[:], Dp[:], [Dp], [s_])
                                    DMA("pool", VC[t0 + s * 128:t0 + (s + 1) * 128, half * 512:(half + 1) * 512], s_[:],
                                        s_.dsem, reads=[s_])
                                proj_tok("cin", o_, D, 3072, 2048 + half * 512, 512,
                                         lambda k, s: hn[:, k, s * 128:(s + 1) * 128], [hn], 8, consVC)
                        DMA("pool", xs[:, :, tsl].rearrange("c p t -> p c t"), xb[:], xb.dsem, reads=[xb])
                    else:
                        rmsnorm_stats([xb[:, c, :] for c in range(8)], [xb] * 8, 1.0 / D)
                        gcol = GCOL["final"]
                        for c in range(8):
                            STT("dve", xb[:, c, :], xb[:, c, :], gall[:, gcol + c:gcol + c + 1], rstd[:], ALU.mult, ALU.mult,
                                [xb, rstd, gall], [xb])
                        for s in range(4):
                            xi = xin[cnt["xin"] % 2]
                            cnt["xin"] += 1
                            for c0 in range(0, 8, 4):
                                TRN([(pTp.t[:, cc * 128:(cc + 1) * 128], xb.t[:, c0 + cc, s * 128:(s + 1) * 128])
                                     for cc in range(4)], [xb], [pTp])
                                CP("dve", xi.t[:, c0 * 128:(c0 + 4) * 128], pTp[:], [pTp], [xi])
                            DMA("pool", y_out[t0 + s * 128:t0 + (s + 1) * 128, :], xi[:], xi.dsem, reads=[xi])
                P.barrier(bsem, bar_a, bar_b)
                P.release_dsems([b.dsem for b in ring + xbs + [otb, ptk] + tab + xin + stg + vstg])

        def ps2(stack, name):
            uid["n"] += 1
            return Buf(stack.enter_context(nc.psum_tensor("p%d_%s" % (uid["n"], name), [128, 1024], F32)))

        def MMS(specs, reads, writes):
            specs = list(specs)

            def fn(e):
                ins = None
                for (o_, a_, b_, s0_, s1_) in specs:
                    ins = e.matmul(o_, a_, b_, start=s0_, stop=s1_)
                return ins
            return P.add("pe", fn, reads, writes)

        def attn_even(e_):
            with ExitStack() as st:
                S2 = [ps2(st, "S2_%d" % i) for i in range(2)]
                O = [ps(st, "O%d" % i) for i in range(2)]
                Bp = ps(st, "Bp")
                Kt = [sb(st, "Kt%d" % i, [128, NT], BF16, dma=True) for i in range(2)]
                Qt = [sb(st, "Qt%d" % i, [128, NT], BF16, dma=True) for i in range(2)]
                Vt = sb(st, "Vt", [128, 64, 520], BF16, dma=True)
                maskb = sb(st, "maskb", [128, 1024], F32, dma=True)
                NP_ = 3
                Pt = [sb(st, "Pt%d" % i, [128, 2 * TT], BF16) for i in range(NP_)]
                rl = [sb(st, "rl%d" % i, [128, TT], F32) for i in range(2)]
                osb = [sb(st, "osb%d" % i, [128, TT], F32) for i in range(2)]
                ostg = [sb(st, "ostg%d" % i, [128, TT], BF16, dma=True) for i in range(4)]
                DMA("sp", maskb[:], maskb_in, maskb.dsem, writes=[maskb])
                for b_ in Kt + Qt:
                    MSET(b_[:], 0.0, [b_])
                qcount = 0
                ecount = 0
                for mixer in ("A", "B"):
                    if mixer == "A":
                        Kd, scale, vw, Vsrc = 128, 64 ** -0.5, 130, VA
                    else:
                        Kd, scale, vw, Vsrc = 96, 96 ** -0.5, 520, VB
                    vsrc = Vsrc.rearrange("(t p) c -> p t c", p=128)
                    DMAN("sp", [(Vt.t[:, 16 * i:16 * (i + 1), 0:vw], vsrc[:, 16 * i:16 * (i + 1), :]) for i in range(4)],
                         Vt.dsem, writes=[Vt])
                    qbase = qcount
                    qcount += 8

                    def Kbuf(h):
                        return Kt[(h // 4) % 2] if mixer == "A" else Kt[h % 2]

                    def Qbuf(h):
                        return Qt[(qbase + h) % 2]

                    def load_head(h):
                        q_ = Qbuf(h)
                        k_ = Kbuf(h)
                        if mixer == "A":
                            g = h // 4
                            if h % 4 == 0:
                                DMA("sp", k_.t[0:64, :], KA[g * 64:(g + 1) * 64, :], k_.dsem, writes=[k_])
                            DMA("sp", q_.t[0:64, :], QA[h * 64:(h + 1) * 64, :], q_.dsem, writes=[q_])
                        else:
                            DMAN("sp", [(k_.t[0:64, :], KB[h, :, :]), (k_.t[64:96, :], KRB[:, :])], k_.dsem, writes=[k_])
                            DMA("sp", q_.t[0:96, :], QB[h, :, :], q_.dsem, writes=[q_])

                    LA = 2
                    n = 8 * 8 * 64
                    load_head(0)
                    for i in range(n + LA):
                        if i % 512 == LA and (i // 512) + 1 < 8:
                            load_head(i // 512 + 1)
                        if i < n:
                            h, qp, t = i // 512, (i % 512) // 64, i % 64
                            K_, Q_ = Kbuf(h), Qbuf(h)
                            Sb = S2[i % 2]
                            kt_ = K_.t[0:Kd, t * 128:(t + 1) * 128]
                            MMS([(Sb.t[:, 0:TT], kt_, Q_.t[0:Kd, (2 * qp) * TT:(2 * qp + 1) * TT], True, True),
                                 (Sb.t[:, TT:2 * TT], kt_, Q_.t[0:Kd, (2 * qp + 1) * TT:(2 * qp + 2) * TT], True, True)],
                                [K_, Q_], [Sb])
                            pt = Pt[i % NP_]
                            blk = (2 * qp) * 64 + t
                            ACT(pt[:], Sb[:], AF.Exp, [Sb, maskb], [pt], bias=maskb[:, blk:blk + 1], scale=scale)
                        j = i - LA
                        if j >= 0:
                            h, qp, t = j // 512, (j % 512) // 64, j % 64
                            vcol = (h // 4) * 65 if mixer == "A" else h * 65
                            orow = (0 if mixer == "A" else 512) + h * 64
                            pt = Pt[j % NP_]
                            vt_ = Vt.t[:, t, vcol:vcol + 65]
                            MMS([(O[0].t[0:65, :], vt_, pt.t[:, 0:TT], t == 0, t == 63),
                                 (O[1].t[0:65, :], vt_, pt.t[:, TT:2 * TT], t == 0, t == 63)], [pt, Vt], [O[0], O[1]])
                            if t == 63:
                                for k2 in range(2):
                                    CP("dve", osb[k2][0:65, :], O[k2].t[0:65, :], [O[k2]], [osb[k2]])
                                for k2 in range(2):
                                    RCP(rl[k2][64:65, :], osb[k2][64:65, :], [osb[k2]], [rl[k2]])
                                for k2 in range(2):
                                    MM(Bp.t[0:64, :], [(ones_f[64:65, 0:64], rl[k2][64:65, :])], [rl[k2], ones_f], [Bp])
                                    og = ostg[ecount % 4]
                                    ecount += 1
                                    TTo("dve", og[0:64, :], osb[k2][0:64, :], Bp.t[0:64, :], ALU.mult, [osb[k2], Bp], [og])
                                    qb = 2 * qp + k2
                                    DMA("pool", OT[orow:orow + 64, qb * TT:(qb + 1) * TT], og[0:64, :], og.dsem, reads=[og])
                P.barrier(bsem, bar_a, bar_b)
                P.release_dsems([b.dsem for b in Kt + Qt + [Vt, maskb] + ostg])

        ALIBI_THR = 48.0

        def attn_odd(o_):
            with ExitStack() as st:
                S = [ps(st, "S%d" % i) for i in range(4)]
                O1 = ps(st, "O1")
                O2 = ps(st, "O2")
                L1 = ps(st, "L1")
                L2 = ps(st, "L2")
                Kt = [sb(st, "Kt%d" % i, [128, NT], BF16, dma=True) for i in range(2)]
                Qt = [sb(st, "Qt%d" % i, [128, NT], BF16, dma=True) for i in range(2)]
                Vt = [sb(st, "Vt%d" % i, [128, 64, 128], BF16, dma=True) for i in range(2)]
                maskb = sb(st, "maskb", [128, 1024], F32, dma=True)
                cdist = sb(st, "cdist", [128, 1024], F32, dma=True)
                dtl = sb(st, "dtl", [128, 6, 512], F32, dma=True)
                biasC = [sb(st, "biasC%d" % i, [128, 1024], F32) for i in range(2)]
                NP_ = 3
                Pt = [sb(st, "Pt%d" % i, [128, 2 * TT], BF16) for i in range(NP_)]
                tm = [sb(st, "tm%d" % i, [128, 2 * TT], F32) for i in range(NP_)]
                r1 = sb(st, "r1", [128, TT], F32)
                r2 = sb(st, "r2", [128, TT], F32)
                o1 = sb(st, "o1", [128, TT], F32)
                o2 = sb(st, "o2", [128, TT], F32)
                sqo = sb(st, "sqo", [128, TT], BF16)
                rs = sb(st, "rs", [128, TT], F32)
                ostg = [sb(st, "ostg%d" % i, [128, TT], BF16, dma=True) for i in range(2)]
                DMA("sp", maskb[:], maskb_in, maskb.dsem, writes=[maskb])
                DMA("sp", cdist[:], cdist_in, cdist.dsem, writes=[cdist])
                DMA("sp", dtl[:], dtl_in, dtl.dsem, writes=[dtl])
                scale = 64 ** -0.5

                def load_head(h):
                    slope = 2.0 ** (-(h + 1))
                    K_, Q_, V_, bC = Kt[h % 2], Qt[h % 2], Vt[h % 2], biasC[h % 2]
                    DMA("sp", K_[:], KC[h * 128:(h + 1) * 128, :], K_.dsem, writes=[K_])
                    DMA("sp", Q_[:], QC[h * 128:(h + 1) * 128, :], Q_.dsem, writes=[Q_])
                    vsrc = VC[:, h * 128:(h + 1) * 128].rearrange("(t p) c -> p t c", p=128)
                    DMAN("sp", [(V_.t[:, 16 * i:16 * (i + 1), :], vsrc[:, 16 * i:16 * (i + 1), :]) for i in range(4)],
                         V_.dsem, writes=[V_])
                    STT("dve", bC[:], cdist[:], slope, maskb[:], ALU.mult, ALU.add, [cdist, maskb], [bC])

                items = []
                head_start = {}
                for h in range(8):
                    slope = 2.0 ** (-(h + 1))
                    head_start[len(items)] = h
                    for qb in range(16):
                        q_lo, q_hi = qb * TT, qb * TT + TT - 1
                        keep = []
                        for t in range(64):
                            s_lo, s_hi = t * 128, t * 128 + 127
                            mind = max(0, s_lo - q_hi, q_lo - s_hi)
                            if slope * mind < ALIBI_THR:
                                keep.append(t)
                        for t in keep:
                            items.append((h, qb, t, t == keep[0], t == keep[-1]))
                LA = 2
                n = len(items)
                load_head(0)
                ecount = 0
                for i in range(n + LA):
                    if (i - LA) in head_start and head_start[i - LA] + 1 < 8:
                        load_head(head_start[i - LA] + 1)
                    if i < n:
                        h, qb, t, fs, ls_ = items[i]
                        slope = 2.0 ** (-(h + 1))
                        K_, Q_, bC = Kt[h % 2], Qt[h % 2], biasC[h % 2]
                        Sa = S[(i % 2) * 2]
                        Sb = S[(i % 2) * 2 + 1]
                        MM(Sa[:], [(K_.t[0:64, t * 128:(t + 1) * 128], Q_.t[0:64, qb * TT:(qb + 1) * TT])], [K_, Q_], [Sa])
                        MM(Sb[:], [(K_.t[64:128, t * 128:(t + 1) * 128], Q_.t[64:128, qb * TT:(qb + 1) * TT])], [K_, Q_], [Sb])
                        if t < 4 * qb:
                            di = 0
                        elif t > 4 * qb + 3:
                            di = 1
                        else:
                            di = 2 + (t - 4 * qb)
                        tmb = tm[i % NP_]
                        fac = slope / scale
                        STT("dve", tmb[:, 0:TT], dtl[:, di, :], fac, Sa[:], ALU.mult, ALU.add, [Sa, dtl], [tmb])
                        STT("dve", tmb[:, TT:2 * TT], dtl[:, di, :], fac, Sb[:], ALU.mult, ALU.add, [Sb, dtl], [tmb])
                        pt = Pt[i % NP_]
                        blk = qb * 64 + t
                        ACT(pt[:], tmb[:], AF.Exp, [tmb, bC], [pt], bias=bC[:, blk:blk + 1], scale=scale)
                    j = i - LA
                    if j >= 0:
                        h, qb, t, fs, ls_ = items[j]
                        V_ = Vt[h % 2]
                        pt = Pt[j % NP_]
                        MMS([(O1.t[:, :], V_.t[:, t, :], pt.t[:, 0:TT], fs, ls_),
                             (L1.t[:, :], ones_bf.t[:, :], pt.t[:, 0:TT], fs, ls_),
                             (O2.t[:, :], V_.t[:, t, :], pt.t[:, TT:2 * TT], fs, ls_),
                             (L2.t[:, :], ones_bf.t[:, :], pt.t[:, TT:2 * TT], fs, ls_)],
                            [pt, V_, ones_bf], [O1, L1, O2, L2])
                        if ls_:
                            ACT(r1[:], L1[:], AF.Ln, [L1], [r1])
                            ACT(r2[:], L2[:], AF.Ln, [L2], [r2])
                            ACT(r1[:], r1[:], AF.Exp, [r1], [r1], scale=-1.0)
                            ACT(r2[:], r2[:], AF.Exp, [r2], [r2], scale=-1.0)
                            TTo("dve", o1[:], O1[:], r1[:], ALU.mult, [O1, r1], [o1])
                            TTo("dve", o2[:], O2[:], r2[:], ALU.mult, [O2, r2], [o2])
                            STT("dve", o1[:], o2[:], nlam[:, o_:o_ + 1], o1[:], ALU.mult, ALU.add, [o1, o2, nlam], [o1])
                            ACT(sqo[:], o1[:], AF.Square, [o1], [sqo])
                            MM(L1[:], [(ones_bf[:], sqo[:])], [sqo, ones_bf], [L1])
                            RSTD(rs, L1, 1.0 / 128)
                            og = ostg[ecount % 2]
                            ecount += 1
                            STT("dve", og[:], o1[:], subg[:, o_:o_ + 1], rs[:], ALU.mult, ALU.mult, [o1, subg, rs], [og])
                            DMA("pool", OT[h * 128:(h + 1) * 128, qb * TT:(qb + 1) * TT], og[:], og.dsem, reads=[og])
                P.barrier(bsem, bar_a, bar_b)
                P.release_dsems([b.dsem for b in Kt + Qt + Vt + [maskb, cdist, dtl] + ostg])

        for ipass in range(DBG["passes"]):
            rowlocal_pass(ipass)
            if ipass < DEPTH and DBG["attn"]:
                if ipass % 2 == 0:
                    attn_even(ipass // 2)
                else:
                    attn_odd(ipass // 2)
        P.add("sp", lambda e: e.nop(), extra=[P.last_barrier])
        P.emit(block)
    return nc


def _tables(seq_len):
    t = np.arange(NT)
    tl = t % seq_len
    row, col = tl // 64, tl % 64
    inv = (10000.0 ** (-np.arange(16, dtype=np.float32) * (2.0 / 32))).astype(np.float32)

    def cs(pos, d_idx):
        jj = d_idx % 32
        i = jj % 16
        ang = pos[None, :].astype(np.float32) * inv[i][:, None]
        c = np.cos(ang).astype(np.float32)
        s = np.sin(ang).astype(np.float32)
        s = np.where((jj < 16)[:, None], -s, s)
        return c, s
    d = np.arange(128)
    j = d % 64
    cA = np.zeros((128, NT), np.float32)
    sA = np.zeros((128, NT), np.float32)
    m_row = j < 32
    c1, s1 = cs(row, d)
    c2, s2 = cs(col, d)
    cA[m_row], sA[m_row] = c1[m_row], s1[m_row]
    cA[~m_row], sA[~m_row] = c2[~m_row], s2[~m_row]
    cB, sB = cs(tl, d)
    ropeA = np.stack([cA, sA]).astype(np.float32)
    ropeB = np.stack([cB, sB]).astype(np.float32)
    qb = np.arange(16)[:, None]
    tt = np.arange(64)[None, :]
    q0 = qb * 512
    s0 = tt * 128
    same = (q0 // seq_len) == (s0 // seq_len)
    maskb = np.where(same, 0.0, NEG).astype(np.float32).reshape(1, 1024)
    diag = (tt >= 4 * qb) & (tt <= 4 * qb + 3)
    cd = np.where(diag, 0.0, -np.abs(q0 - s0)).astype(np.float32).reshape(1, 1024)
    maskb = np.repeat(maskb, 128, 0)
    cd = np.repeat(cd, 128, 0)
    return ropeA, ropeB, maskb, cd


def _dtiles():
    p = np.arange(128)[:, None]
    j = np.arange(512)[None, :]
    tiles = [-(j - p), (j - p)] + [-np.abs(j - p - 128 * c) for c in range(4)]
    return np.ascontiguousarray(np.stack(tiles, 1).astype(np.float32))


_CACHE = {}


def kernel(**inp):
    f = lambda a: np.ascontiguousarray(np.asarray(a, dtype=np.float32))
    x_prompt, x_sample = f(inp["x_prompt"]), f(inp["x_sample"])
    p_prompt, p_sample = f(inp["p_prompt"]), f(inp["p_sample"])

    sw64 = np.concatenate([_swap32(64)])
    abin = f(inp["ab_w_in"])
    q_idx = np.arange(512)
    q_sw_idx = (q_idx // 64) * 64 + sw64[q_idx % 64]
    k_idx = 512 + np.arange(128)
    k_sw_idx = 512 + (np.arange(128) // 64) * 64 + sw64[np.arange(128) % 64]
    kr_idx = 1152 + np.arange(32)
    kr_sw_idx = 1152 + _swap32(32)
    cols = np.concatenate([q_idx, q_sw_idx, k_idx, k_sw_idx, 768 + np.arange(256), 1024 + np.arange(128),
                           kr_idx, kr_sw_idx, 640 + np.arange(128)])
    abin_x = np.ascontiguousarray(abin[:, :, cols])
    wuq = f(inp["b_w_uq"])
    hh = np.arange(8)[:, None]
    nope_idx = (hh * 96 + np.arange(64)[None, :]).reshape(-1)
    rope_idx = (hh * 96 + 64 + np.arange(32)[None, :]).reshape(-1)
    rope_sw_idx = (hh * 96 + 64 + _swap32(32)[None, :]).reshape(-1)
    wuq_x = np.ascontiguousarray(wuq[:, :, np.concatenate([nope_idx, rope_idx, rope_sw_idx])])
    wukv = f(inp["b_w_ukv"])
    kn_idx = (hh * 128 + np.arange(64)[None, :]).reshape(-1)
    v_idx = (hh * 128 + 64 + np.arange(64)[None, :]).reshape(-1)
    wukv_x = np.ascontiguousarray(wukv[:, :, np.concatenate([kn_idx, v_idx])])

    gall = np.zeros((128, NG), np.float32)

    def put(col, vec):
        v = np.asarray(vec, np.float32).reshape(-1, 128).T
        gall[:, col:col + v.shape[1]] = v
    for i in range(DEPTH):
        put(GCOL[("ffn1", i)], inp["ffn1_norm"][i])
        put(GCOL[("mix", i)], inp["mix_norm"][i])
        put(GCOL[("ffn2", i)], inp["ffn2_norm"][i])
        put(GCOL[("ple", i)], inp["ple_norm"][i])
    put(GCOL["final"], inp["final_norm"])
    for e in range(2):
        aq = np.asarray(inp["a_q_norm"][e], np.float32)
        ak = np.asarray(inp["a_k_norm"][e], np.float32)
        put(GCOL[("aq", e)], np.tile(aq, 2))
        put(GCOL[("aq_sw", e)], np.tile(aq[sw64], 2))
        put(GCOL[("ak", e)], np.tile(ak, 2))
        put(GCOL[("ak_sw", e)], np.tile(ak[sw64], 2))
        put(GCOL[("bq", e)], inp["b_q_norm"][e])
        put(GCOL[("bkv", e)], inp["b_kv_norm"][e])
    for o in range(2):
        put(GCOL[("sub", o)], inp["c_sub_norm"][o])
    lamv = np.stack([np.stack([f(inp["c_lambda_q1"])[o], f(inp["c_lambda_k1"])[o],
                               f(inp["c_lambda_q2"])[o], f(inp["c_lambda_k2"])[o]]) for o in range(2)])
    lamv = np.ascontiguousarray(lamv.reshape(1, 512))

    shared = {
        "f1g": f(inp["ffn1_wg"]), "f1u": f(inp["ffn1_wu"]), "f1d": f(inp["ffn1_wd"]),
        "f2g": f(inp["ffn2_wg"]), "f2u": f(inp["ffn2_wu"]), "f2d": f(inp["ffn2_wd"]),
        "abin": abin_x, "wuq": wuq_x, "wukv": wukv_x, "about": f(inp["ab_w_out"]),
        "cin": f(inp["c_w_in"]), "cout": f(inp["c_w_out"]),
        "pleg": f(inp["ple_w_gate"]), "plep": f(inp["ple_w_proj"]),
        "gall": gall, "lamv": lamv, "dtiles": _dtiles(),
    }
    tabs = {8192: _tables(8192), 2048: _tables(2048)}
    in_maps = []
    for core in range(8):
        u = core if core < N_UNITS else core - 2
        if u < 4:
            xu = x_prompt[u]
            pu = p_prompt[:, u]
            tb = tabs[8192]
        else:
            s = (u - 4) * 4
            xu = x_sample[s:s + 4].reshape(NT, D)
            pu = p_sample[:, s:s + 4].reshape(DEPTH, NT, 256)
            tb = tabs[2048]
        m = dict(shared)
        m["x"] = np.ascontiguousarray(xu)
        m["p"] = np.ascontiguousarray(pu)
        m["ropeA"], m["ropeB"], m["maskb"], m["cdist"] = tb
        in_maps.append(m)

    if "nc" not in _CACHE:
        _CACHE["nc"] = build_program()
    nc = _CACHE["nc"]
    res = run_bass_kernel_spmd(nc, in_maps, core_ids=list(range(8)))
    if DBG.get("dump"):
        _CACHE["res"] = res.results
    ys = [np.asarray(r["y"], dtype=np.float32) for r in res.results]
    y_prompt = np.stack(ys[0:4]).reshape(4, NT, D)
    y_sample = np.concatenate([ys[4].reshape(4, 2048, D), ys[5].reshape(4, 2048, D)], 0)
    return (y_prompt, y_sample)
```

```python
import math
from contextlib import ExitStack

import numpy as np
import concourse.bass as bass
import concourse.mybir as mybir
from concourse.bass_utils import run_bass_kernel_spmd

F32 = mybir.dt.float32
BF16 = mybir.dt.bfloat16
AF = mybir.ActivationFunctionType
ALU = mybir.AluOpType

DEPTH = 4
D = 1024
DFF = 2816
NT = 8192
TT = 512
NTILE = NT // TT
EPS = 1e-6
NEG = -30000.0
N_UNITS = 6

DBG = {"passes": 5, "attn": True}


class DSem:
    def __init__(self, handle):
        self.handle = handle
        self.total = 0
        self.last = None


class Op:
    __slots__ = ("eng", "fn", "deps", "is_dma", "sem", "val", "needs_inc", "n")

    def __init__(self, eng, fn, deps, is_dma=False, sem=None, n=1):
        self.eng = eng
        self.fn = fn
        self.deps = deps
        self.is_dma = is_dma
        self.sem = sem
        self.val = 0
        self.needs_inc = False
        self.n = n


class Buf:
    def __init__(self, t, dsem=None):
        self.t = t
        self.w = None
        self.r = []
        self.dsem = dsem

    def __getitem__(self, k):
        return self.t[k]


class Prog:
    ENGS = ("pe", "act", "dve", "pool", "sp")

    def __init__(self, nc, esems, dsem_handles):
        self.nc = nc
        self.ops = {e: [] for e in self.ENGS}
        self.esem = esems
        self.free_dsems = [DSem(h) for h in dsem_handles]
        self.all_dsems = list(self.free_dsems)
        self.last_barrier = None

    def new_dsem(self):
        return self.free_dsems.pop()

    def release_dsems(self, ds):
        self.free_dsems.extend(ds)

    def _deps(self, eng, reads, writes, extra):
        deps = []
        for b in reads:
            if b.w is not None:
                deps.append(b.w)
        for b in writes:
            for r in b.r:
                if r.eng != eng or r.is_dma:
                    deps.append(r)
            if b.w is not None and (b.w.eng != eng or b.w.is_dma):
                deps.append(b.w)
        deps.extend(d for d in extra if d is not None)
        if self.last_barrier is not None:
            deps.append(self.last_barrier)
        return deps

    def add(self, eng, fn, reads=(), writes=(), extra=()):
        op = Op(eng, fn, self._deps(eng, reads, writes, extra))
        self.ops[eng].append(op)
        for b in reads:
            b.r.append(op)
        for b in writes:
            b.w = op
            b.r = []
        return op

    def dma(self, eng, fn, sem, reads=(), writes=(), extra=(), n=1):
        deps = self._deps(eng, reads, writes, extra)
        if sem.last is not None:
            deps.append(sem.last)
        op = Op(eng, fn, deps, is_dma=True, sem=sem, n=n)
        sem.total += 16 * n
        op.val = sem.total
        sem.last = op
        self.ops[eng].append(op)
        for b in reads:
            b.r.append(op)
        for b in writes:
            b.w = op
            b.r = []
        return op

    def barrier(self, bsem, scratch_src, scratch_dst):
        deps = []
        for e in self.ENGS:
            for op in reversed(self.ops[e]):
                if not op.is_dma:
                    deps.append(op)
                    break
        for ds in self.all_dsems:
            if ds.last is not None:
                deps.append(ds.last)
        self.last_barrier = None
        op = self.dma("sp", lambda e: e.dma_start(out=scratch_dst, in_=scratch_src), bsem, extra=deps)
        self.last_barrier = op
        return op

    def emit(self, block):
        for e in self.ENGS:
            for op in self.ops[e]:
                for d in op.deps:
                    if not d.is_dma:
                        d.needs_inc = True
        for e in self.ENGS:
            c = 0
            for op in self.ops[e]:
                if not op.is_dma and op.needs_inc:
                    c += 1
                    op.val = c
        esem = self.esem

        def run(engname, eng):
            waited = {}
            for op in self.ops[engname]:
                need = {}
                for d in op.deps:
                    s = d.sem.handle if d.is_dma else esem[d.eng]
                    key = id(s)
                    if waited.get(key, 0) < d.val and need.get(key, (None, 0))[1] < d.val:
                        need[key] = (s, d.val)
                for key, (s, v) in need.items():
                    eng.wait_ge(s, v)
                    waited[key] = v
                ins = op.fn(eng)
                if op.is_dma:
                    if not isinstance(ins, (list, tuple)):
                        ins = [ins]
                    assert len(ins) == op.n, (len(ins), op.n)
                    for i_ in ins:
                        i_.then_inc(op.sem.handle, 16)
                elif op.needs_inc:
                    ins.then_inc(esem[engname], 1)

        block.tensor(lambda eng: run("pe", eng))
        block.scalar(lambda eng: run("act", eng))
        block.vector(lambda eng: run("dve", eng))
        block.gpsimd(lambda eng: run("pool", eng))
        block.sync(lambda eng: run("sp", eng))


def _gcols():
    cols = {}
    c = 0
    for i in range(DEPTH):
        for nm in ("ffn1", "mix", "ffn2", "ple"):
            cols[(nm, i)] = c
            c += 8
    cols["final"] = c
    c += 8
    for e in range(2):
        for nm, w in (("aq", 1), ("aq_sw", 1), ("ak", 1), ("ak_sw", 1), ("bq", 2), ("bkv", 1)):
            cols[(nm, e)] = c
            c += w
    for o in range(2):
        cols[("sub", o)] = c
        c += 1
    return cols, c


GCOL, NG = _gcols()

ABX = {"q": 0, "q_sw": 512, "k": 1024, "k_sw": 1152, "cq": 1280, "ckv": 1536, "kr": 1664, "kr_sw": 1696,
       "v": 1728}
ABX_N = 1856


def _swap32(n):
    idx = np.arange(n)
    j = idx % 32
    return np.where(j < 16, idx + 16, idx - 16)


def build_program():
    nc = bass.Bass("TRN2", target_bir_lowering=False)

    def dram(name, shape, dt, kind="Internal"):
        if name in DBG.get("dump", ()):
            kind = "ExternalOutput"
        return nc.dram_tensor(name, list(shape), dt, kind=kind).ap()

    x_in = dram("x", [NT, D], F32, "ExternalInput")
    p_in = dram("p", [DEPTH, NT, 256], F32, "ExternalInput")
    y_out = dram("y", [NT, D], F32, "ExternalOutput")
    wsrc = {}
    wshape = {
        "f1g": (4, D, DFF), "f1u": (4, D, DFF), "f1d": (4, DFF, D),
        "f2g": (4, D, DFF), "f2u": (4, D, DFF), "f2d": (4, DFF, D),
        "abin": (2, D, ABX_N), "wuq": (2, 256, 1024), "wukv": (2, 128, 1024), "about": (2, D, D),
        "cin": (2, D, 3072), "cout": (2, D, D), "pleg": (4, D, D), "plep": (4, 256, D),
    }
    for k, shp in wshape.items():
        wsrc[k] = dram(k, shp, F32, "ExternalInput")
    gall_in = dram("gall", [128, NG], F32, "ExternalInput")
    lamv_in = dram("lamv", [1, 2 * 4 * 64], F32, "ExternalInput")
    ropeA_in = dram("ropeA", [2, 128, NT], F32, "ExternalInput")
    ropeB_in = dram("ropeB", [2, 128, NT], F32, "ExternalInput")
    maskb_in = dram("maskb", [128, 1024], F32, "ExternalInput")
    cdist_in = dram("cdist", [128, 1024], F32, "ExternalInput")
    dtl_in = dram("dtiles", [128, 6, 512], F32, "ExternalInput")

    wt = {}
    for k, (L, K, N) in wshape.items():
        kp = min(K, 128)
        wt[k] = dram("wt_" + k, [L, kp, (K // kp) * N], BF16)
    xs = dram("xs", [8, 128, NT], F32)
    QA = dram("QA", [512, NT], BF16)
    KA = dram("KA", [128, NT], BF16)
    VA = dram("VA", [NT, 130], BF16)
    QB = dram("QB", [8, 96, NT], BF16)
    KB = dram("KB", [8, 64, NT], BF16)
    KRB = dram("KRB", [32, NT], BF16)
    VB = dram("VB", [NT, 520], BF16)
    QC = dram("QC", [1024, NT], BF16)
    KC = dram("KC", [1024, NT], BF16)
    VC = dram("VC", [NT, 1024], BF16)
    OT = dram("OT", [1024, NT], BF16)
    bar_a = dram("bar_a", [1, 64], F32)
    bar_b = dram("bar_b", [1, 64], F32)

    with ExitStack() as top:
        esems = {e: top.enter_context(nc.semaphore("es_" + e)) for e in Prog.ENGS}
        dhandles = [top.enter_context(nc.semaphore("ds%d" % i)) for i in range(60)]
        block = top.enter_context(nc.Block())
        P = Prog(nc, esems, dhandles)
        bsem = P.new_dsem()

        uid = {"n": 0}

        def sb(stack, name, shape, dt, dma=False):
            uid["n"] += 1
            t = stack.enter_context(nc.sbuf_tensor("s%d_%s" % (uid["n"], name), list(shape), dt))
            return Buf(t, P.new_dsem() if dma else None)

        def ps(stack, name):
            uid["n"] += 1
            return Buf(stack.enter_context(nc.psum_tensor("p%d_%s" % (uid["n"], name), [128, 512], F32)))

        def ACT(out, in_, func, reads, writes, **kw):
            return P.add("act", lambda e: e.activation(out=out, in_=in_, func=func, **kw), reads, writes)

        def TTo(eng, out, a, b, op, reads, writes):
            return P.add(eng, lambda e: e.tensor_tensor(out, a, b, op), reads, writes)

        def STT(eng, out, in0, scalar, in1, op0, op1, reads, writes):
            return P.add(eng, lambda e: e.scalar_tensor_tensor(out, in0, scalar, in1, op0=op0, op1=op1), reads, writes)

        def TS(eng, out, in0, s1, s2, op0, op1, reads, writes):
            return P.add(eng, lambda e: e.tensor_scalar(out, in0, s1, s2, op0=op0, op1=op1), reads, writes)

        def TSS(eng, out, in0, s, op, reads, writes):
            return P.add(eng, lambda e: e.tensor_single_scalar(out, in0, s, op), reads, writes)

        def CP(eng, out, in_, reads, writes):
            return P.add(eng, lambda e: e.tensor_copy(out, in_), reads, writes)

        def RSTD(dst, src_ps, inv_n):
            ACT(dst[:], src_ps[:], AF.Ln, [src_ps, epsb], [dst], bias=epsb[:, 0:1], scale=inv_n)
            ACT(dst[:], dst[:], AF.Exp, [dst], [dst], scale=-0.5)

        def RCP(out, in_, reads, writes):
            return P.add("dve", lambda e: e.reciprocal(out, in_), reads, writes)

        def MSET(ap, val, writes):
            return P.add("pool", lambda e: e.memset(ap, val), (), writes)

        def DMA(eng, out, in_, sem, reads=(), writes=(), extra=()):
            return P.dma(eng, lambda e: e.dma_start(out=out, in_=in_), sem, reads, writes, extra)

        def DMAN(eng, pairs, sem, reads=(), writes=(), extra=()):
            pairs = list(pairs)
            return P.dma(eng, lambda e: [e.dma_start(out=a, in_=b) for a, b in pairs], sem, reads, writes, extra,
                         n=len(pairs))

        def MM(out_ap, pairs, reads, writes, first=True, last=True):
            pairs = list(pairs)

            def fn(e):
                ins = None
                n = len(pairs)
                for i, (a, b) in enumerate(pairs):
                    ins = e.matmul(out_ap, a, b, start=(first and i == 0), stop=(last and i == n - 1))
                return ins
            return P.add("pe", fn, reads, writes)

        def TRN(pairs, reads, writes):
            pairs = list(pairs)

            def fn(e):
                ins = None
                for o_, i_ in pairs:
                    ins = e.transpose(o_, i_, ident.t[:])
                return ins
            return P.add("pe", fn, list(reads) + [ident], writes)

        gall = sb(top, "gall", [128, NG], F32, dma=True)
        ones_bf = sb(top, "ones_bf", [128, 128], BF16)
        ones_f = sb(top, "ones_f", [128, 128], F32)
        bd64 = sb(top, "bd64", [128, 128], BF16)
        ident = sb(top, "ident", [128, 128], F32)
        nlam = sb(top, "nlam", [128, 2], F32)
        subg = sb(top, "subg", [128, 2], F32)

        epsb = sb(top, "epsb", [128, 1], F32)
        MSET(epsb[:], EPS, [epsb])
        DMA("sp", gall[:], gall_in, gall.dsem, writes=[gall])
        MSET(ones_bf[:], 1.0, [ones_bf])
        MSET(ones_f[:], 1.0, [ones_f])
        MSET(bd64[:], 0.0, [bd64])
        MSET(bd64[0:64, 0:64], 1.0, [bd64])
        MSET(bd64[64:128, 64:128], 1.0, [bd64])
        P.add("pool", lambda e: e.iota(ident[:], pattern=[[1, 128]], base=0, channel_multiplier=-1,
                                       allow_small_or_imprecise_dtypes=True), writes=[ident])
        TSS("dve", ident[:], ident[:], 0.0, ALU.is_equal, [ident], [ident])

        wsem = [P.new_dsem() for _ in range(4)]
        wi = 0
        order = ["f1g", "f1u", "f1d", "abin", "wuq", "wukv", "about", "cin", "cout", "f2g", "f2u", "f2d",
                 "pleg", "plep"]
        for l in range(4):
            for k in order:
                L, K, N = wshape[k]
                if l >= L:
                    continue
                kp = min(K, 128)
                src = wsrc[k][l].rearrange("(k p) n -> p k n", p=kp)
                dst = wt[k][l].rearrange("p (k n) -> p k n", n=N)
                nk = K // kp
                step = max(1, nk // 4) if nk >= 8 else nk
                for k0 in range(0, nk, step):
                    k1 = min(nk, k0 + step)
                    DMA("pool", dst[:, k0:k1, :], src[:, k0:k1, :], wsem[wi % 4])
                    wi += 1

        with ExitStack() as st0:
            lv = sb(st0, "lv", [1, 512], F32, dma=True)
            lp = sb(st0, "lp", [1, 256], F32)
            ls = sb(st0, "ls", [1, 4], F32)
            le = sb(st0, "le", [1, 4], F32)
            ln2 = sb(st0, "ln2", [1, 2], F32)
            pst = ps(st0, "ps_pro")
            DMA("sp", lv[:], lamv_in, lv.dsem, writes=[lv])
            lvv = lv.t[:].rearrange("p (o f d) -> p o f d", o=2, f=4)
            lpv = lp.t[:].rearrange("p (o f d) -> p o f d", o=2, f=2)
            for o in range(2):
                for f in range(2):
                    TTo("dve", lpv[:, o, f, :], lvv[:, o, 2 * f, :], lvv[:, o, 2 * f + 1, :], ALU.mult, [lv], [lp])
            lp3 = lp.t[:].rearrange("p (g d) -> p g d", d=64)
            P.add("dve", lambda e: e.reduce_sum(ls[:, 0:4], lp3, axis=mybir.AxisListType.X), reads=[lp], writes=[ls])
            ACT(le[:], ls[:], AF.Exp, [ls], [le])
            lev = le.t[:].rearrange("p (o f) -> p o f", f=2)
            for o in range(2):
                li = 0.8 - 0.6 * math.exp(-0.3 * (2 * o + 1))
                STT("dve", ln2[:, o:o + 1], lev[:, o, 1:2], -li, lev[:, o, 0:1], ALU.add, ALU.subtract, [le], [ln2])
            MM(pst.t[:, 0:2], [(ones_f[0:1, :], ln2[0:1, :])], [ln2, ones_f], [pst])
            CP("dve", nlam[:], pst.t[:, 0:2], [pst], [nlam])
            for o in range(2):
                li = 0.8 - 0.6 * math.exp(-0.3 * (2 * o + 1))
                c0 = GCOL[("sub", o)]
                TSS("dve", subg[:, o:o + 1], gall[:, c0:c0 + 1], 1.0 - li, ALU.mult, [gall], [subg])
            P.barrier(bsem, bar_a, bar_b)

        def rowlocal_pass(ipass):
            with ExitStack() as st:
                RING = 5
                ring = [sb(st, "ring%d" % i, [128, 5632], BF16, dma=True) for i in range(RING)]
                xbs = [sb(st, "xb%d" % i, [128, 8, TT], F32, dma=True) for i in range(2)]
                xb = xbs[0]
                hn = sb(st, "hn", [128, 8, TT], BF16)
                act = sb(st, "act", [128, 22, TT], BF16)
                sq = sb(st, "sq", [128, 8, TT], BF16)
                rstd = sb(st, "rstd", [128, TT], F32)
                tmpf = [sb(st, "tmpf%d" % i, [128, TT], F32) for i in range(6)]
                tab = [sb(st, "tab%d" % i, [128, TT], F32, dma=True) for i in range(4)]
                otb = sb(st, "otb", [128, 8, TT], BF16, dma=True)
                xin = [sb(st, "xin%d" % i, [128, D], F32, dma=True) for i in range(2)]
                ptk = sb(st, "ptk", [128, 4, 256], F32, dma=True)
                pT = sb(st, "pT", [128, 2, TT], BF16)
                NSTG = 8
                stg = [sb(st, "stg%d" % i, [128, TT], BF16, dma=True) for i in range(NSTG)]
                vstg = [sb(st, "vstg%d" % i, [128, 520], BF16, dma=True) for i in range(2)]
                cqn_b = [sb(st, "cqn%d" % i, [128, TT], BF16) for i in range(2)]
                ckvn_b = sb(st, "ckvn", [128, TT], BF16)
                pG = [ps(st, "pG%d" % i) for i in range(2)]
                pU = [ps(st, "pU%d" % i) for i in range(2)]
                pD = [ps(st, "pD%d" % i) for i in range(2)]
                pN = ps(st, "pN")
                pTp = ps(st, "pTp")
                cnt = {"ring": 0, "stg": 0, "vstg": 0, "G": 0, "D": 0, "tmp": 0, "xin": 0}

                for v in vstg:
                    MSET(v[:], 1.0, [v])

                def slab(wname, l, K, N, c0, ncols):
                    b = ring[cnt["ring"] % RING]
                    cnt["ring"] += 1
                    kp = min(K, 128)
                    nk = K // kp
                    src = wt[wname][l].rearrange("p (k n) -> p k n", n=N)[:, :, c0:c0 + ncols]
                    view = b.t[0:kp, 0:nk * ncols].rearrange("p (k n) -> p k n", n=ncols)
                    DMA("sp", view, src, b.dsem, writes=[b])
                    return b, view

                def next_tmp():
                    t = tmpf[cnt["tmp"] % len(tmpf)]
                    cnt["tmp"] += 1
                    return t

                def next_stg():
                    s_ = stg[cnt["stg"] % NSTG]
                    cnt["stg"] += 1
                    return s_

                def next_D():
                    d_ = pD[cnt["D"] % 2]
                    cnt["D"] += 1
                    return d_

                def rstd_from(ps_buf, inv_n):
                    RSTD(rstd, ps_buf, inv_n)

                def rmsnorm_stats(src_aps, src_bufs, inv_n):
                    n = len(src_aps)
                    for c, (ap, bb) in enumerate(zip(src_aps, src_bufs)):
                        if c % 2 == 0:
                            ACT(sq[:, c, :], ap, AF.Square, [bb], [sq])
                        else:
                            TTo("dve", sq[:, c, :], ap, ap, ALU.mult, [bb], [sq])
                    MM(pN[:], [(ones_bf[:], sq[:, c, :]) for c in range(n)], [sq, ones_bf], [pN])
                    rstd_from(pN, inv_n)

                def norm_x(gcol):
                    rmsnorm_stats([xb[:, c, :] for c in range(8)], [xb] * 8, 1.0 / D)
                    for c in range(8):
                        STT("dve", hn[:, c, :], xb[:, c, :], gall[:, gcol + c:gcol + c + 1], rstd[:], ALU.mult, ALU.mult,
                            [xb, rstd, gall], [hn])

                def ffn(wg, wu, wd, l, gcol):
                    norm_x(gcol)
                    for s0 in range(0, DFF, 512):
                        ncols = min(512, DFF - s0)
                        bg, vg = slab(wg, l, D, DFF, s0, ncols)
                        bu, vu = slab(wu, l, D, DFF, s0, ncols)
                        for j in range(ncols // 128):
                            f = (s0 // 128) + j
                            G = pG[cnt["G"] % 2]
                            U = pU[cnt["G"] % 2]
                            cnt["G"] += 1
                            MM(G[:], [(vg[:, k, j * 128:(j + 1) * 128], hn[:, k, :]) for k in range(8)], [bg, hn], [G])
                            MM(U[:], [(vu[:, k, j * 128:(j + 1) * 128], hn[:, k, :]) for k in range(8)], [bu, hn], [U])
                            t = next_tmp()
                            ACT(t[:], G[:], AF.Silu, [G], [t])
                            TTo("dve", act[:, f, :], t[:], U[:], ALU.mult, [t, U], [act])
                    for m0 in range(0, D, 256):
                        bd_, vd = slab(wd, l, DFF, D, m0, 256)
                        for j in range(2):
                            m = m0 // 128 + j
                            Dp = next_D()
                            MM(Dp[:], [(vd[:, k, j * 128:(j + 1) * 128], act[:, k, :]) for k in range(22)], [bd_, act], [Dp])
                            STT("dve", xb[:, m, :], Dp[:], 0.5, xb[:, m, :], ALU.mult, ALU.add, [Dp, xb], [xb])

                def proj(wname, l, K, N, c0, ncols, rhs_aps, rhs_bufs, consume):
                    nk = len(rhs_aps)
                    done = 0
                    while done < ncols:
                        sc = min(512, ncols - done)
                        b, v = slab(wname, l, K, N, c0 + done, sc)
                        for j in range((sc + 127) // 128):
                            mcols = min(128, sc - j * 128)
                            Dp = next_D()
                            MM(Dp.t[0:mcols, :], [(v[:, k, j * 128:j * 128 + mcols], rhs_aps[k]) for k in range(nk)],
                               [b] + list(rhs_bufs), [Dp])
                            consume((done // 128) + j, Dp, mcols)
                        done += sc

                def proj_tok(wname, l, K, N, c0, ncols, lhs_fn, lhs_bufs, nk, consume):
                    b, v = slab(wname, l, K, N, c0, ncols)
                    for s in range(4):
                        Dp = next_D()
                        MM(Dp.t[:, 0:ncols], [(lhs_fn(k, s), v[:, k, :]) for k in range(nk)], [b] + list(lhs_bufs), [Dp])
                        consume(s, Dp)

                def evac_to(dst_buf, rows=128):
                    def cons(j, Dp, mcols):
                        ACT(dst_buf.t[0:rows, :], Dp.t[0:rows, :], AF.Identity, [Dp], [dst_buf])
                    return cons

                hn_aps = [hn[:, k, :] for k in range(8)]

                for ti in range(DBG.get("ntile", NTILE)):
                    t0 = ti * TT
                    tsl = slice(t0, t0 + TT)
                    xb = xbs[ti % 2]
                    if ipass == 0:
                        for s in range(4):
                            xi = xin[cnt["xin"] % 2]
                            cnt["xin"] += 1
                            DMA("sp", xi[:], x_in[t0 + s * 128:t0 + (s + 1) * 128, :], xi.dsem, writes=[xi])
                            for c0 in range(0, 8, 4):
                                TRN([(pTp.t[:, cc * 128:(cc + 1) * 128], xi.t[:, (c0 + cc) * 128:(c0 + cc + 1) * 128])
                                     for cc in range(4)], [xi], [pTp])
                                CP("dve", xb.t[:, c0:c0 + 4, s * 128:(s + 1) * 128],
                                   pTp.t[:].rearrange("p (c t) -> p c t", t=128), [pTp], [xb])
                    else:
                        DMA("sp", xb[:], xs[:, :, tsl].rearrange("c p t -> p c t"), xb.dsem, writes=[xb])

                    if ipass > 0:
                        lp_ = ipass - 1
                        DMA("sp", otb[:], OT[:, tsl].rearrange("(c p) t -> p c t", p=128), otb.dsem, writes=[otb])
                        wo = "about" if lp_ % 2 == 0 else "cout"

                        def cons_out(m, Dp, mcols):
                            TTo("dve", xb[:, m, :], Dp[:], xb[:, m, :], ALU.add, [Dp, xb], [xb])
                        proj(wo, lp_ // 2, D, D, 0, D, [otb[:, k, :] for k in range(8)], [otb], cons_out)
                        ffn("f2g", "f2u", "f2d", lp_, GCOL[("ffn2", lp_)])
                        DMA("sp", ptk[:], p_in[lp_, t0:t0 + TT, :].rearrange("(s p) d -> p s d", p=128), ptk.dsem,
                            writes=[ptk])
                        for dc in range(2):
                            TRN([(pTp.t[:, s * 128:(s + 1) * 128], ptk.t[:, s, dc * 128:(dc + 1) * 128]) for s in range(4)],
                                [ptk], [pTp])
                            CP("dve", pT[:, dc, :], pTp[:], [pTp], [pT])
                        norm_x(GCOL[("ple", lp_)])
                        for m0 in range(0, D, 512):
                            bg_, vg_ = slab("pleg", lp_, D, D, m0, 512)
                            bp_, vp_ = slab("plep", lp_, 256, D, m0, 512)
                            for j in range(4):
                                m = m0 // 128 + j
                                G = pG[cnt["G"] % 2]
                                U = pU[cnt["G"] % 2]
                                cnt["G"] += 1
                                MM(G[:], [(vg_[:, k, j * 128:(j + 1) * 128], hn[:, k, :]) for k in range(8)], [bg_, hn], [G])
                                MM(U[:], [(vp_[:, k, j * 128:(j + 1) * 128], pT[:, k, :]) for k in range(2)], [bp_, pT], [U])
                                t = next_tmp()
                                ACT(t[:], G[:], AF.Sigmoid, [G], [t])
                                t2 = next_tmp()
                                TTo("dve", t2[:], t[:], U[:], ALU.mult, [t, U], [t2])
                                TTo("dve", xb[:, m, :], t2[:], xb[:, m, :], ALU.add, [t2, xb], [xb])

                    if ipass < DEPTH:
                        li = ipass
                        ffn("f1g", "f1u", "f1d", li, GCOL[("ffn1", li)])
                        norm_x(GCOL[("mix", li)])
                        if li % 2 == 0:
                            e_ = li // 2
                            for i_, (src_, row) in enumerate(((ropeA_in, 0), (ropeA_in, 1), (ropeB_in, 0), (ropeB_in, 1))):
                                DMA("sp", tab[i_][:], src_[row, :, tsl], tab[i_].dsem, writes=[tab[i_]])
                            for (nm, nch, gq, dst) in (("q", 4, "aq", QA), ("k", 1, "ak", KA)):
                                for c in range(nch):
                                    z = next_tmp()
                                    zw = next_tmp()
                                    proj("abin", e_, D, ABX_N, ABX[nm] + c * 128, 128, hn_aps, [hn], evac_to(z))
                                    proj("abin", e_, D, ABX_N, ABX[nm + "_sw"] + c * 128, 128, hn_aps, [hn], evac_to(zw))
                                    ACT(sq[:, 0, :], z[:], AF.Square, [z], [sq])
                                    MM(pN[:], [(bd64[:], sq[:, 0, :])], [sq, bd64], [pN])
                                    rstd_from(pN, 1.0 / 64)
                                    g0 = GCOL[(gq, e_)]
                                    g1 = GCOL[(gq + "_sw", e_)]
                                    STT("dve", z[:], z[:], gall[:, g0:g0 + 1], tab[0][:], ALU.mult, ALU.mult,
                                        [z, gall, tab[0]], [z])
                                    STT("dve", zw[:], zw[:], gall[:, g1:g1 + 1], tab[1][:], ALU.mult, ALU.mult,
                                        [zw, gall, tab[1]], [zw])
                                    TTo("dve", z[:], z[:], zw[:], ALU.add, [z, zw], [z])
                                    s_ = next_stg()
                                    TTo("dve", s_[:], z[:], rstd[:], ALU.mult, [z, rstd], [s_])
                                    DMA("pool", dst[c * 128:(c + 1) * 128, tsl], s_[:], s_.dsem, reads=[s_])
                            vs = vstg[cnt["vstg"] % 2]
                            cnt["vstg"] += 1

                            def consVA(s, Dp, vs=vs, t0=t0):
                                CP("dve", vs.t[:, 0:130].rearrange("p (g d) -> p g d", d=65)[:, :, 0:64],
                                   Dp.t[:, 0:128].rearrange("p (g d) -> p g d", d=64), [Dp], [vs])
                                DMA("pool", VA[t0 + s * 128:t0 + (s + 1) * 128, :], vs.t[:, 0:130], vs.dsem, reads=[vs])
                            proj_tok("abin", e_, D, ABX_N, ABX["v"], 128,
                                     lambda k, s: hn[:, k, s * 128:(s + 1) * 128], [hn], 8, consVA)
                            cq = [next_tmp(), next_tmp()]
                            for c in range(2):
                                proj("abin", e_, D, ABX_N, ABX["cq"] + c * 128, 128, hn_aps, [hn], evac_to(cq[c]))
                            rmsnorm_stats([cq[0][:], cq[1][:]], cq, 1.0 / 256)
                            cqn = cqn_b
                            for c in range(2):
                                gq_ = GCOL[("bq", e_)] + c
                                STT("dve", cqn[c][:], cq[c][:], gall[:, gq_:gq_ + 1], rstd[:], ALU.mult, ALU.mult,
                                    [cq[c], gall, rstd], [cqn[c]])
                            ckv = next_tmp()
                            proj("abin", e_, D, ABX_N, ABX["ckv"], 128, hn_aps, [hn], evac_to(ckv))
                            rmsnorm_stats([ckv[:]], [ckv], 1.0 / 128)
                            ckvn = ckvn_b
                            gk_ = GCOL[("bkv", e_)]
                            STT("dve", ckvn[:], ckv[:], gall[:, gk_:gk_ + 1], rstd[:], ALU.mult, ALU.mult,
                                [ckv, gall, rstd], [ckvn])
                            kr = next_tmp()
                            krw = next_tmp()
                            proj("abin", e_, D, ABX_N, ABX["kr"], 32, hn_aps, [hn], evac_to(kr, 32))
                            proj("abin", e_, D, ABX_N, ABX["kr_sw"], 32, hn_aps, [hn], evac_to(krw, 32))
                            TTo("dve", kr[0:32, :], kr[0:32, :], tab[2][0:32, :], ALU.mult, [kr, tab[2]], [kr])
                            TTo("dve", krw[0:32, :], krw[0:32, :], tab[3][0:32, :], ALU.mult, [krw, tab[3]], [krw])
                            s_ = next_stg()
                            TTo("dve", s_[0:32, :], kr[0:32, :], krw[0:32, :], ALU.add, [kr, krw], [s_])
                            DMA("pool", KRB[:, tsl], s_[0:32, :], s_.dsem, reads=[s_])

                            def cons_qn(j, Dp, mcols, tsl=tsl):
                                s_ = next_stg()
                                CP("dve", s_[:], Dp[:], [Dp], [s_])
                                DMAN("pool", [(QB[2 * j + hh_, 0:64, tsl], s_.t[hh_ * 64:(hh_ + 1) * 64, :]) for hh_ in range(2)],
                                     s_.dsem, reads=[s_])
                            proj("wuq", e_, 256, 1024, 0, 512, [cqn[0][:], cqn[1][:]], cqn, cons_qn)
                            for c in range(2):
                                z = next_tmp()
                                zw = next_tmp()
                                proj("wuq", e_, 256, 1024, 512 + c * 128, 128, [cqn[0][:], cqn[1][:]], cqn, evac_to(z))
                                proj("wuq", e_, 256, 1024, 768 + c * 128, 128, [cqn[0][:], cqn[1][:]], cqn, evac_to(zw))
                                TTo("dve", z[:], z[:], tab[2][:], ALU.mult, [z, tab[2]], [z])
                                TTo("dve", zw[:], zw[:], tab[3][:], ALU.mult, [zw, tab[3]], [zw])
                                s_ = next_stg()
                                TTo("dve", s_[:], z[:], zw[:], ALU.add, [z, zw], [s_])
                                DMAN("pool", [(QB[4 * c + hh_, 64:96, tsl], s_.t[hh_ * 32:(hh_ + 1) * 32, :]) for hh_ in range(4)],
                                     s_.dsem, reads=[s_])

                            def cons_kn(j, Dp, mcols, tsl=tsl):
                                s_ = next_stg()
                                CP("dve", s_[:], Dp[:], [Dp], [s_])
                                DMAN("pool", [(KB[2 * j + hh_, :, tsl], s_.t[hh_ * 64:(hh_ + 1) * 64, :]) for hh_ in range(2)],
                                     s_.dsem, reads=[s_])
                            proj("wukv", e_, 128, 1024, 0, 512, [ckvn[:]], [ckvn], cons_kn)
                            vs = vstg[cnt["vstg"] % 2]
                            cnt["vstg"] += 1

                            def consVB(s, Dp, vs=vs, t0=t0):
                                CP("dve", vs.t[:, 0:520].rearrange("p (g d) -> p g d", d=65)[:, :, 0:64],
                                   Dp.t[:, 0:512].rearrange("p (g d) -> p g d", d=64), [Dp], [vs])
                                DMA("pool", VB[t0 + s * 128:t0 + (s + 1) * 128, :], vs.t[:, 0:520], vs.dsem, reads=[vs])
                            proj_tok("wukv", e_, 128, 1024, 512, 512,
                                     lambda k, s: ckvn[:, s * 128:(s + 1) * 128], [ckvn], 1, consVB)
                        else:
                            o_ = li // 2
                            for which, dst in ((0, QC), (1024, KC)):
                                def cons_qk(j, Dp, mcols, dst=dst, tsl=tsl):
                                    s_ = next_stg()
                                    CP("dve", s_[:], Dp[:], [Dp], [s_])
                                    DMA("pool", dst[j * 128:(j + 1) * 128, tsl], s_[:], s_.dsem, reads=[s_])
                                proj("cin", o_, D, 3072, which, 1024, hn_aps, [hn], cons_qk)
                            for half in range(2):
                                def consVC(s, Dp, half=half, t0=t0):
                                    s_ = next_stg()
                                    CP("dve", s_[:], Dp[:], [Dp], [s_])
                                    DMA("pool", VC[t0 + s * 128:t0 + (s + 1) * 128, half * 512:(half + 1) * 512], s_[:],
                                        s_.dsem, reads=[s_])
                                proj_tok("cin", o_, D, 3072, 2048 + half * 512, 512,
                                         lambda k, s: hn[:, k, s * 128:(s + 1) * 128], [hn], 8, consVC)
                        DMA("pool", xs[:, :, tsl].rearrange("c p t -> p c t"), xb[:], xb.dsem, reads=[xb])
                    else:
                        rmsnorm_stats([xb[:, c, :] for c in range(8)], [xb] * 8, 1.0 / D)
                        gcol = GCOL["final"]
                        for c in range(8):
                            STT("dve", xb[:, c, :], xb[:, c, :], gall[:, gcol + c:gcol + c + 1], rstd[:], ALU.mult, ALU.mult,
                                [xb, rstd, gall], [xb])
                        for s in range(4):
                            xi = xin[cnt["xin"] % 2]
                            cnt["xin"] += 1
                            for c0 in range(0, 8, 4):
                                TRN([(pTp.t[:, cc * 128:(cc + 1) * 128], xb.t[:, c0 + cc, s * 128:(s + 1) * 128])
                                     for cc in range(4)], [xb], [pTp])
                                CP("dve", xi.t[:, c0 * 128:(c0 + 4) * 128], pTp[:], [pTp], [xi])
                            DMA("pool", y_out[t0 + s * 128:t0 + (s + 1) * 128, :], xi[:], xi.dsem, reads=[xi])
                P.barrier(bsem, bar_a, bar_b)
                P.release_dsems([b.dsem for b in ring + xbs + [otb, ptk] + tab + xin + stg + vstg])

        def ps2(stack, name):
            uid["n"] += 1
            return Buf(stack.enter_context(nc.psum_tensor("p%d_%s" % (uid["n"], name), [128, 1024], F32)))

        def MMS(specs, reads, writes):
            specs = list(specs)

            def fn(e):
                ins = None
                for (o_, a_, b_, s0_, s1_) in specs:
                    ins = e.matmul(o_, a_, b_, start=s0_, stop=s1_)
                return ins
            return P.add("pe", fn, reads, writes)

        def attn_even(e_):
            with ExitStack() as st:
                S2 = [ps2(st, "S2_%d" % i) for i in range(2)]
                O = [ps(st, "O%d" % i) for i in range(2)]
                Bp = ps(st, "Bp")
                Kt = [sb(st, "Kt%d" % i, [128, NT], BF16, dma=True) for i in range(2)]
                Qt = [sb(st, "Qt%d" % i, [128, NT], BF16, dma=True) for i in range(2)]
                Vt = sb(st, "Vt", [128, 64, 584], BF16, dma=True)
                maskb = sb(st, "maskb", [128, 1024], F32, dma=True)
                NP_ = 3
                Pt = [sb(st, "Pt%d" % i, [128, 2 * TT], BF16) for i in range(NP_)]
                rl = [sb(st, "rl%d" % i, [128, TT], F32) for i in range(2)]
                osb = [sb(st, "osb%d" % i, [128, TT], F32) for i in range(2)]
                ostg = [sb(st, "ostg%d" % i, [128, TT], BF16, dma=True) for i in range(4)]
                DMA("sp", maskb[:], maskb_in, maskb.dsem, writes=[maskb])
                for b_ in Kt + Qt + [Vt]:
                    MSET(b_[:], 0.0, [b_])
                qcount = 0
                ecount = 0
                for mixer in ("A", "B"):
                    if mixer == "A":
                        Kd, scale, vw, Vsrc = 128, 64 ** -0.5, 130, VA
                    else:
                        Kd, scale, vw, Vsrc = 96, 96 ** -0.5, 520, VB
                    vsrc = Vsrc.rearrange("(t p) c -> p t c", p=128)
                    DMAN("sp", [(Vt.t[:, 16 * i:16 * (i + 1), 0:vw], vsrc[:, 16 * i:16 * (i + 1), :]) for i in range(4)],
                         Vt.dsem, writes=[Vt])
                    qbase = qcount
                    qcount += 8

                    def Kbuf(h):
                        return Kt[(h // 4) % 2] if mixer == "A" else Kt[h % 2]

                    def Qbuf(h):
                        return Qt[(qbase + h) % 2]

                    def load_head(h):
                        q_ = Qbuf(h)
                        k_ = Kbuf(h)
                        if mixer == "A":
                            g = h // 4
                            if h % 4 == 0:
                                DMA("sp", k_.t[0:64, :], KA[g * 64:(g + 1) * 64, :], k_.dsem, writes=[k_])
                            DMA("sp", q_.t[0:64, :], QA[h * 64:(h + 1) * 64, :], q_.dsem, writes=[q_])
                        else:
                            DMAN("sp", [(k_.t[0:64, :], KB[h, :, :]), (k_.t[64:96, :], KRB[:, :])], k_.dsem, writes=[k_])
                            DMA("sp", q_.t[0:96, :], QB[h, :, :], q_.dsem, writes=[q_])

                    LA = 2
                    n = 8 * 8 * 64
                    load_head(0)
                    for i in range(n + LA):
                        if i % 512 == LA and (i // 512) + 1 < 8:
                            load_head(i // 512 + 1)
                        if i < n:
                            h, qp, t = i // 512, (i % 512) // 64, i % 64
                            K_, Q_ = Kbuf(h), Qbuf(h)
                            Sb = S2[i % 2]
                            kt_ = K_.t[0:Kd, t * 128:(t + 1) * 128]
                            MMS([(Sb.t[:, 0:TT], kt_, Q_.t[0:Kd, (2 * qp) * TT:(2 * qp + 1) * TT], True, True),
                                 (Sb.t[:, TT:2 * TT], kt_, Q_.t[0:Kd, (2 * qp + 1) * TT:(2 * qp + 2) * TT], True, True)],
                                [K_, Q_], [Sb])
                            pt = Pt[i % NP_]
                            blk = (2 * qp) * 64 + t
                            ACT(pt[:], Sb[:], AF.Exp, [Sb, maskb], [pt], bias=maskb[:, blk:blk + 1], scale=scale)
                        j = i - LA
                        if j >= 0:
                            h, qp, t = j // 512, (j % 512) // 64, j % 64
                            vcol = (h // 4) * 65 if mixer == "A" else h * 65
                            orow = (0 if mixer == "A" else 512) + h * 64
                            pt = Pt[j % NP_]
                            vt_ = Vt.t[:, t, vcol:vcol + 128]
                            MMS([(O[0].t[:, :], vt_, pt.t[:, 0:TT], t == 0, t == 63),
                                 (O[1].t[:, :], vt_, pt.t[:, TT:2 * TT], t == 0, t == 63)], [pt, Vt], [O[0], O[1]])
                            if t == 63:
                                for k2 in range(2):
                                    CP("dve", osb[k2][0:65, :], O[k2].t[0:65, :], [O[k2]], [osb[k2]])
                                for k2 in range(2):
                                    RCP(rl[k2][64:65, :], osb[k2][64:65, :], [osb[k2]], [rl[k2]])
                                for k2 in range(2):
                                    MM(Bp.t[0:64, :], [(ones_f[64:65, 0:64], rl[k2][64:65, :])], [rl[k2], ones_f], [Bp])
                                    og = ostg[ecount % 4]
                                    ecount += 1
                                    TTo("dve", og[0:64, :], osb[k2][0:64, :], Bp.t[0:64, :], ALU.mult, [osb[k2], Bp], [og])
                                    qb = 2 * qp + k2
                                    DMA("pool", OT[orow:orow + 64, qb * TT:(qb + 1) * TT], og[0:64, :], og.dsem, reads=[og])
                P.barrier(bsem, bar_a, bar_b)
                P.release_dsems([b.dsem for b in Kt + Qt + [Vt, maskb] + ostg])

        ALIBI_THR = 48.0

        def attn_odd(o_):
            with ExitStack() as st:
                S = [ps(st, "S%d" % i) for i in range(4)]
                O1 = ps(st, "O1")
                O2 = ps(st, "O2")
                L1 = ps(st, "L1")
                L2 = ps(st, "L2")
                Kt = [sb(st, "Kt%d" % i, [128, NT], BF16, dma=True) for i in range(2)]
                Qt = [sb(st, "Qt%d" % i, [128, NT], BF16, dma=True) for i in range(2)]
                Vt = [sb(st, "Vt%d" % i, [128, 64, 128], BF16, dma=True) for i in range(2)]
                maskb = sb(st, "maskb", [128, 1024], F32, dma=True)
                cdist = sb(st, "cdist", [128, 1024], F32, dma=True)
                dtl = sb(st, "dtl", [128, 6, 512], F32, dma=True)
                biasC = [sb(st, "biasC%d" % i, [128, 1024], F32) for i in range(2)]
                NP_ = 3
                Pt = [sb(st, "Pt%d" % i, [128, 2 * TT], BF16) for i in range(NP_)]
                tm = [sb(st, "tm%d" % i, [128, 2 * TT], F32) for i in range(NP_)]
                r1 = sb(st, "r1", [128, TT], F32)
                r2 = sb(st, "r2", [128, TT], F32)
                o1 = sb(st, "o1", [128, TT], F32)
                o2 = sb(st, "o2", [128, TT], F32)
                sqo = sb(st, "sqo", [128, TT], BF16)
                rs = sb(st, "rs", [128, TT], F32)
                ostg = [sb(st, "ostg%d" % i, [128, TT], BF16, dma=True) for i in range(2)]
                DMA("sp", maskb[:], maskb_in, maskb.dsem, writes=[maskb])
                DMA("sp", cdist[:], cdist_in, cdist.dsem, writes=[cdist])
                DMA("sp", dtl[:], dtl_in, dtl.dsem, writes=[dtl])
                scale = 64 ** -0.5

                def load_head(h):
                    slope = 2.0 ** (-(h + 1))
                    K_, Q_, V_, bC = Kt[h % 2], Qt[h % 2], Vt[h % 2], biasC[h % 2]
                    DMA("sp", K_[:], KC[h * 128:(h + 1) * 128, :], K_.dsem, writes=[K_])
                    DMA("sp", Q_[:], QC[h * 128:(h + 1) * 128, :], Q_.dsem, writes=[Q_])
                    vsrc = VC[:, h * 128:(h + 1) * 128].rearrange("(t p) c -> p t c", p=128)
                    DMAN("sp", [(V_.t[:, 16 * i:16 * (i + 1), :], vsrc[:, 16 * i:16 * (i + 1), :]) for i in range(4)],
                         V_.dsem, writes=[V_])
                    STT("dve", bC[:], cdist[:], slope, maskb[:], ALU.mult, ALU.add, [cdist, maskb], [bC])

                items = []
                head_start = {}
                for h in range(8):
                    slope = 2.0 ** (-(h + 1))
                    head_start[len(items)] = h
                    for qb in range(16):
                        q_lo, q_hi = qb * TT, qb * TT + TT - 1
                        keep = []
                        for t in range(64):
                            s_lo, s_hi = t * 128, t * 128 + 127
                            mind = max(0, s_lo - q_hi, q_lo - s_hi)
                            if slope * mind < ALIBI_THR:
                                keep.append(t)
                        for t in keep:
                            items.append((h, qb, t, t == keep[0], t == keep[-1]))
                LA = 2
                n = len(items)
                load_head(0)
                ecount = 0
                for i in range(n + LA):
                    if (i - LA) in head_start and head_start[i - LA] + 1 < 8:
                        load_head(head_start[i - LA] + 1)
                    if i < n:
                        h, qb, t, fs, ls_ = items[i]
                        slope = 2.0 ** (-(h + 1))
                        K_, Q_, bC = Kt[h % 2], Qt[h % 2], biasC[h % 2]
                        Sa = S[(i % 2) * 2]
                        Sb = S[(i % 2) * 2 + 1]
                        MM(Sa[:], [(K_.t[0:64, t * 128:(t + 1) * 128], Q_.t[0:64, qb * TT:(qb + 1) * TT])], [K_, Q_], [Sa])
                        MM(Sb[:], [(K_.t[64:128, t * 128:(t + 1) * 128], Q_.t[64:128, qb * TT:(qb + 1) * TT])], [K_, Q_], [Sb])
                        if t < 4 * qb:
                            di = 0
                        elif t > 4 * qb + 3:
                            di = 1
                        else:
                            di = 2 + (t - 4 * qb)
                        tmb = tm[i % NP_]
                        fac = slope / scale
                        STT("dve", tmb[:, 0:TT], dtl[:, di, :], fac, Sa[:], ALU.mult, ALU.add, [Sa, dtl], [tmb])
                        STT("dve", tmb[:, TT:2 * TT], dtl[:, di, :], fac, Sb[:], ALU.mult, ALU.add, [Sb, dtl], [tmb])
                        pt = Pt[i % NP_]
                        blk = qb * 64 + t
                        ACT(pt[:], tmb[:], AF.Exp, [tmb, bC], [pt], bias=bC[:, blk:blk + 1], scale=scale)
                    j = i - LA
                    if j >= 0:
                        h, qb, t, fs, ls_ = items[j]
                        V_ = Vt[h % 2]
                        pt = Pt[j % NP_]
                        MMS([(O1.t[:, :], V_.t[:, t, :], pt.t[:, 0:TT], fs, ls_),
                             (L1.t[:, :], ones_bf.t[:, :], pt.t[:, 0:TT], fs, ls_),
                             (O2.t[:, :], V_.t[:, t, :], pt.t[:, TT:2 * TT], fs, ls_),
                             (L2.t[:, :], ones_bf.t[:, :], pt.t[:, TT:2 * TT], fs, ls_)],
                            [pt, V_, ones_bf], [O1, L1, O2, L2])
                        if ls_:
                            ACT(r1[:], L1[:], AF.Ln, [L1], [r1])
                            ACT(r2[:], L2[:], AF.Ln, [L2], [r2])
                            ACT(r1[:], r1[:], AF.Exp, [r1], [r1], scale=-1.0)
                            ACT(r2[:], r2[:], AF.Exp, [r2], [r2], scale=-1.0)
                            TTo("dve", o1[:], O1[:], r1[:], ALU.mult, [O1, r1], [o1])
                            TTo("dve", o2[:], O2[:], r2[:], ALU.mult, [O2, r2], [o2])
                            STT("dve", o1[:], o2[:], nlam[:, o_:o_ + 1], o1[:], ALU.mult, ALU.add, [o1, o2, nlam], [o1])
                            ACT(sqo[:], o1[:], AF.Square, [o1], [sqo])
                            MM(L1[:], [(ones_bf[:], sqo[:])], [sqo, ones_bf], [L1])
                            RSTD(rs, L1, 1.0 / 128)
                            og = ostg[ecount % 2]
                            ecount += 1
                            STT("dve", og[:], o1[:], subg[:, o_:o_ + 1], rs[:], ALU.mult, ALU.mult, [o1, subg, rs], [og])
                            DMA("pool", OT[h * 128:(h + 1) * 128, qb * TT:(qb + 1) * TT], og[:], og.dsem, reads=[og])
                P.barrier(bsem, bar_a, bar_b)
                P.release_dsems([b.dsem for b in Kt + Qt + Vt + [maskb, cdist, dtl] + ostg])

        for ipass in range(DBG["passes"]):
            rowlocal_pass(ipass)
            if ipass < DEPTH and DBG["attn"]:
                if ipass % 2 == 0:
                    attn_even(ipass // 2)
                else:
                    attn_odd(ipass // 2)
        P.add("sp", lambda e: e.nop(), extra=[P.last_barrier])
        P.emit(block)
    return nc


def _tables(seq_len):
    t = np.arange(NT)
    tl = t % seq_len
    row, col = tl // 64, tl % 64
    inv = (10000.0 ** (-np.arange(16, dtype=np.float32) * (2.0 / 32))).astype(np.float32)

    def cs(pos, d_idx):
        jj = d_idx % 32
        i = jj % 16
        ang = pos[None, :].astype(np.float32) * inv[i][:, None]
        c = np.cos(ang).astype(np.float32)
        s = np.sin(ang).astype(np.float32)
        s = np.where((jj < 16)[:, None], -s, s)
        return c, s
    d = np.arange(128)
    j = d % 64
    cA = np.zeros((128, NT), np.float32)
    sA = np.zeros((128, NT), np.float32)
    m_row = j < 32
    c1, s1 = cs(row, d)
    c2, s2 = cs(col, d)
    cA[m_row], sA[m_row] = c1[m_row], s1[m_row]
    cA[~m_row], sA[~m_row] = c2[~m_row], s2[~m_row]
    cB, sB = cs(tl, d)
    ropeA = np.stack([cA, sA]).astype(np.float32)
    ropeB = np.stack([cB, sB]).astype(np.float32)
    qb = np.arange(16)[:, None]
    tt = np.arange(64)[None, :]
    q0 = qb * 512
    s0 = tt * 128
    same = (q0 // seq_len) == (s0 // seq_len)
    maskb = np.where(same, 0.0, NEG).astype(np.float32).reshape(1, 1024)
    diag = (tt >= 4 * qb) & (tt <= 4 * qb + 3)
    cd = np.where(diag, 0.0, -np.abs(q0 - s0)).astype(np.float32).reshape(1, 1024)
    maskb = np.repeat(maskb, 128, 0)
    cd = np.repeat(cd, 128, 0)
    return ropeA, ropeB, maskb, cd


def _dtiles():
    p = np.arange(128)[:, None]
    j = np.arange(512)[None, :]
    tiles = [-(j - p), (j - p)] + [-np.abs(j - p - 128 * c) for c in range(4)]
    return np.ascontiguousarray(np.stack(tiles, 1).astype(np.float32))


_CACHE = {}


def kernel(**inp):
    f = lambda a: np.ascontiguousarray(np.asarray(a, dtype=np.float32))
    x_prompt, x_sample = f(inp["x_prompt"]), f(inp["x_sample"])
    p_prompt, p_sample = f(inp["p_prompt"]), f(inp["p_sample"])

    sw64 = np.concatenate([_swap32(64)])
    abin = f(inp["ab_w_in"])
    q_idx = np.arange(512)
    q_sw_idx = (q_idx // 64) * 64 + sw64[q_idx % 64]
    k_idx = 512 + np.arange(128)
    k_sw_idx = 512 + (np.arange(128) // 64) * 64 + sw64[np.arange(128) % 64]
    kr_idx = 1152 + np.arange(32)
    kr_sw_idx = 1152 + _swap32(32)
    cols = np.concatenate([q_idx, q_sw_idx, k_idx, k_sw_idx, 768 + np.arange(256), 1024 + np.arange(128),
                           kr_idx, kr_sw_idx, 640 + np.arange(128)])
    abin_x = np.ascontiguousarray(abin[:, :, cols])
    wuq = f(inp["b_w_uq"])
    hh = np.arange(8)[:, None]
    nope_idx = (hh * 96 + np.arange(64)[None, :]).reshape(-1)
    rope_idx = (hh * 96 + 64 + np.arange(32)[None, :]).reshape(-1)
    rope_sw_idx = (hh * 96 + 64 + _swap32(32)[None, :]).reshape(-1)
    wuq_x = np.ascontiguousarray(wuq[:, :, np.concatenate([nope_idx, rope_idx, rope_sw_idx])])
    wukv = f(inp["b_w_ukv"])
    kn_idx = (hh * 128 + np.arange(64)[None, :]).reshape(-1)
    v_idx = (hh * 128 + 64 + np.arange(64)[None, :]).reshape(-1)
    wukv_x = np.ascontiguousarray(wukv[:, :, np.concatenate([kn_idx, v_idx])])

    gall = np.zeros((128, NG), np.float32)

    def put(col, vec):
        v = np.asarray(vec, np.float32).reshape(-1, 128).T
        gall[:, col:col + v.shape[1]] = v
    for i in range(DEPTH):
        put(GCOL[("ffn1", i)], inp["ffn1_norm"][i])
        put(GCOL[("mix", i)], inp["mix_norm"][i])
        put(GCOL[("ffn2", i)], inp["ffn2_norm"][i])
        put(GCOL[("ple", i)], inp["ple_norm"][i])
    put(GCOL["final"], inp["final_norm"])
    for e in range(2):
        aq = np.asarray(inp["a_q_norm"][e], np.float32)
        ak = np.asarray(inp["a_k_norm"][e], np.float32)
        put(GCOL[("aq", e)], np.tile(aq, 2))
        put(GCOL[("aq_sw", e)], np.tile(aq[sw64], 2))
        put(GCOL[("ak", e)], np.tile(ak, 2))
        put(GCOL[("ak_sw", e)], np.tile(ak[sw64], 2))
        put(GCOL[("bq", e)], inp["b_q_norm"][e])
        put(GCOL[("bkv", e)], inp["b_kv_norm"][e])
    for o in range(2):
        put(GCOL[("sub", o)], inp["c_sub_norm"][o])
    lamv = np.stack([np.stack([f(inp["c_lambda_q1"])[o], f(inp["c_lambda_k1"])[o],
                               f(inp["c_lambda_q2"])[o], f(inp["c_lambda_k2"])[o]]) for o in range(2)])
    lamv = np.ascontiguousarray(lamv.reshape(1, 512))

    shared = {
        "f1g": f(inp["ffn1_wg"]), "f1u": f(inp["ffn1_wu"]), "f1d": f(inp["ffn1_wd"]),
        "f2g": f(inp["ffn2_wg"]), "f2u": f(inp["ffn2_wu"]), "f2d": f(inp["ffn2_wd"]),
        "abin": abin_x, "wuq": wuq_x, "wukv": wukv_x, "about": f(inp["ab_w_out"]),
        "cin": f(inp["c_w_in"]), "cout": f(inp["c_w_out"]),
        "pleg": f(inp["ple_w_gate"]), "plep": f(inp["ple_w_proj"]),
        "gall": gall, "lamv": lamv, "dtiles": _dtiles(),
    }
    tabs = {8192: _tables(8192), 2048: _tables(2048)}
    in_maps = []
    for core in range(8):
        u = core if core < N_UNITS else core - 2
        if u < 4:
            xu = x_prompt[u]
            pu = p_prompt[:, u]
            tb = tabs[8192]
        else:
            s = (u - 4) * 4
            xu = x_sample[s:s + 4].reshape(NT, D)
            pu = p_sample[:, s:s + 4].reshape(DEPTH, NT, 256)
            tb = tabs[2048]
        m = dict(shared)
        m["x"] = np.ascontiguousarray(xu)
        m["p"] = np.ascontiguousarray(pu)
        m["ropeA"], m["ropeB"], m["maskb"], m["cdist"] = tb
        in_maps.append(m)

    if "nc" not in _CACHE:
        _CACHE["nc"] = build_program()
    nc = _CACHE["nc"]
    res = run_bass_kernel_spmd(nc, in_maps, core_ids=list(range(8)))
    if DBG.get("dump"):
        _CACHE["res"] = res.results
    ys = [np.asarray(r["y"], dtype=np.float32) for r in res.results]
    y_prompt = np.stack(ys[0:4]).reshape(4, NT, D)
    y_sample = np.concatenate([ys[4].reshape(4, 2048, D), ys[5].reshape(4, 2048, D)], 0)
    return (y_prompt, y_sample)
```

```python
import math
from contextlib import ExitStack

import numpy as np
import concourse.bass as bass
import concourse.mybir as mybir
from concourse.bass_utils import run_bass_kernel_spmd

F32 = mybir.dt.float32
BF16 = mybir.dt.bfloat16
AF = mybir.ActivationFunctionType
ALU = mybir.AluOpType

DEPTH = 4
D = 1024
DFF = 2816
NT = 8192
TT = 512
NTILE = NT // TT
EPS = 1e-6
NEG = -30000.0
N_UNITS = 6

DBG = {"passes": 5, "attn": True}


class DSem:
    def __init__(self, handle):
        self.handle = handle
        self.total = 0
        self.last = None


class Op:
    __slots__ = ("eng", "fn", "deps", "is_dma", "sem", "val", "needs_inc", "n")

    def __init__(self, eng, fn, deps, is_dma=False, sem=None, n=1):
        self.eng = eng
        self.fn = fn
        self.deps = deps
        self.is_dma = is_dma
        self.sem = sem
        self.val = 0
        self.needs_inc = False
        self.n = n


class Buf:
    def __init__(self, t, dsem=None):
        self.t = t
        self.w = None
        self.r = []
        self.dsem = dsem

    def __getitem__(self, k):
        return self.t[k]


class Prog:
    ENGS = ("pe", "act", "dve", "pool", "sp")

    def __init__(self, nc, esems, dsem_handles):
        self.nc = nc
        self.ops = {e: [] for e in self.ENGS}
        self.esem = esems
        self.free_dsems = [DSem(h) for h in dsem_handles]
        self.all_dsems = list(self.free_dsems)
        self.last_barrier = None

    def new_dsem(self):
        return self.free_dsems.pop()

    def release_dsems(self, ds):
        self.free_dsems.extend(ds)

    def _deps(self, eng, reads, writes, extra):
        deps = []
        for b in reads:
            if b.w is not None:
                deps.append(b.w)
        for b in writes:
            for r in b.r:
                if r.eng != eng or r.is_dma:
                    deps.append(r)
            if b.w is not None and (b.w.eng != eng or b.w.is_dma):
                deps.append(b.w)
        deps.extend(d for d in extra if d is not None)
        if self.last_barrier is not None:
            deps.append(self.last_barrier)
        return deps

    def add(self, eng, fn, reads=(), writes=(), extra=()):
        op = Op(eng, fn, self._deps(eng, reads, writes, extra))
        self.ops[eng].append(op)
        for b in reads:
            b.r.append(op)
        for b in writes:
            b.w = op
            b.r = []
        return op

    def dma(self, eng, fn, sem, reads=(), writes=(), extra=(), n=1):
        deps = self._deps(eng, reads, writes, extra)
        if sem.last is not None:
            deps.append(sem.last)
        op = Op(eng, fn, deps, is_dma=True, sem=sem, n=n)
        sem.total += 16 * n
        op.val = sem.total
        sem.last = op
        self.ops[eng].append(op)
        for b in reads:
            b.r.append(op)
        for b in writes:
            b.w = op
            b.r = []
        return op

    def barrier(self, bsem, scratch_src, scratch_dst):
        deps = []
        for e in self.ENGS:
            for op in reversed(self.ops[e]):
                if not op.is_dma:
                    deps.append(op)
                    break
        for ds in self.all_dsems:
            if ds.last is not None:
                deps.append(ds.last)
        self.last_barrier = None
        op = self.dma("sp", lambda e: e.dma_start(out=scratch_dst, in_=scratch_src), bsem, extra=deps)
        self.last_barrier = op
        return op

    def emit(self, block):
        for e in self.ENGS:
            for op in self.ops[e]:
                for d in op.deps:
                    if not d.is_dma:
                        d.needs_inc = True
        for e in self.ENGS:
            c = 0
            for op in self.ops[e]:
                if not op.is_dma and op.needs_inc:
                    c += 1
                    op.val = c
        esem = self.esem

        def run(engname, eng):
            waited = {}
            for op in self.ops[engname]:
                need = {}
                for d in op.deps:
                    s = d.sem.handle if d.is_dma else esem[d.eng]
                    key = id(s)
                    if waited.get(key, 0) < d.val and need.get(key, (None, 0))[1] < d.val:
                        need[key] = (s, d.val)
                for key, (s, v) in need.items():
                    eng.wait_ge(s, v)
                    waited[key] = v
                ins = op.fn(eng)
                if op.is_dma:
                    if not isinstance(ins, (list, tuple)):
                        ins = [ins]
                    assert len(ins) == op.n, (len(ins), op.n)
                    for i_ in ins:
                        i_.then_inc(op.sem.handle, 16)
                elif op.needs_inc:
                    ins.then_inc(esem[engname], 1)

        block.tensor(lambda eng: run("pe", eng))
        block.scalar(lambda eng: run("act", eng))
        block.vector(lambda eng: run("dve", eng))
        block.gpsimd(lambda eng: run("pool", eng))
        block.sync(lambda eng: run("sp", eng))


def _gcols():
    cols = {}
    c = 0
    for i in range(DEPTH):
        for nm in ("ffn1", "mix", "ffn2", "ple"):
            cols[(nm, i)] = c
            c += 8
    cols["final"] = c
    c += 8
    for e in range(2):
        for nm, w in (("aq", 1), ("aq_sw", 1), ("ak", 1), ("ak_sw", 1), ("bq", 2), ("bkv", 1)):
            cols[(nm, e)] = c
            c += w
    for o in range(2):
        cols[("sub", o)] = c
        c += 1
    return cols, c


GCOL, NG = _gcols()

ABX = {"q": 0, "q_sw": 512, "k": 1024, "k_sw": 1152, "cq": 1280, "ckv": 1536, "kr": 1664, "kr_sw": 1696,
       "v": 1728}
ABX_N = 1856


def _swap32(n):
    idx = np.arange(n)
    j = idx % 32
    return np.where(j < 16, idx + 16, idx - 16)


def build_program():
    nc = bass.Bass("TRN2", target_bir_lowering=False)

    def dram(name, shape, dt, kind="Internal"):
        if name in DBG.get("dump", ()):
            kind = "ExternalOutput"
        return nc.dram_tensor(name, list(shape), dt, kind=kind).ap()

    x_in = dram("x", [NT, D], F32, "ExternalInput")
    p_in = dram("p", [DEPTH, NT, 256], F32, "ExternalInput")
    y_out = dram("y", [NT, D], F32, "ExternalOutput")
    wsrc = {}
    wshape = {
        "f1g": (4, D, DFF), "f1u": (4, D, DFF), "f1d": (4, DFF, D),
        "f2g": (4, D, DFF), "f2u": (4, D, DFF), "f2d": (4, DFF, D),
        "abin": (2, D, ABX_N), "wuq": (2, 256, 1024), "wukv": (2, 128, 1024), "about": (2, D, D),
        "cin": (2, D, 3072), "cout": (2, D, D), "pleg": (4, D, D), "plep": (4, 256, D),
    }
    for k, shp in wshape.items():
        wsrc[k] = dram(k, shp, F32, "ExternalInput")
    gall_in = dram("gall", [128, NG], F32, "ExternalInput")
    lamv_in = dram("lamv", [1, 2 * 4 * 64], F32, "ExternalInput")
    ropeA_in = dram("ropeA", [2, 128, NT], F32, "ExternalInput")
    ropeB_in = dram("ropeB", [2, 128, NT], F32, "ExternalInput")
    maskb_in = dram("maskb", [128, 1024], F32, "ExternalInput")
    cdist_in = dram("cdist", [128, 1024], F32, "ExternalInput")
    dtl_in = dram("dtiles", [128, 6, 512], F32, "ExternalInput")

    wt = {}
    for k, (L, K, N) in wshape.items():
        kp = min(K, 128)
        wt[k] = dram("wt_" + k, [L, kp, (K // kp) * N], BF16)
    xs = dram("xs", [8, 128, NT], F32)
    QA = dram("QA", [512, NT], BF16)
    KA = dram("KA", [128, NT], BF16)
    VA = dram("VA", [NT, 130], BF16)
    QB = dram("QB", [8, 96, NT], BF16)
    KB = dram("KB", [8, 64, NT], BF16)
    KRB = dram("KRB", [32, NT], BF16)
    VB = dram("VB", [NT, 520], BF16)
    QC = dram("QC", [1024, NT], BF16)
    KC = dram("KC", [1024, NT], BF16)
    VC = dram("VC", [NT, 1024], BF16)
    OT = dram("OT", [1024, NT], BF16)
    bar_a = dram("bar_a", [1, 64], F32)
    bar_b = dram("bar_b", [1, 64], F32)

    with ExitStack() as top:
        esems = {e: top.enter_context(nc.semaphore("es_" + e)) for e in Prog.ENGS}
        dhandles = [top.enter_context(nc.semaphore("ds%d" % i)) for i in range(60)]
        block = top.enter_context(nc.Block())
        P = Prog(nc, esems, dhandles)
        bsem = P.new_dsem()

        uid = {"n": 0}

        def sb(stack, name, shape, dt, dma=False):
            uid["n"] += 1
            t = stack.enter_context(nc.sbuf_tensor("s%d_%s" % (uid["n"], name), list(shape), dt))
            return Buf(t, P.new_dsem() if dma else None)

        def ps(stack, name):
            uid["n"] += 1
            return Buf(stack.enter_context(nc.psum_tensor("p%d_%s" % (uid["n"], name), [128, 512], F32)))

        def ACT(out, in_, func, reads, writes, **kw):
            return P.add("act", lambda e: e.activation(out=out, in_=in_, func=func, **kw), reads, writes)

        def TTo(eng, out, a, b, op, reads, writes):
            return P.add(eng, lambda e: e.tensor_tensor(out, a, b, op), reads, writes)

        def STT(eng, out, in0, scalar, in1, op0, op1, reads, writes):
            return P.add(eng, lambda e: e.scalar_tensor_tensor(out, in0, scalar, in1, op0=op0, op1=op1), reads, writes)

        def TS(eng, out, in0, s1, s2, op0, op1, reads, writes):
            return P.add(eng, lambda e: e.tensor_scalar(out, in0, s1, s2, op0=op0, op1=op1), reads, writes)

        def TSS(eng, out, in0, s, op, reads, writes):
            return P.add(eng, lambda e: e.tensor_single_scalar(out, in0, s, op), reads, writes)

        def CP(eng, out, in_, reads, writes):
            return P.add(eng, lambda e: e.tensor_copy(out, in_), reads, writes)

        def RSTD(dst, src_ps, inv_n):
            ACT(dst[:], src_ps[:], AF.Ln, [src_ps, epsb], [dst], bias=epsb[:, 0:1], scale=inv_n)
            ACT(dst[:], dst[:], AF.Exp, [dst], [dst], scale=-0.5)

        def RCP(out, in_, reads, writes):
            return P.add("dve", lambda e: e.reciprocal(out, in_), reads, writes)

        def MSET(ap, val, writes):
            return P.add("pool", lambda e: e.memset(ap, val), (), writes)

        def DMA(eng, out, in_, sem, reads=(), writes=(), extra=()):
            return P.dma(eng, lambda e: e.dma_start(out=out, in_=in_), sem, reads, writes, extra)

        def DMAN(eng, pairs, sem, reads=(), writes=(), extra=()):
            pairs = list(pairs)
            return P.dma(eng, lambda e: [e.dma_start(out=a, in_=b) for a, b in pairs], sem, reads, writes, extra,
                         n=len(pairs))

        def MM(out_ap, pairs, reads, writes, first=True, last=True):
            pairs = list(pairs)

            def fn(e):
                ins = None
                n = len(pairs)
                for i, (a, b) in enumerate(pairs):
                    ins = e.matmul(out_ap, a, b, start=(first and i == 0), stop=(last and i == n - 1))
                return ins
            return P.add("pe", fn, reads, writes)

        def TRN(pairs, reads, writes):
            pairs = list(pairs)

            def fn(e):
                ins = None
                for o_, i_ in pairs:
                    ins = e.transpose(o_, i_, ident.t[:])
                return ins
            return P.add("pe", fn, list(reads) + [ident], writes)

        gall = sb(top, "gall", [128, NG], F32, dma=True)
        ones_bf = sb(top, "ones_bf", [128, 128], BF16)
        ones_f = sb(top, "ones_f", [128, 128], F32)
        bd64 = sb(top, "bd64", [128, 128], BF16)
        ident = sb(top, "ident", [128, 128], F32)
        nlam = sb(top, "nlam", [128, 2], F32)
        subg = sb(top, "subg", [128, 2], F32)

        epsb = sb(top, "epsb", [128, 1], F32)
        MSET(epsb[:], EPS, [epsb])
        DMA("sp", gall[:], gall_in, gall.dsem, writes=[gall])
        MSET(ones_bf[:], 1.0, [ones_bf])
        MSET(ones_f[:], 1.0, [ones_f])
        MSET(bd64[:], 0.0, [bd64])
        MSET(bd64[0:64, 0:64], 1.0, [bd64])
        MSET(bd64[64:128, 64:128], 1.0, [bd64])
        P.add("pool", lambda e: e.iota(ident[:], pattern=[[1, 128]], base=0, channel_multiplier=-1,
                                       allow_small_or_imprecise_dtypes=True), writes=[ident])
        TSS("dve", ident[:], ident[:], 0.0, ALU.is_equal, [ident], [ident])

        NWS = 8
        wsem = [P.new_dsem() for _ in range(NWS)]
        wi = 0
        order = ["f1g", "f1u", "f1d", "abin", "wuq", "wukv", "about", "cin", "cout", "f2g", "f2u", "f2d",
                 "pleg", "plep"]
        for l in range(4):
            for k in order:
                L, K, N = wshape[k]
                if l >= L:
                    continue
                kp = min(K, 128)
                src = wsrc[k][l].rearrange("(k p) n -> p k n", p=kp)
                dst = wt[k][l].rearrange("p (k n) -> p k n", n=N)
                nk = K // kp
                step = max(1, nk // 4) if nk >= 8 else nk
                for k0 in range(0, nk, step):
                    k1 = min(nk, k0 + step)
                    DMA("pool", dst[:, k0:k1, :], src[:, k0:k1, :], wsem[wi % NWS])
                    wi += 1

        with ExitStack() as st0:
            lv = sb(st0, "lv", [1, 512], F32, dma=True)
            lp = sb(st0, "lp", [1, 256], F32)
            ls = sb(st0, "ls", [1, 4], F32)
            le = sb(st0, "le", [1, 4], F32)
            ln2 = sb(st0, "ln2", [1, 2], F32)
            pst = ps(st0, "ps_pro")
            DMA("sp", lv[:], lamv_in, lv.dsem, writes=[lv])
            lvv = lv.t[:].rearrange("p (o f d) -> p o f d", o=2, f=4)
            lpv = lp.t[:].rearrange("p (o f d) -> p o f d", o=2, f=2)
            for o in range(2):
                for f in range(2):
                    TTo("dve", lpv[:, o, f, :], lvv[:, o, 2 * f, :], lvv[:, o, 2 * f + 1, :], ALU.mult, [lv], [lp])
            lp3 = lp.t[:].rearrange("p (g d) -> p g d", d=64)
            P.add("dve", lambda e: e.reduce_sum(ls[:, 0:4], lp3, axis=mybir.AxisListType.X), reads=[lp], writes=[ls])
            ACT(le[:], ls[:], AF.Exp, [ls], [le])
            lev = le.t[:].rearrange("p (o f) -> p o f", f=2)
            for o in range(2):
                li = 0.8 - 0.6 * math.exp(-0.3 * (2 * o + 1))
                STT("dve", ln2[:, o:o + 1], lev[:, o, 1:2], -li, lev[:, o, 0:1], ALU.add, ALU.subtract, [le], [ln2])
            MM(pst.t[:, 0:2], [(ones_f[0:1, :], ln2[0:1, :])], [ln2, ones_f], [pst])
            CP("dve", nlam[:], pst.t[:, 0:2], [pst], [nlam])
            for o in range(2):
                li = 0.8 - 0.6 * math.exp(-0.3 * (2 * o + 1))
                c0 = GCOL[("sub", o)]
                TSS("dve", subg[:, o:o + 1], gall[:, c0:c0 + 1], 1.0 - li, ALU.mult, [gall], [subg])
            P.barrier(bsem, bar_a, bar_b)

        def rowlocal_pass(ipass):
            with ExitStack() as st:
                RING = 5
                ring = [sb(st, "ring%d" % i, [128, 5632], BF16, dma=True) for i in range(RING)]
                xbs = [sb(st, "xb%d" % i, [128, 8, TT], F32, dma=True) for i in range(2)]
                xb = xbs[0]
                hn = sb(st, "hn", [128, 8, TT], BF16)
                act = sb(st, "act", [128, 22, TT], BF16)
                sq = sb(st, "sq", [128, 8, TT], BF16)
                rstd = sb(st, "rstd", [128, TT], F32)
                tmpf = [sb(st, "tmpf%d" % i, [128, TT], F32) for i in range(6)]
                tab = [sb(st, "tab%d" % i, [128, TT], F32, dma=True) for i in range(4)]
                otb = sb(st, "otb", [128, 8, TT], BF16, dma=True)
                xin = [sb(st, "xin%d" % i, [128, D], F32, dma=True) for i in range(2)]
                ptk = sb(st, "ptk", [128, 4, 256], F32, dma=True)
                pT = sb(st, "pT", [128, 2, TT], BF16)
                NSTG = 8
                stg = [sb(st, "stg%d" % i, [128, TT], BF16, dma=True) for i in range(NSTG)]
                vstg = [sb(st, "vstg%d" % i, [128, 520], BF16, dma=True) for i in range(2)]
                cqn_b = [sb(st, "cqn%d" % i, [128, TT], BF16) for i in range(2)]
                ckvn_b = sb(st, "ckvn", [128, TT], BF16)
                pG = [ps(st, "pG%d" % i) for i in range(2)]
                pU = [ps(st, "pU%d" % i) for i in range(2)]
                pD = [ps(st, "pD%d" % i) for i in range(2)]
                pN = ps(st, "pN")
                pTp = ps(st, "pTp")
                cnt = {"ring": 0, "stg": 0, "vstg": 0, "G": 0, "D": 0, "tmp": 0, "xin": 0}

                for v in vstg:
                    MSET(v[:], 1.0, [v])

                def slab(wname, l, K, N, c0, ncols):
                    b = ring[cnt["ring"] % RING]
                    cnt["ring"] += 1
                    kp = min(K, 128)
                    nk = K // kp
                    src = wt[wname][l].rearrange("p (k n) -> p k n", n=N)[:, :, c0:c0 + ncols]
                    view = b.t[0:kp, 0:nk * ncols].rearrange("p (k n) -> p k n", n=ncols)
                    DMA("sp", view, src, b.dsem, writes=[b])
                    return b, view

                def next_tmp():
                    t = tmpf[cnt["tmp"] % len(tmpf)]
                    cnt["tmp"] += 1
                    return t

                def next_stg():
                    s_ = stg[cnt["stg"] % NSTG]
                    cnt["stg"] += 1
                    return s_

                def next_D():
                    d_ = pD[cnt["D"] % 2]
                    cnt["D"] += 1
                    return d_

                def rstd_from(ps_buf, inv_n):
                    RSTD(rstd, ps_buf, inv_n)

                def rmsnorm_stats(src_aps, src_bufs, inv_n):
                    n = len(src_aps)
                    for c, (ap, bb) in enumerate(zip(src_aps, src_bufs)):
                        if c % 2 == 0:
                            ACT(sq[:, c, :], ap, AF.Square, [bb], [sq])
                        else:
                            TTo("dve", sq[:, c, :], ap, ap, ALU.mult, [bb], [sq])
                    MM(pN[:], [(ones_bf[:], sq[:, c, :]) for c in range(n)], [sq, ones_bf], [pN])
                    rstd_from(pN, inv_n)

                def norm_x(gcol):
                    rmsnorm_stats([xb[:, c, :] for c in range(8)], [xb] * 8, 1.0 / D)
                    for c in range(8):
                        STT("dve", hn[:, c, :], xb[:, c, :], gall[:, gcol + c:gcol + c + 1], rstd[:], ALU.mult, ALU.mult,
                            [xb, rstd, gall], [hn])

                def ffn(wg, wu, wd, l, gcol):
                    norm_x(gcol)
                    for s0 in range(0, DFF, 512):
                        ncols = min(512, DFF - s0)
                        bg, vg = slab(wg, l, D, DFF, s0, ncols)
                        bu, vu = slab(wu, l, D, DFF, s0, ncols)
                        for j in range(ncols // 128):
                            f = (s0 // 128) + j
                            G = pG[cnt["G"] % 2]
                            U = pU[cnt["G"] % 2]
                            cnt["G"] += 1
                            MM(G[:], [(vg[:, k, j * 128:(j + 1) * 128], hn[:, k, :]) for k in range(8)], [bg, hn], [G])
                            MM(U[:], [(vu[:, k, j * 128:(j + 1) * 128], hn[:, k, :]) for k in range(8)], [bu, hn], [U])
                            t = next_tmp()
                            ACT(t[:], G[:], AF.Silu, [G], [t])
                            TTo("dve", act[:, f, :], t[:], U[:], ALU.mult, [t, U], [act])
                    for m0 in range(0, D, 256):
                        bd_, vd = slab(wd, l, DFF, D, m0, 256)
                        for j in range(2):
                            m = m0 // 128 + j
                            Dp = next_D()
                            MM(Dp[:], [(vd[:, k, j * 128:(j + 1) * 128], act[:, k, :]) for k in range(22)], [bd_, act], [Dp])
                            STT("dve", xb[:, m, :], Dp[:], 0.5, xb[:, m, :], ALU.mult, ALU.add, [Dp, xb], [xb])

                def proj(wname, l, K, N, c0, ncols, rhs_aps, rhs_bufs, consume):
                    nk = len(rhs_aps)
                    done = 0
                    while done < ncols:
                        sc = min(512, ncols - done)
                        b, v = slab(wname, l, K, N, c0 + done, sc)
                        for j in range((sc + 127) // 128):
                            mcols = min(128, sc - j * 128)
                            Dp = next_D()
                            MM(Dp.t[0:mcols, :], [(v[:, k, j * 128:j * 128 + mcols], rhs_aps[k]) for k in range(nk)],
                               [b] + list(rhs_bufs), [Dp])
                            consume((done // 128) + j, Dp, mcols)
                        done += sc

                def proj_tok(wname, l, K, N, c0, ncols, lhs_fn, lhs_bufs, nk, consume):
                    b, v = slab(wname, l, K, N, c0, ncols)
                    for s in range(4):
                        Dp = next_D()
                        MM(Dp.t[:, 0:ncols], [(lhs_fn(k, s), v[:, k, :]) for k in range(nk)], [b] + list(lhs_bufs), [Dp])
                        consume(s, Dp)

                def evac_to(dst_buf, rows=128):
                    def cons(j, Dp, mcols):
                        ACT(dst_buf.t[0:rows, :], Dp.t[0:rows, :], AF.Identity, [Dp], [dst_buf])
                    return cons

                hn_aps = [hn[:, k, :] for k in range(8)]

                for ti in range(DBG.get("ntile", NTILE)):
                    t0 = ti * TT
                    tsl = slice(t0, t0 + TT)
                    xb = xbs[ti % 2]
                    if ipass == 0:
                        for s in range(4):
                            xi = xin[cnt["xin"] % 2]
                            cnt["xin"] += 1
                            DMA("sp", xi[:], x_in[t0 + s * 128:t0 + (s + 1) * 128, :], xi.dsem, writes=[xi])
                            for c0 in range(0, 8, 4):
                                TRN([(pTp.t[:, cc * 128:(cc + 1) * 128], xi.t[:, (c0 + cc) * 128:(c0 + cc + 1) * 128])
                                     for cc in range(4)], [xi], [pTp])
                                CP("dve", xb.t[:, c0:c0 + 4, s * 128:(s + 1) * 128],
                                   pTp.t[:].rearrange("p (c t) -> p c t", t=128), [pTp], [xb])
                    else:
                        DMA("sp", xb[:], xs[:, :, tsl].rearrange("c p t -> p c t"), xb.dsem, writes=[xb])

                    if ipass > 0:
                        lp_ = ipass - 1
                        DMA("sp", otb[:], OT[:, tsl].rearrange("(c p) t -> p c t", p=128), otb.dsem, writes=[otb])
                        wo = "about" if lp_ % 2 == 0 else "cout"

                        def cons_out(m, Dp, mcols):
                            TTo("dve", xb[:, m, :], Dp[:], xb[:, m, :], ALU.add, [Dp, xb], [xb])
                        proj(wo, lp_ // 2, D, D, 0, D, [otb[:, k, :] for k in range(8)], [otb], cons_out)
                        ffn("f2g", "f2u", "f2d", lp_, GCOL[("ffn2", lp_)])
                        DMA("sp", ptk[:], p_in[lp_, t0:t0 + TT, :].rearrange("(s p) d -> p s d", p=128), ptk.dsem,
                            writes=[ptk])
                        for dc in range(2):
                            TRN([(pTp.t[:, s * 128:(s + 1) * 128], ptk.t[:, s, dc * 128:(dc + 1) * 128]) for s in range(4)],
                                [ptk], [pTp])
                            CP("dve", pT[:, dc, :], pTp[:], [pTp], [pT])
                        norm_x(GCOL[("ple", lp_)])
                        for m0 in range(0, D, 512):
                            bg_, vg_ = slab("pleg", lp_, D, D, m0, 512)
                            bp_, vp_ = slab("plep", lp_, 256, D, m0, 512)
                            for j in range(4):
                                m = m0 // 128 + j
                                G = pG[cnt["G"] % 2]
                                U = pU[cnt["G"] % 2]
                                cnt["G"] += 1
                                MM(G[:], [(vg_[:, k, j * 128:(j + 1) * 128], hn[:, k, :]) for k in range(8)], [bg_, hn], [G])
                                MM(U[:], [(vp_[:, k, j * 128:(j + 1) * 128], pT[:, k, :]) for k in range(2)], [bp_, pT], [U])
                                t = next_tmp()
                                ACT(t[:], G[:], AF.Sigmoid, [G], [t])
                                t2 = next_tmp()
                                TTo("dve", t2[:], t[:], U[:], ALU.mult, [t, U], [t2])
                                TTo("dve", xb[:, m, :], t2[:], xb[:, m, :], ALU.add, [t2, xb], [xb])

                    if ipass < DEPTH:
                        li = ipass
                        ffn("f1g", "f1u", "f1d", li, GCOL[("ffn1", li)])
                        norm_x(GCOL[("mix", li)])
                        if li % 2 == 0:
                            e_ = li // 2
                            for i_, (src_, row) in enumerate(((ropeA_in, 0), (ropeA_in, 1), (ropeB_in, 0), (ropeB_in, 1))):
                                DMA("sp", tab[i_][:], src_[row, :, tsl], tab[i_].dsem, writes=[tab[i_]])
                            for (nm, nch, gq, dst) in (("q", 4, "aq", QA), ("k", 1, "ak", KA)):
                                for c in range(nch):
                                    z = next_tmp()
                                    zw = next_tmp()
                                    proj("abin", e_, D, ABX_N, ABX[nm] + c * 128, 128, hn_aps, [hn], evac_to(z))
                                    proj("abin", e_, D, ABX_N, ABX[nm + "_sw"] + c * 128, 128, hn_aps, [hn], evac_to(zw))
                                    ACT(sq[:, 0, :], z[:], AF.Square, [z], [sq])
                                    MM(pN[:], [(bd64[:], sq[:, 0, :])], [sq, bd64], [pN])
                                    rstd_from(pN, 1.0 / 64)
                                    g0 = GCOL[(gq, e_)]
                                    g1 = GCOL[(gq + "_sw", e_)]
                                    STT("dve", z[:], z[:], gall[:, g0:g0 + 1], tab[0][:], ALU.mult, ALU.mult,
                                        [z, gall, tab[0]], [z])
                                    STT("dve", zw[:], zw[:], gall[:, g1:g1 + 1], tab[1][:], ALU.mult, ALU.mult,
                                        [zw, gall, tab[1]], [zw])
                                    TTo("dve", z[:], z[:], zw[:], ALU.add, [z, zw], [z])
                                    s_ = next_stg()
                                    TTo("dve", s_[:], z[:], rstd[:], ALU.mult, [z, rstd], [s_])
                                    DMA("pool", dst[c * 128:(c + 1) * 128, tsl], s_[:], s_.dsem, reads=[s_])
                            vs = vstg[cnt["vstg"] % 2]
                            cnt["vstg"] += 1

                            def consVA(s, Dp, vs=vs, t0=t0):
                                CP("dve", vs.t[:, 0:130].rearrange("p (g d) -> p g d", d=65)[:, :, 0:64],
                                   Dp.t[:, 0:128].rearrange("p (g d) -> p g d", d=64), [Dp], [vs])
                                DMA("pool", VA[t0 + s * 128:t0 + (s + 1) * 128, :], vs.t[:, 0:130], vs.dsem, reads=[vs])
                            proj_tok("abin", e_, D, ABX_N, ABX["v"], 128,
                                     lambda k, s: hn[:, k, s * 128:(s + 1) * 128], [hn], 8, consVA)
                            cq = [next_tmp(), next_tmp()]
                            for c in range(2):
                                proj("abin", e_, D, ABX_N, ABX["cq"] + c * 128, 128, hn_aps, [hn], evac_to(cq[c]))
                            rmsnorm_stats([cq[0][:], cq[1][:]], cq, 1.0 / 256)
                            cqn = cqn_b
                            for c in range(2):
                                gq_ = GCOL[("bq", e_)] + c
                                STT("dve", cqn[c][:], cq[c][:], gall[:, gq_:gq_ + 1], rstd[:], ALU.mult, ALU.mult,
                                    [cq[c], gall, rstd], [cqn[c]])
                            ckv = next_tmp()
                            proj("abin", e_, D, ABX_N, ABX["ckv"], 128, hn_aps, [hn], evac_to(ckv))
                            rmsnorm_stats([ckv[:]], [ckv], 1.0 / 128)
                            ckvn = ckvn_b
                            gk_ = GCOL[("bkv", e_)]
                            STT("dve", ckvn[:], ckv[:], gall[:, gk_:gk_ + 1], rstd[:], ALU.mult, ALU.mult,
                                [ckv, gall, rstd], [ckvn])
                            kr = next_tmp()
                            krw = next_tmp()
                            proj("abin", e_, D, ABX_N, ABX["kr"], 32, hn_aps, [hn], evac_to(kr, 32))
                            proj("abin", e_, D, ABX_N, ABX["kr_sw"], 32, hn_aps, [hn], evac_to(krw, 32))
                            TTo("dve", kr[0:32, :], kr[0:32, :], tab[2][0:32, :], ALU.mult, [kr, tab[2]], [kr])
                            TTo("dve", krw[0:32, :], krw[0:32, :], tab[3][0:32, :], ALU.mult, [krw, tab[3]], [krw])
                            s_ = next_stg()
                            TTo("dve", s_[0:32, :], kr[0:32, :], krw[0:32, :], ALU.add, [kr, krw], [s_])
                            DMA("pool", KRB[:, tsl], s_[0:32, :], s_.dsem, reads=[s_])

                            def cons_qn(j, Dp, mcols, tsl=tsl):
                                s_ = next_stg()
                                CP("dve", s_[:], Dp[:], [Dp], [s_])
                                DMAN("pool", [(QB[2 * j + hh_, 0:64, tsl], s_.t[hh_ * 64:(hh_ + 1) * 64, :]) for hh_ in range(2)],
                                     s_.dsem, reads=[s_])
                            proj("wuq", e_, 256, 1024, 0, 512, [cqn[0][:], cqn[1][:]], cqn, cons_qn)
                            for c in range(2):
                                z = next_tmp()
                                zw = next_tmp()
                                proj("wuq", e_, 256, 1024, 512 + c * 128, 128, [cqn[0][:], cqn[1][:]], cqn, evac_to(z))
                                proj("wuq", e_, 256, 1024, 768 + c * 128, 128, [cqn[0][:], cqn[1][:]], cqn, evac_to(zw))
                                TTo("dve", z[:], z[:], tab[2][:], ALU.mult, [z, tab[2]], [z])
                                TTo("dve", zw[:], zw[:], tab[3][:], ALU.mult, [zw, tab[3]], [zw])
                                s_ = next_stg()
                                TTo("dve", s_[:], z[:], zw[:], ALU.add, [z, zw], [s_])
                                DMAN("pool", [(QB[4 * c + hh_, 64:96, tsl], s_.t[hh_ * 32:(hh_ + 1) * 32, :]) for hh_ in range(4)],
                                     s_.dsem, reads=[s_])

                            def cons_kn(j, Dp, mcols, tsl=tsl):
                                s_ = next_stg()
                                CP("dve", s_[:], Dp[:], [Dp], [s_])
                                DMAN("pool", [(KB[2 * j + hh_, :, tsl], s_.t[hh_ * 64:(hh_ + 1) * 64, :]) for hh_ in range(2)],
                                     s_.dsem, reads=[s_])
                            proj("wukv", e_, 128, 1024, 0, 512, [ckvn[:]], [ckvn], cons_kn)
                            vs = vstg[cnt["vstg"] % 2]
                            cnt["vstg"] += 1

                            def consVB(s, Dp, vs=vs, t0=t0):
                                CP("dve", vs.t[:, 0:520].rearrange("p (g d) -> p g d", d=65)[:, :, 0:64],
                                   Dp.t[:, 0:512].rearrange("p (g d) -> p g d", d=64), [Dp], [vs])
                                DMA("pool", VB[t0 + s * 128:t0 + (s + 1) * 128, :], vs.t[:, 0:520], vs.dsem, reads=[vs])
                            proj_tok("wukv", e_, 128, 1024, 512, 512,
                                     lambda k, s: ckvn[:, s * 128:(s + 1) * 128], [ckvn], 1, consVB)
                        else:
                            o_ = li // 2
                            for which, dst in ((0, QC), (1024, KC)):
                                def cons_qk(j, Dp, mcols, dst=dst, tsl=tsl):
                                    s_ = next_stg()
                                    CP("dve", s_[:], Dp[:], [Dp], [s_])
                                    DMA("pool", dst[j * 128:(j + 1) * 128, tsl], s_[:], s_.dsem, reads=[s_])
                                proj("cin", o_, D, 3072, which, 1024, hn_aps, [hn], cons_qk)
                            for half in range(2):
                                def consVC(s, Dp, half=half, t0=t0):
                                    s_ = next_stg()
                                    CP("dve", s_[:], Dp[:], [Dp], [s_])
                                    DMA("pool", VC[t0 + s * 128:t0 + (s + 1) * 128, half * 512:(half + 1) * 512], s_[:],
                                        s_.dsem, reads=[s_])
                                proj_tok("cin", o_, D, 3072, 2048 + half * 512, 512,
                                         lambda k, s: hn[:, k, s * 128:(s + 1) * 128], [hn], 8, consVC)
                        DMA("pool", xs[:, :, tsl].rearrange("c p t -> p c t"), xb[:], xb.dsem, reads=[xb])
                    else:
                        rmsnorm_stats([xb[:, c, :] for c in range(8)], [xb] * 8, 1.0 / D)
                        gcol = GCOL["final"]
                        for c in range(8):
                            STT("dve", xb[:, c, :], xb[:, c, :], gall[:, gcol + c:gcol + c + 1], rstd[:], ALU.mult, ALU.mult,
                                [xb, rstd, gall], [xb])
                        for s in range(4):
                            xi = xin[cnt["xin"] % 2]
                            cnt["xin"] += 1
                            for c0 in range(0, 8, 4):
                                TRN([(pTp.t[:, cc * 128:(cc + 1) * 128], xb.t[:, c0 + cc, s * 128:(s + 1) * 128])
                                     for cc in range(4)], [xb], [pTp])
                                CP("dve", xi.t[:, c0 * 128:(c0 + 4) * 128], pTp[:], [pTp], [xi])
                            DMA("pool", y_out[t0 + s * 128:t0 + (s + 1) * 128, :], xi[:], xi.dsem, reads=[xi])
                P.barrier(bsem, bar_a, bar_b)
                P.release_dsems([b.dsem for b in ring + xbs + [otb, ptk] + tab + xin + stg + vstg])

        def ps2(stack, name):
            uid["n"] += 1
            return Buf(stack.enter_context(nc.psum_tensor("p%d_%s" % (uid["n"], name), [128, 1024], F32)))

        def MMS(specs, reads, writes):
            specs = list(specs)

            def fn(e):
                ins = None
                for (o_, a_, b_, s0_, s1_) in specs:
                    ins = e.matmul(o_, a_, b_, start=s0_, stop=s1_)
                return ins
            return P.add("pe", fn, reads, writes)

        def attn_even(e_):
            with ExitStack() as st:
                S2 = [ps2(st, "S2_%d" % i) for i in range(2)]
                O = [ps(st, "O%d" % i) for i in range(2)]
                Bp = ps(st, "Bp")
                Kt = [sb(st, "Kt%d" % i, [128, NT], BF16, dma=True) for i in range(2)]
                Qt = [sb(st, "Qt%d" % i, [128, NT], BF16, dma=True) for i in range(2)]
                Vt = sb(st, "Vt", [128, 64, 584], BF16, dma=True)
                maskb = sb(st, "maskb", [128, 1024], F32, dma=True)
                NP_ = 3
                Pt = [sb(st, "Pt%d" % i, [128, 2 * TT], BF16) for i in range(NP_)]
                rl = [sb(st, "rl%d" % i, [128, TT], F32) for i in range(2)]
                osb = [sb(st, "osb%d" % i, [128, TT], F32) for i in range(2)]
                ostg = [sb(st, "ostg%d" % i, [128, TT], BF16, dma=True) for i in range(4)]
                DMA("sp", maskb[:], maskb_in, maskb.dsem, writes=[maskb])
                for b_ in Kt + Qt + [Vt]:
                    MSET(b_[:], 0.0, [b_])
                qcount = 0
                ecount = 0
                for mixer in ("A", "B"):
                    if mixer == "A":
                        Kd, scale, vw, Vsrc = 128, 64 ** -0.5, 130, VA
                    else:
                        Kd, scale, vw, Vsrc = 96, 96 ** -0.5, 520, VB
                    vsrc = Vsrc.rearrange("(t p) c -> p t c", p=128)
                    DMAN("sp", [(Vt.t[:, 16 * i:16 * (i + 1), 0:vw], vsrc[:, 16 * i:16 * (i + 1), :]) for i in range(4)],
                         Vt.dsem, writes=[Vt])
                    qbase = qcount
                    qcount += 8

                    def Kbuf(h):
                        return Kt[(h // 4) % 2] if mixer == "A" else Kt[h % 2]

                    def Qbuf(h):
                        return Qt[(qbase + h) % 2]

                    def load_head(h):
                        q_ = Qbuf(h)
                        k_ = Kbuf(h)
                        if mixer == "A":
                            g = h // 4
                            if h % 4 == 0:
                                DMA("sp", k_.t[0:64, :], KA[g * 64:(g + 1) * 64, :], k_.dsem, writes=[k_])
                            DMA("sp", q_.t[0:64, :], QA[h * 64:(h + 1) * 64, :], q_.dsem, writes=[q_])
                        else:
                            DMAN("sp", [(k_.t[0:64, :], KB[h, :, :]), (k_.t[64:96, :], KRB[:, :])], k_.dsem, writes=[k_])
                            DMA("sp", q_.t[0:96, :], QB[h, :, :], q_.dsem, writes=[q_])

                    LA = 2
                    n = 8 * 8 * 64
                    load_head(0)
                    for i in range(n + LA):
                        if i % 512 == LA and (i // 512) + 1 < 8:
                            load_head(i // 512 + 1)
                        if i < n:
                            h, qp, t = i // 512, (i % 512) // 64, i % 64
                            K_, Q_ = Kbuf(h), Qbuf(h)
                            Sb = S2[i % 2]
                            kt_ = K_.t[0:Kd, t * 128:(t + 1) * 128]
                            MMS([(Sb.t[:, 0:TT], kt_, Q_.t[0:Kd, (2 * qp) * TT:(2 * qp + 1) * TT], True, True),
                                 (Sb.t[:, TT:2 * TT], kt_, Q_.t[0:Kd, (2 * qp + 1) * TT:(2 * qp + 2) * TT], True, True)],
                                [K_, Q_], [Sb])
                            pt = Pt[i % NP_]
                            blk = (2 * qp) * 64 + t
                            ACT(pt[:], Sb[:], AF.Exp, [Sb, maskb], [pt], bias=maskb[:, blk:blk + 1], scale=scale)
                        j = i - LA
                        if j >= 0:
                            h, qp, t = j // 512, (j % 512) // 64, j % 64
                            vcol = (h // 4) * 65 if mixer == "A" else h * 65
                            orow = (0 if mixer == "A" else 512) + h * 64
                            pt = Pt[j % NP_]
                            vt_ = Vt.t[:, t, vcol:vcol + 128]
                            MMS([(O[0].t[:, :], vt_, pt.t[:, 0:TT], t == 0, t == 63),
                                 (O[1].t[:, :], vt_, pt.t[:, TT:2 * TT], t == 0, t == 63)], [pt, Vt], [O[0], O[1]])
                            if t == 63:
                                for k2 in range(2):
                                    CP("dve", osb[k2][0:65, :], O[k2].t[0:65, :], [O[k2]], [osb[k2]])
                                for k2 in range(2):
                                    RCP(rl[k2][64:65, :], osb[k2][64:65, :], [osb[k2]], [rl[k2]])
                                for k2 in range(2):
                                    MM(Bp.t[0:64, :], [(ones_f[64:65, 0:64], rl[k2][64:65, :])], [rl[k2], ones_f], [Bp])
                                    og = ostg[ecount % 4]
                                    ecount += 1
                                    TTo("dve", og[0:64, :], osb[k2][0:64, :], Bp.t[0:64, :], ALU.mult, [osb[k2], Bp], [og])
                                    qb = 2 * qp + k2
                                    DMA("pool", OT[orow:orow + 64, qb * TT:(qb + 1) * TT], og[0:64, :], og.dsem, reads=[og])
                P.barrier(bsem, bar_a, bar_b)
                P.release_dsems([b.dsem for b in Kt + Qt + [Vt, maskb] + ostg])

        ALIBI_THR = 40.0

        def attn_odd(o_):
            with ExitStack() as st:
                S = [ps(st, "S%d" % i) for i in range(4)]
                O1 = ps(st, "O1")
                O2 = ps(st, "O2")
                L1 = ps(st, "L1")
                L2 = ps(st, "L2")
                Kt = [sb(st, "Kt%d" % i, [128, NT], BF16, dma=True) for i in range(2)]
                Qt = [sb(st, "Qt%d" % i, [128, NT], BF16, dma=True) for i in range(2)]
                Vt = [sb(st, "Vt%d" % i, [128, 64, 128], BF16, dma=True) for i in range(2)]
                maskb = sb(st, "maskb", [128, 1024], F32, dma=True)
                cdist = sb(st, "cdist", [128, 1024], F32, dma=True)
                dtl = sb(st, "dtl", [128, 6, 512], F32, dma=True)
                biasC = [sb(st, "biasC%d" % i, [128, 1024], F32) for i in range(2)]
                NP_ = 3
                Pt = [sb(st, "Pt%d" % i, [128, 2 * TT], BF16) for i in range(NP_)]
                tm = [sb(st, "tm%d" % i, [128, 2 * TT], F32) for i in range(NP_)]
                r1 = sb(st, "r1", [128, TT], F32)
                r2 = sb(st, "r2", [128, TT], F32)
                o1 = sb(st, "o1", [128, TT], F32)
                o2 = sb(st, "o2", [128, TT], F32)
                sqo = sb(st, "sqo", [128, TT], BF16)
                rs = sb(st, "rs", [128, TT], F32)
                ostg = [sb(st, "ostg%d" % i, [128, TT], BF16, dma=True) for i in range(2)]
                DMA("sp", maskb[:], maskb_in, maskb.dsem, writes=[maskb])
                DMA("sp", cdist[:], cdist_in, cdist.dsem, writes=[cdist])
                DMA("sp", dtl[:], dtl_in, dtl.dsem, writes=[dtl])
                scale = 64 ** -0.5

                def load_head(h):
                    slope = 2.0 ** (-(h + 1))
                    K_, Q_, V_, bC = Kt[h % 2], Qt[h % 2], Vt[h % 2], biasC[h % 2]
                    DMA("sp", K_[:], KC[h * 128:(h + 1) * 128, :], K_.dsem, writes=[K_])
                    DMA("sp", Q_[:], QC[h * 128:(h + 1) * 128, :], Q_.dsem, writes=[Q_])
                    vsrc = VC[:, h * 128:(h + 1) * 128].rearrange("(t p) c -> p t c", p=128)
                    DMAN("sp", [(V_.t[:, 16 * i:16 * (i + 1), :], vsrc[:, 16 * i:16 * (i + 1), :]) for i in range(4)],
                         V_.dsem, writes=[V_])
                    STT("dve", bC[:], cdist[:], slope, maskb[:], ALU.mult, ALU.add, [cdist, maskb], [bC])

                items = []
                head_start = {}
                for h in range(8):
                    slope = 2.0 ** (-(h + 1))
                    head_start[len(items)] = h
                    for qb in range(16):
                        q_lo, q_hi = qb * TT, qb * TT + TT - 1
                        keep = []
                        for t in range(64):
                            s_lo, s_hi = t * 128, t * 128 + 127
                            mind = max(0, s_lo - q_hi, q_lo - s_hi)
                            if slope * mind < ALIBI_THR:
                                keep.append(t)
                        for t in keep:
                            items.append((h, qb, t, t == keep[0], t == keep[-1]))
                LA = 2
                n = len(items)
                load_head(0)
                ecount = 0
                for i in range(n + LA):
                    if (i - LA) in head_start and head_start[i - LA] + 1 < 8:
                        load_head(head_start[i - LA] + 1)
                    if i < n:
                        h, qb, t, fs, ls_ = items[i]
                        slope = 2.0 ** (-(h + 1))
                        K_, Q_, bC = Kt[h % 2], Qt[h % 2], biasC[h % 2]
                        Sa = S[(i % 2) * 2]
                        Sb = S[(i % 2) * 2 + 1]
                        MM(Sa[:], [(K_.t[0:64, t * 128:(t + 1) * 128], Q_.t[0:64, qb * TT:(qb + 1) * TT])], [K_, Q_], [Sa])
                        MM(Sb[:], [(K_.t[64:128, t * 128:(t + 1) * 128], Q_.t[64:128, qb * TT:(qb + 1) * TT])], [K_, Q_], [Sb])
                        if t < 4 * qb:
                            di = 0
                        elif t > 4 * qb + 3:
                            di = 1
                        else:
                            di = 2 + (t - 4 * qb)
                        tmb = tm[i % NP_]
                        fac = slope / scale
                        STT("dve", tmb[:, 0:TT], dtl[:, di, :], fac, Sa[:], ALU.mult, ALU.add, [Sa, dtl], [tmb])
                        STT("dve", tmb[:, TT:2 * TT], dtl[:, di, :], fac, Sb[:], ALU.mult, ALU.add, [Sb, dtl], [tmb])
                        pt = Pt[i % NP_]
                        blk = qb * 64 + t
                        ACT(pt[:], tmb[:], AF.Exp, [tmb, bC], [pt], bias=bC[:, blk:blk + 1], scale=scale)
                    j = i - LA
                    if j >= 0:
                        h, qb, t, fs, ls_ = items[j]
                        V_ = Vt[h % 2]
                        pt = Pt[j % NP_]
                        MMS([(O1.t[:, :], V_.t[:, t, :], pt.t[:, 0:TT], fs, ls_),
                             (L1.t[:, :], ones_bf.t[:, :], pt.t[:, 0:TT], fs, ls_),
                             (O2.t[:, :], V_.t[:, t, :], pt.t[:, TT:2 * TT], fs, ls_),
                             (L2.t[:, :], ones_bf.t[:, :], pt.t[:, TT:2 * TT], fs, ls_)],
                            [pt, V_, ones_bf], [O1, L1, O2, L2])
                        if ls_:
                            ACT(r1[:], L1[:], AF.Ln, [L1], [r1])
                            ACT(r2[:], L2[:], AF.Ln, [L2], [r2])
                            ACT(r1[:], r1[:], AF.Exp, [r1], [r1], scale=-1.0)
                            ACT(r2[:], r2[:], AF.Exp, [r2], [r2], scale=-1.0)
                            TTo("dve", o1[:], O1[:], r1[:], ALU.mult, [O1, r1], [o1])
                            TTo("dve", o2[:], O2[:], r2[:], ALU.mult, [O2, r2], [o2])
                            STT("dve", o1[:], o2[:], nlam[:, o_:o_ + 1], o1[:], ALU.mult, ALU.add, [o1, o2, nlam], [o1])
                            ACT(sqo[:], o1[:], AF.Square, [o1], [sqo])
                            MM(L1[:], [(ones_bf[:], sqo[:])], [sqo, ones_bf], [L1])
                            RSTD(rs, L1, 1.0 / 128)
                            og = ostg[ecount % 2]
                            ecount += 1
                            STT("dve", og[:], o1[:], subg[:, o_:o_ + 1], rs[:], ALU.mult, ALU.mult, [o1, subg, rs], [og])
                            DMA("pool", OT[h * 128:(h + 1) * 128, qb * TT:(qb + 1) * TT], og[:], og.dsem, reads=[og])
                P.barrier(bsem, bar_a, bar_b)
                P.release_dsems([b.dsem for b in Kt + Qt + Vt + [maskb, cdist, dtl] + ostg])

        for ipass in range(DBG["passes"]):
            rowlocal_pass(ipass)
            if ipass < DEPTH and DBG["attn"]:
                if ipass % 2 == 0:
                    attn_even(ipass // 2)
                else:
                    attn_odd(ipass // 2)
        P.add("sp", lambda e: e.nop(), extra=[P.last_barrier])
        P.emit(block)
    return nc


def _tables(seq_len):
    t = np.arange(NT)
    tl = t % seq_len
    row, col = tl // 64, tl % 64
    inv = (10000.0 ** (-np.arange(16, dtype=np.float32) * (2.0 / 32))).astype(np.float32)

    def cs(pos, d_idx):
        jj = d_idx % 32
        i = jj % 16
        ang = pos[None, :].astype(np.float32) * inv[i][:, None]
        c = np.cos(ang).astype(np.float32)
        s = np.sin(ang).astype(np.float32)
        s = np.where((jj < 16)[:, None], -s, s)
        return c, s
    d = np.arange(128)
    j = d % 64
    cA = np.zeros((128, NT), np.float32)
    sA = np.zeros((128, NT), np.float32)
    m_row = j < 32
    c1, s1 = cs(row, d)
    c2, s2 = cs(col, d)
    cA[m_row], sA[m_row] = c1[m_row], s1[m_row]
    cA[~m_row], sA[~m_row] = c2[~m_row], s2[~m_row]
    cB, sB = cs(tl, d)
    ropeA = np.stack([cA, sA]).astype(np.float32)
    ropeB = np.stack([cB, sB]).astype(np.float32)
    qb = np.arange(16)[:, None]
    tt = np.arange(64)[None, :]
    q0 = qb * 512
    s0 = tt * 128
    same = (q0 // seq_len) == (s0 // seq_len)
    maskb = np.where(same, 0.0, NEG).astype(np.float32).reshape(1, 1024)
    diag = (tt >= 4 * qb) & (tt <= 4 * qb + 3)
    cd = np.where(diag, 0.0, -np.abs(q0 - s0)).astype(np.float32).reshape(1, 1024)
    maskb = np.repeat(maskb, 128, 0)
    cd = np.repeat(cd, 128, 0)
    return ropeA, ropeB, maskb, cd


def _dtiles():
    p = np.arange(128)[:, None]
    j = np.arange(512)[None, :]
    tiles = [-(j - p), (j - p)] + [-np.abs(j - p - 128 * c) for c in range(4)]
    return np.ascontiguousarray(np.stack(tiles, 1).astype(np.float32))


_CACHE = {}


def kernel(**inp):
    f = lambda a: np.ascontiguousarray(np.asarray(a, dtype=np.float32))
    x_prompt, x_sample = f(inp["x_prompt"]), f(inp["x_sample"])
    p_prompt, p_sample = f(inp["p_prompt"]), f(inp["p_sample"])

    sw64 = np.concatenate([_swap32(64)])
    abin = f(inp["ab_w_in"])
    q_idx = np.arange(512)
    q_sw_idx = (q_idx // 64) * 64 + sw64[q_idx % 64]
    k_idx = 512 + np.arange(128)
    k_sw_idx = 512 + (np.arange(128) // 64) * 64 + sw64[np.arange(128) % 64]
    kr_idx = 1152 + np.arange(32)
    kr_sw_idx = 1152 + _swap32(32)
    cols = np.concatenate([q_idx, q_sw_idx, k_idx, k_sw_idx, 768 + np.arange(256), 1024 + np.arange(128),
                           kr_idx, kr_sw_idx, 640 + np.arange(128)])
    abin_x = np.ascontiguousarray(abin[:, :, cols])
    wuq = f(inp["b_w_uq"])
    hh = np.arange(8)[:, None]
    nope_idx = (hh * 96 + np.arange(64)[None, :]).reshape(-1)
    rope_idx = (hh * 96 + 64 + np.arange(32)[None, :]).reshape(-1)
    rope_sw_idx = (hh * 96 + 64 + _swap32(32)[None, :]).reshape(-1)
    wuq_x = np.ascontiguousarray(wuq[:, :, np.concatenate([nope_idx, rope_idx, rope_sw_idx])])
    wukv = f(inp["b_w_ukv"])
    kn_idx = (hh * 128 + np.arange(64)[None, :]).reshape(-1)
    v_idx = (hh * 128 + 64 + np.arange(64)[None, :]).reshape(-1)
    wukv_x = np.ascontiguousarray(wukv[:, :, np.concatenate([kn_idx, v_idx])])

    gall = np.zeros((128, NG), np.float32)

    def put(col, vec):
        v = np.asarray(vec, np.float32).reshape(-1, 128).T
        gall[:, col:col + v.shape[1]] = v
    for i in range(DEPTH):
        put(GCOL[("ffn1", i)], inp["ffn1_norm"][i])
        put(GCOL[("mix", i)], inp["mix_norm"][i])
        put(GCOL[("ffn2", i)], inp["ffn2_norm"][i])
        put(GCOL[("ple", i)], inp["ple_norm"][i])
    put(GCOL["final"], inp["final_norm"])
    for e in range(2):
        aq = np.asarray(inp["a_q_norm"][e], np.float32)
        ak = np.asarray(inp["a_k_norm"][e], np.float32)
        put(GCOL[("aq", e)], np.tile(aq, 2))
        put(GCOL[("aq_sw", e)], np.tile(aq[sw64], 2))
        put(GCOL[("ak", e)], np.tile(ak, 2))
        put(GCOL[("ak_sw", e)], np.tile(ak[sw64], 2))
        put(GCOL[("bq", e)], inp["b_q_norm"][e])
        put(GCOL[("bkv", e)], inp["b_kv_norm"][e])
    for o in range(2):
        put(GCOL[("sub", o)], inp["c_sub_norm"][o])
    lamv = np.stack([np.stack([f(inp["c_lambda_q1"])[o], f(inp["c_lambda_k1"])[o],
                               f(inp["c_lambda_q2"])[o], f(inp["c_lambda_k2"])[o]]) for o in range(2)])
    lamv = np.ascontiguousarray(lamv.reshape(1, 512))

    shared = {
        "f1g": f(inp["ffn1_wg"]), "f1u": f(inp["ffn1_wu"]), "f1d": f(inp["ffn1_wd"]),
        "f2g": f(inp["ffn2_wg"]), "f2u": f(inp["ffn2_wu"]), "f2d": f(inp["ffn2_wd"]),
        "abin": abin_x, "wuq": wuq_x, "wukv": wukv_x, "about": f(inp["ab_w_out"]),
        "cin": f(inp["c_w_in"]), "cout": f(inp["c_w_out"]),
        "pleg": f(inp["ple_w_gate"]), "plep": f(inp["ple_w_proj"]),
        "gall": gall, "lamv": lamv, "dtiles": _dtiles(),
    }
    tabs = {8192: _tables(8192), 2048: _tables(2048)}
    in_maps = []
    for core in range(8):
        u = core if core < N_UNITS else core - 2
        if u < 4:
            xu = x_prompt[u]
            pu = p_prompt[:, u]
            tb = tabs[8192]
        else:
            s = (u - 4) * 4
            xu = x_sample[s:s + 4].reshape(NT, D)
            pu = p_sample[:, s:s + 4].reshape(DEPTH, NT, 256)
            tb = tabs[2048]
        m = dict(shared)
        m["x"] = np.ascontiguousarray(xu)
        m["p"] = np.ascontiguousarray(pu)
        m["ropeA"], m["ropeB"], m["maskb"], m["cdist"] = tb
        in_maps.append(m)

    if "nc" not in _CACHE:
        _CACHE["nc"] = build_program()
    nc = _CACHE["nc"]
    res = run_bass_kernel_spmd(nc, in_maps, core_ids=list(range(8)))
    if DBG.get("dump"):
        _CACHE["res"] = res.results
    ys = [np.asarray(r["y"], dtype=np.float32) for r in res.results]
    y_prompt = np.stack(ys[0:4]).reshape(4, NT, D)
    y_sample = np.concatenate([ys[4].reshape(4, 2048, D), ys[5].reshape(4, 2048, D)], 0)
    return (y_prompt, y_sample)
```

```python
import math
from contextlib import ExitStack

import numpy as np
import concourse.bass as bass
import concourse.mybir as mybir
from concourse.bass_utils import run_bass_kernel_spmd

F32 = mybir.dt.float32
BF16 = mybir.dt.bfloat16
AF = mybir.ActivationFunctionType
ALU = mybir.AluOpType

DEPTH = 4
D = 1024
DFF = 2816
NT = 8192
TT = 512
NTILE = NT // TT
EPS = 1e-6
NEG = -30000.0
N_UNITS = 6

DBG = {"passes": 5, "attn": True}


class DSem:
    def __init__(self, handle):
        self.handle = handle
        self.total = 0
        self.last = None


class Op:
    __slots__ = ("eng", "fn", "deps", "is_dma", "sem", "val", "needs_inc", "n")

    def __init__(self, eng, fn, deps, is_dma=False, sem=None, n=1):
        self.eng = eng
        self.fn = fn
        self.deps = deps
        self.is_dma = is_dma
        self.sem = sem
        self.val = 0
        self.needs_inc = False
        self.n = n


class Buf:
    def __init__(self, t, dsem=None):
        self.t = t
        self.w = None
        self.r = []
        self.dsem = dsem

    def __getitem__(self, k):
        return self.t[k]


class Prog:
    ENGS = ("pe", "act", "dve", "pool", "sp")

    def __init__(self, nc, esems, dsem_handles):
        self.nc = nc
        self.ops = {e: [] for e in self.ENGS}
        self.esem = esems
        self.free_dsems = [DSem(h) for h in dsem_handles]
        self.all_dsems = list(self.free_dsems)
        self.last_barrier = None

    def new_dsem(self):
        return self.free_dsems.pop()

    def release_dsems(self, ds):
        self.free_dsems.extend(ds)

    def _deps(self, eng, reads, writes, extra):
        deps = []
        for b in reads:
            if b.w is not None:
                deps.append(b.w)
        for b in writes:
            for r in b.r:
                if r.eng != eng or r.is_dma:
                    deps.append(r)
            if b.w is not None and (b.w.eng != eng or b.w.is_dma):
                deps.append(b.w)
        deps.extend(d for d in extra if d is not None)
        if self.last_barrier is not None:
            deps.append(self.last_barrier)
        return deps

    def add(self, eng, fn, reads=(), writes=(), extra=()):
        op = Op(eng, fn, self._deps(eng, reads, writes, extra))
        self.ops[eng].append(op)
        for b in reads:
            b.r.append(op)
        for b in writes:
            b.w = op
            b.r = []
        return op

    def dma(self, eng, fn, sem, reads=(), writes=(), extra=(), n=1):
        deps = self._deps(eng, reads, writes, extra)
        if sem.last is not None:
            deps.append(sem.last)
        op = Op(eng, fn, deps, is_dma=True, sem=sem, n=n)
        sem.total += 16 * n
        op.val = sem.total
        sem.last = op
        self.ops[eng].append(op)
        for b in reads:
            b.r.append(op)
        for b in writes:
            b.w = op
            b.r = []
        return op

    def barrier(self, bsem, scratch_src, scratch_dst, skip=()):
        deps = []
        for e in self.ENGS:
            for op in reversed(self.ops[e]):
                if not op.is_dma:
                    deps.append(op)
                    break
        for ds in self.all_dsems:
            if ds.last is not None and ds not in skip:
                deps.append(ds.last)
        self.last_barrier = None
        op = self.dma("sp", lambda e: e.dma_start(out=scratch_dst, in_=scratch_src), bsem, extra=deps)
        self.last_barrier = op
        return op

    def emit(self, block):
        for e in self.ENGS:
            for op in self.ops[e]:
                for d in op.deps:
                    if not d.is_dma:
                        d.needs_inc = True
        for e in self.ENGS:
            c = 0
            for op in self.ops[e]:
                if not op.is_dma and op.needs_inc:
                    c += 1
                    op.val = c
        esem = self.esem

        def run(engname, eng):
            waited = {}
            for op in self.ops[engname]:
                need = {}
                for d in op.deps:
                    s = d.sem.handle if d.is_dma else esem[d.eng]
                    key = id(s)
                    if waited.get(key, 0) < d.val and need.get(key, (None, 0))[1] < d.val:
                        need[key] = (s, d.val)
                for key, (s, v) in need.items():
                    eng.wait_ge(s, v)
                    waited[key] = v
                ins = op.fn(eng)
                if op.is_dma:
                    if not isinstance(ins, (list, tuple)):
                        ins = [ins]
                    assert len(ins) == op.n, (len(ins), op.n)
                    for i_ in ins:
                        i_.then_inc(op.sem.handle, 16)
                elif op.needs_inc:
                    ins.then_inc(esem[engname], 1)

        block.tensor(lambda eng: run("pe", eng))
        block.scalar(lambda eng: run("act", eng))
        block.vector(lambda eng: run("dve", eng))
        block.gpsimd(lambda eng: run("pool", eng))
        block.sync(lambda eng: run("sp", eng))


def _gcols():
    cols = {}
    c = 0
    for i in range(DEPTH):
        for nm in ("ffn1", "mix", "ffn2", "ple"):
            cols[(nm, i)] = c
            c += 8
    cols["final"] = c
    c += 8
    for e in range(2):
        for nm, w in (("aq", 1), ("aq_sw", 1), ("ak", 1), ("ak_sw", 1), ("bq", 2), ("bkv", 1)):
            cols[(nm, e)] = c
            c += w
    for o in range(2):
        cols[("sub", o)] = c
        c += 1
    return cols, c


GCOL, NG = _gcols()

ABX = {"q": 0, "q_sw": 512, "k": 1024, "k_sw": 1152, "cq": 1280, "ckv": 1536, "kr": 1664, "kr_sw": 1696,
       "v": 1728}
ABX_N = 1856


def _swap32(n):
    idx = np.arange(n)
    j = idx % 32
    return np.where(j < 16, idx + 16, idx - 16)


def build_program():
    nc = bass.Bass("TRN2", target_bir_lowering=False)

    def dram(name, shape, dt, kind="Internal"):
        if name in DBG.get("dump", ()):
            kind = "ExternalOutput"
        return nc.dram_tensor(name, list(shape), dt, kind=kind).ap()

    x_in = dram("x", [NT, D], F32, "ExternalInput")
    p_in = dram("p", [DEPTH, NT, 256], F32, "ExternalInput")
    y_out = dram("y", [NT, D], F32, "ExternalOutput")
    wsrc = {}
    wshape = {
        "f1g": (4, D, DFF), "f1u": (4, D, DFF), "f1d": (4, DFF, D),
        "f2g": (4, D, DFF), "f2u": (4, D, DFF), "f2d": (4, DFF, D),
        "abin": (2, D, ABX_N), "wuq": (2, 256, 1024), "wukv": (2, 128, 1024), "about": (2, D, D),
        "cin": (2, D, 3072), "cout": (2, D, D), "pleg": (4, D, D), "plep": (4, 256, D),
    }
    for k, shp in wshape.items():
        wsrc[k] = dram(k, shp, F32, "ExternalInput")
    gall_in = dram("gall", [128, NG], F32, "ExternalInput")
    lamv_in = dram("lamv", [1, 2 * 4 * 64], F32, "ExternalInput")
    ropeA_in = dram("ropeA", [2, 128, NT], F32, "ExternalInput")
    ropeB_in = dram("ropeB", [2, 128, NT], F32, "ExternalInput")
    maskb_in = dram("maskb", [128, 1024], F32, "ExternalInput")
    cdist_in = dram("cdist", [128, 1024], F32, "ExternalInput")
    dtl_in = dram("dtiles", [128, 6, 512], F32, "ExternalInput")

    wt = {}
    for k, (L, K, N) in wshape.items():
        kp = min(K, 128)
        wt[k] = dram("wt_" + k, [L, kp, (K // kp) * N], BF16)
    xs = dram("xs", [8, 128, NT], F32)
    QA = dram("QA", [512, NT], BF16)
    KA = dram("KA", [128, NT], BF16)
    VA = dram("VA", [NT, 130], BF16)
    QB = dram("QB", [8, 96, NT], BF16)
    KB = dram("KB", [8, 64, NT], BF16)
    KRB = dram("KRB", [32, NT], BF16)
    VB = dram("VB", [NT, 520], BF16)
    QC = dram("QC", [1024, NT], BF16)
    KC = dram("KC", [1024, NT], BF16)
    VC = dram("VC", [NT, 1024], BF16)
    OT = dram("OT", [1024, NT], BF16)
    bar_a = dram("bar_a", [1, 64], F32)
    bar_b = dram("bar_b", [1, 64], F32)

    with ExitStack() as top:
        esems = {e: top.enter_context(nc.semaphore("es_" + e)) for e in Prog.ENGS}
        dhandles = [top.enter_context(nc.semaphore("ds%d" % i)) for i in range(60)]
        block = top.enter_context(nc.Block())
        P = Prog(nc, esems, dhandles)
        bsem = P.new_dsem()

        uid = {"n": 0}

        def sb(stack, name, shape, dt, dma=False):
            uid["n"] += 1
            t = stack.enter_context(nc.sbuf_tensor("s%d_%s" % (uid["n"], name), list(shape), dt))
            return Buf(t, P.new_dsem() if dma else None)

        def ps(stack, name):
            uid["n"] += 1
            return Buf(stack.enter_context(nc.psum_tensor("p%d_%s" % (uid["n"], name), [128, 512], F32)))

        def ACT(out, in_, func, reads, writes, **kw):
            return P.add("act", lambda e: e.activation(out=out, in_=in_, func=func, **kw), reads, writes)

        def TTo(eng, out, a, b, op, reads, writes):
            return P.add(eng, lambda e: e.tensor_tensor(out, a, b, op), reads, writes)

        def STT(eng, out, in0, scalar, in1, op0, op1, reads, writes):
            return P.add(eng, lambda e: e.scalar_tensor_tensor(out, in0, scalar, in1, op0=op0, op1=op1), reads, writes)

        def TS(eng, out, in0, s1, s2, op0, op1, reads, writes):
            return P.add(eng, lambda e: e.tensor_scalar(out, in0, s1, s2, op0=op0, op1=op1), reads, writes)

        def TSS(eng, out, in0, s, op, reads, writes):
            return P.add(eng, lambda e: e.tensor_single_scalar(out, in0, s, op), reads, writes)

        def CP(eng, out, in_, reads, writes):
            return P.add(eng, lambda e: e.tensor_copy(out, in_), reads, writes)

        def RSTD(dst, src_ps, inv_n):
            ACT(dst[:], src_ps[:], AF.Ln, [src_ps, epsb], [dst], bias=epsb[:, 0:1], scale=inv_n)
            ACT(dst[:], dst[:], AF.Exp, [dst], [dst], scale=-0.5)

        def RCP(out, in_, reads, writes):
            return P.add("dve", lambda e: e.reciprocal(out, in_), reads, writes)

        def MSET(ap, val, writes):
            return P.add("pool", lambda e: e.memset(ap, val), (), writes)

        def DMA(eng, out, in_, sem, reads=(), writes=(), extra=()):
            return P.dma(eng, lambda e: e.dma_start(out=out, in_=in_), sem, reads, writes, extra)

        def DMAN(eng, pairs, sem, reads=(), writes=(), extra=()):
            pairs = list(pairs)
            return P.dma(eng, lambda e: [e.dma_start(out=a, in_=b) for a, b in pairs], sem, reads, writes, extra,
                         n=len(pairs))

        def MM(out_ap, pairs, reads, writes, first=True, last=True):
            pairs = list(pairs)

            def fn(e):
                ins = None
                n = len(pairs)
                for i, (a, b) in enumerate(pairs):
                    ins = e.matmul(out_ap, a, b, start=(first and i == 0), stop=(last and i == n - 1))
                return ins
            return P.add("pe", fn, reads, writes)

        def TRN(pairs, reads, writes):
            pairs = list(pairs)

            def fn(e):
                ins = None
                for o_, i_ in pairs:
                    ins = e.transpose(o_, i_, ident.t[:])
                return ins
            return P.add("pe", fn, list(reads) + [ident], writes)

        gall = sb(top, "gall", [128, NG], F32, dma=True)
        ones_bf = sb(top, "ones_bf", [128, 128], BF16)
        ones_f = sb(top, "ones_f", [128, 128], F32)
        bd64 = sb(top, "bd64", [128, 128], BF16)
        ident = sb(top, "ident", [128, 128], F32)
        nlam = sb(top, "nlam", [128, 2], F32)
        subg = sb(top, "subg", [128, 2], F32)

        epsb = sb(top, "epsb", [128, 1], F32)
        MSET(epsb[:], EPS, [epsb])
        DMA("sp", gall[:], gall_in, gall.dsem, writes=[gall])
        MSET(ones_bf[:], 1.0, [ones_bf])
        MSET(ones_f[:], 1.0, [ones_f])
        MSET(bd64[:], 0.0, [bd64])
        MSET(bd64[0:64, 0:64], 1.0, [bd64])
        MSET(bd64[64:128, 64:128], 1.0, [bd64])
        P.add("pool", lambda e: e.iota(ident[:], pattern=[[1, 128]], base=0, channel_multiplier=-1,
                                       allow_small_or_imprecise_dtypes=True), writes=[ident])
        TSS("dve", ident[:], ident[:], 0.0, ALU.is_equal, [ident], [ident])

        NWS = 8
        wsem = [P.new_dsem() for _ in range(NWS)]
        wi = 0
        order = ["f1g", "f1u", "f1d", "abin", "wuq", "wukv", "about", "cin", "cout", "f2g", "f2u", "f2d",
                 "pleg", "plep"]
        conv_pending = []
        conv_ops = {}
        for l in range(4):
            for k in order:
                L, K, N = wshape[k]
                if l >= L:
                    continue
                kp = min(K, 128)
                src = wsrc[k][l].rearrange("(k p) n -> p k n", p=kp)
                dst = wt[k][l].rearrange("p (k n) -> p k n", n=N)
                nk = K // kp
                step = max(1, nk // 4) if nk >= 8 else nk
                conv_ops[(k, l)] = []
                for k0 in range(0, nk, step):
                    k1 = min(nk, k0 + step)
                    conv_pending.append(((k, l), dst[:, k0:k1, :], src[:, k0:k1, :]))
        conv_state = {"i": 0}

        def issue_conv(n_):
            while n_ > 0 and conv_pending:
                key_, d_, s_ = conv_pending.pop(0)
                op_ = DMA("pool", d_, s_, wsem[conv_state["i"] % NWS])
                conv_state["i"] += 1
                conv_ops[key_].append(op_)
                n_ -= 1

        n_first = 0
        for key_, _, _ in conv_pending:
            if key_[1] == 0 and key_[0] in ("f1g", "f1u", "f1d", "abin", "wuq", "wukv"):
                n_first += 1
            else:
                break
        issue_conv(n_first)

        with ExitStack() as st0:
            lv = sb(st0, "lv", [1, 512], F32, dma=True)
            lp = sb(st0, "lp", [1, 256], F32)
            ls = sb(st0, "ls", [1, 4], F32)
            le = sb(st0, "le", [1, 4], F32)
            ln2 = sb(st0, "ln2", [1, 2], F32)
            pst = ps(st0, "ps_pro")
            DMA("sp", lv[:], lamv_in, lv.dsem, writes=[lv])
            lvv = lv.t[:].rearrange("p (o f d) -> p o f d", o=2, f=4)
            lpv = lp.t[:].rearrange("p (o f d) -> p o f d", o=2, f=2)
            for o in range(2):
                for f in range(2):
                    TTo("dve", lpv[:, o, f, :], lvv[:, o, 2 * f, :], lvv[:, o, 2 * f + 1, :], ALU.mult, [lv], [lp])
            lp3 = lp.t[:].rearrange("p (g d) -> p g d", d=64)
            P.add("dve", lambda e: e.reduce_sum(ls[:, 0:4], lp3, axis=mybir.AxisListType.X), reads=[lp], writes=[ls])
            ACT(le[:], ls[:], AF.Exp, [ls], [le])
            lev = le.t[:].rearrange("p (o f) -> p o f", f=2)
            for o in range(2):
                li = 0.8 - 0.6 * math.exp(-0.3 * (2 * o + 1))
                STT("dve", ln2[:, o:o + 1], lev[:, o, 1:2], -li, lev[:, o, 0:1], ALU.add, ALU.subtract, [le], [ln2])
            MM(pst.t[:, 0:2], [(ones_f[0:1, :], ln2[0:1, :])], [ln2, ones_f], [pst])
            CP("dve", nlam[:], pst.t[:, 0:2], [pst], [nlam])
            for o in range(2):
                li = 0.8 - 0.6 * math.exp(-0.3 * (2 * o + 1))
                c0 = GCOL[("sub", o)]
                TSS("dve", subg[:, o:o + 1], gall[:, c0:c0 + 1], 1.0 - li, ALU.mult, [gall], [subg])
            P.barrier(bsem, bar_a, bar_b, skip=wsem)

        def rowlocal_pass(ipass):
            with ExitStack() as st:
                RING = 5
                ring = [sb(st, "ring%d" % i, [128, 5632], BF16, dma=True) for i in range(RING)]
                xbs = [sb(st, "xb%d" % i, [128, 8, TT], F32, dma=True) for i in range(2)]
                xb = xbs[0]
                hn = sb(st, "hn", [128, 8, TT], BF16)
                act = sb(st, "act", [128, 22, TT], BF16)
                sq = sb(st, "sq", [128, 8, TT], BF16)
                rstd = sb(st, "rstd", [128, TT], F32)
                tmpf = [sb(st, "tmpf%d" % i, [128, TT], F32) for i in range(6)]
                tab = [sb(st, "tab%d" % i, [128, TT], F32, dma=True) for i in range(4)]
                otb = sb(st, "otb", [128, 8, TT], BF16, dma=True)
                xin = [sb(st, "xin%d" % i, [128, D], F32, dma=True) for i in range(2)]
                ptk = sb(st, "ptk", [128, 4, 256], F32, dma=True)
                pT = sb(st, "pT", [128, 2, TT], BF16)
                NSTG = 8
                stg = [sb(st, "stg%d" % i, [128, TT], BF16, dma=True) for i in range(NSTG)]
                vstg = [sb(st, "vstg%d" % i, [128, 520], BF16, dma=True) for i in range(2)]
                cqn_b = [sb(st, "cqn%d" % i, [128, TT], BF16) for i in range(2)]
                ckvn_b = sb(st, "ckvn", [128, TT], BF16)
                pG = [ps(st, "pG%d" % i) for i in range(2)]
                pU = [ps(st, "pU%d" % i) for i in range(2)]
                pD = [ps(st, "pD%d" % i) for i in range(2)]
                pN = ps(st, "pN")
                pTp = ps(st, "pTp")
                cnt = {"ring": 0, "stg": 0, "vstg": 0, "G": 0, "D": 0, "tmp": 0, "xin": 0}

                for v in vstg:
                    MSET(v[:], 1.0, [v])

                def slab(wname, l, K, N, c0, ncols):
                    b = ring[cnt["ring"] % RING]
                    cnt["ring"] += 1
                    kp = min(K, 128)
                    nk = K // kp
                    src = wt[wname][l].rearrange("p (k n) -> p k n", n=N)[:, :, c0:c0 + ncols]
                    view = b.t[0:kp, 0:nk * ncols].rearrange("p (k n) -> p k n", n=ncols)
                    assert conv_ops[(wname, l)], (wname, l)
                    DMA("sp", view, src, b.dsem, writes=[b], extra=conv_ops[(wname, l)])
                    return b, view

                def next_tmp():
                    t = tmpf[cnt["tmp"] % len(tmpf)]
                    cnt["tmp"] += 1
                    return t

                def next_stg():
                    s_ = stg[cnt["stg"] % NSTG]
                    cnt["stg"] += 1
                    return s_

                def next_D():
                    d_ = pD[cnt["D"] % 2]
                    cnt["D"] += 1
                    return d_

                def rstd_from(ps_buf, inv_n):
                    RSTD(rstd, ps_buf, inv_n)

                def rmsnorm_stats(src_aps, src_bufs, inv_n):
                    n = len(src_aps)
                    for c, (ap, bb) in enumerate(zip(src_aps, src_bufs)):
                        if c % 2 == 0:
                            ACT(sq[:, c, :], ap, AF.Square, [bb], [sq])
                        else:
                            TTo("dve", sq[:, c, :], ap, ap, ALU.mult, [bb], [sq])
                    MM(pN[:], [(ones_bf[:], sq[:, c, :]) for c in range(n)], [sq, ones_bf], [pN])
                    rstd_from(pN, inv_n)

                def norm_x(gcol):
                    rmsnorm_stats([xb[:, c, :] for c in range(8)], [xb] * 8, 1.0 / D)
                    for c in range(8):
                        STT("dve", hn[:, c, :], xb[:, c, :], gall[:, gcol + c:gcol + c + 1], rstd[:], ALU.mult, ALU.mult,
                            [xb, rstd, gall], [hn])

                def ffn(wg, wu, wd, l, gcol):
                    norm_x(gcol)
                    for s0 in range(0, DFF, 512):
                        ncols = min(512, DFF - s0)
                        bg, vg = slab(wg, l, D, DFF, s0, ncols)
                        bu, vu = slab(wu, l, D, DFF, s0, ncols)
                        for j in range(ncols // 128):
                            f = (s0 // 128) + j
                            G = pG[cnt["G"] % 2]
                            U = pU[cnt["G"] % 2]
                            cnt["G"] += 1
                            MM(G[:], [(vg[:, k, j * 128:(j + 1) * 128], hn[:, k, :]) for k in range(8)], [bg, hn], [G])
                            MM(U[:], [(vu[:, k, j * 128:(j + 1) * 128], hn[:, k, :]) for k in range(8)], [bu, hn], [U])
                            t = next_tmp()
                            ACT(t[:], G[:], AF.Silu, [G], [t])
                            TTo("dve", act[:, f, :], t[:], U[:], ALU.mult, [t, U], [act])
                    for m0 in range(0, D, 256):
                        bd_, vd = slab(wd, l, DFF, D, m0, 256)
                        for j in range(2):
                            m = m0 // 128 + j
                            Dp = next_D()
                            MM(Dp[:], [(vd[:, k, j * 128:(j + 1) * 128], act[:, k, :]) for k in range(22)], [bd_, act], [Dp])
                            STT("dve", xb[:, m, :], Dp[:], 0.5, xb[:, m, :], ALU.mult, ALU.add, [Dp, xb], [xb])

                def proj(wname, l, K, N, c0, ncols, rhs_aps, rhs_bufs, consume):
                    nk = len(rhs_aps)
                    done = 0
                    while done < ncols:
                        sc = min(512, ncols - done)
                        b, v = slab(wname, l, K, N, c0 + done, sc)
                        for j in range((sc + 127) // 128):
                            mcols = min(128, sc - j * 128)
                            Dp = next_D()
                            MM(Dp.t[0:mcols, :], [(v[:, k, j * 128:j * 128 + mcols], rhs_aps[k]) for k in range(nk)],
                               [b] + list(rhs_bufs), [Dp])
                            consume((done // 128) + j, Dp, mcols)
                        done += sc

                def proj_tok(wname, l, K, N, c0, ncols, lhs_fn, lhs_bufs, nk, consume):
                    b, v = slab(wname, l, K, N, c0, ncols)
                    for s in range(4):
                        Dp = next_D()
                        MM(Dp.t[:, 0:ncols], [(lhs_fn(k, s), v[:, k, :]) for k in range(nk)], [b] + list(lhs_bufs), [Dp])
                        consume(s, Dp)

                def evac_to(dst_buf, rows=128):
                    def cons(j, Dp, mcols):
                        ACT(dst_buf.t[0:rows, :], Dp.t[0:rows, :], AF.Identity, [Dp], [dst_buf])
                    return cons

                hn_aps = [hn[:, k, :] for k in range(8)]

                for ti in range(DBG.get("ntile", NTILE)):
                    t0 = ti * TT
                    tsl = slice(t0, t0 + TT)
                    xb = xbs[ti % 2]
                    if ipass == 0:
                        for s in range(4):
                            xi = xin[cnt["xin"] % 2]
                            cnt["xin"] += 1
                            DMA("sp", xi[:], x_in[t0 + s * 128:t0 + (s + 1) * 128, :], xi.dsem, writes=[xi])
                            for c0 in range(0, 8, 4):
                                TRN([(pTp.t[:, cc * 128:(cc + 1) * 128], xi.t[:, (c0 + cc) * 128:(c0 + cc + 1) * 128])
                                     for cc in range(4)], [xi], [pTp])
                                CP("dve", xb.t[:, c0:c0 + 4, s * 128:(s + 1) * 128],
                                   pTp.t[:].rearrange("p (c t) -> p c t", t=128), [pTp], [xb])
                    else:
                        DMA("sp", xb[:], xs[:, :, tsl].rearrange("c p t -> p c t"), xb.dsem, writes=[xb])

                    if ipass > 0:
                        lp_ = ipass - 1
                        DMA("sp", otb[:], OT[:, tsl].rearrange("(c p) t -> p c t", p=128), otb.dsem, writes=[otb])
                        wo = "about" if lp_ % 2 == 0 else "cout"

                        def cons_out(m, Dp, mcols):
                            TTo("dve", xb[:, m, :], Dp[:], xb[:, m, :], ALU.add, [Dp, xb], [xb])
                        proj(wo, lp_ // 2, D, D, 0, D, [otb[:, k, :] for k in range(8)], [otb], cons_out)
                        ffn("f2g", "f2u", "f2d", lp_, GCOL[("ffn2", lp_)])
                        DMA("sp", ptk[:], p_in[lp_, t0:t0 + TT, :].rearrange("(s p) d -> p s d", p=128), ptk.dsem,
                            writes=[ptk])
                        for dc in range(2):
                            TRN([(pTp.t[:, s * 128:(s + 1) * 128], ptk.t[:, s, dc * 128:(dc + 1) * 128]) for s in range(4)],
                                [ptk], [pTp])
                            CP("dve", pT[:, dc, :], pTp[:], [pTp], [pT])
                        norm_x(GCOL[("ple", lp_)])
                        for m0 in range(0, D, 512):
                            bg_, vg_ = slab("pleg", lp_, D, D, m0, 512)
                            bp_, vp_ = slab("plep", lp_, 256, D, m0, 512)
                            for j in range(4):
                                m = m0 // 128 + j
                                G = pG[cnt["G"] % 2]
                                U = pU[cnt["G"] % 2]
                                cnt["G"] += 1
                                MM(G[:], [(vg_[:, k, j * 128:(j + 1) * 128], hn[:, k, :]) for k in range(8)], [bg_, hn], [G])
                                MM(U[:], [(vp_[:, k, j * 128:(j + 1) * 128], pT[:, k, :]) for k in range(2)], [bp_, pT], [U])
                                t = next_tmp()
                                ACT(t[:], G[:], AF.Sigmoid, [G], [t])
                                t2 = next_tmp()
                                TTo("dve", t2[:], t[:], U[:], ALU.mult, [t, U], [t2])
                                TTo("dve", xb[:, m, :], t2[:], xb[:, m, :], ALU.add, [t2, xb], [xb])

                    if ipass < DEPTH:
                        li = ipass
                        ffn("f1g", "f1u", "f1d", li, GCOL[("ffn1", li)])
                        norm_x(GCOL[("mix", li)])
                        if li % 2 == 0:
                            e_ = li // 2
                            for i_, (src_, row) in enumerate(((ropeA_in, 0), (ropeA_in, 1), (ropeB_in, 0), (ropeB_in, 1))):
                                DMA("sp", tab[i_][:], src_[row, :, tsl], tab[i_].dsem, writes=[tab[i_]])
                            for (nm, nch, gq, dst) in (("q", 4, "aq", QA), ("k", 1, "ak", KA)):
                                for c in range(nch):
                                    z = next_tmp()
                                    zw = next_tmp()
                                    proj("abin", e_, D, ABX_N, ABX[nm] + c * 128, 128, hn_aps, [hn], evac_to(z))
                                    proj("abin", e_, D, ABX_N, ABX[nm + "_sw"] + c * 128, 128, hn_aps, [hn], evac_to(zw))
                                    ACT(sq[:, 0, :], z[:], AF.Square, [z], [sq])
                                    MM(pN[:], [(bd64[:], sq[:, 0, :])], [sq, bd64], [pN])
                                    rstd_from(pN, 1.0 / 64)
                                    g0 = GCOL[(gq, e_)]
                                    g1 = GCOL[(gq + "_sw", e_)]
                                    STT("dve", z[:], z[:], gall[:, g0:g0 + 1], tab[0][:], ALU.mult, ALU.mult,
                                        [z, gall, tab[0]], [z])
                                    STT("dve", zw[:], zw[:], gall[:, g1:g1 + 1], tab[1][:], ALU.mult, ALU.mult,
                                        [zw, gall, tab[1]], [zw])
                                    TTo("dve", z[:], z[:], zw[:], ALU.add, [z, zw], [z])
                                    s_ = next_stg()
                                    TTo("dve", s_[:], z[:], rstd[:], ALU.mult, [z, rstd], [s_])
                                    DMA("pool", dst[c * 128:(c + 1) * 128, tsl], s_[:], s_.dsem, reads=[s_])
                            vs = vstg[cnt["vstg"] % 2]
                            cnt["vstg"] += 1

                            def consVA(s, Dp, vs=vs, t0=t0):
                                CP("dve", vs.t[:, 0:130].rearrange("p (g d) -> p g d", d=65)[:, :, 0:64],
                                   Dp.t[:, 0:128].rearrange("p (g d) -> p g d", d=64), [Dp], [vs])
                                DMA("pool", VA[t0 + s * 128:t0 + (s + 1) * 128, :], vs.t[:, 0:130], vs.dsem, reads=[vs])
                            proj_tok("abin", e_, D, ABX_N, ABX["v"], 128,
                                     lambda k, s: hn[:, k, s * 128:(s + 1) * 128], [hn], 8, consVA)
                            cq = [next_tmp(), next_tmp()]
                            for c in range(2):
                                proj("abin", e_, D, ABX_N, ABX["cq"] + c * 128, 128, hn_aps, [hn], evac_to(cq[c]))
                            rmsnorm_stats([cq[0][:], cq[1][:]], cq, 1.0 / 256)
                            cqn = cqn_b
                            for c in range(2):
                                gq_ = GCOL[("bq", e_)] + c
                                STT("dve", cqn[c][:], cq[c][:], gall[:, gq_:gq_ + 1], rstd[:], ALU.mult, ALU.mult,
                                    [cq[c], gall, rstd], [cqn[c]])
                            ckv = next_tmp()
                            proj("abin", e_, D, ABX_N, ABX["ckv"], 128, hn_aps, [hn], evac_to(ckv))
                            rmsnorm_stats([ckv[:]], [ckv], 1.0 / 128)
                            ckvn = ckvn_b
                            gk_ = GCOL[("bkv", e_)]
                            STT("dve", ckvn[:], ckv[:], gall[:, gk_:gk_ + 1], rstd[:], ALU.mult, ALU.mult,
                                [ckv, gall, rstd], [ckvn])
                            kr = next_tmp()
                            krw = next_tmp()
                            proj("abin", e_, D, ABX_N, ABX["kr"], 32, hn_aps, [hn], evac_to(kr, 32))
                            proj("abin", e_, D, ABX_N, ABX["kr_sw"], 32, hn_aps, [hn], evac_to(krw, 32))
                            TTo("dve", kr[0:32, :], kr[0:32, :], tab[2][0:32, :], ALU.mult, [kr, tab[2]], [kr])
                            TTo("dve", krw[0:32, :], krw[0:32, :], tab[3][0:32, :], ALU.mult, [krw, tab[3]], [krw])
                            s_ = next_stg()
                            TTo("dve", s_[0:32, :], kr[0:32, :], krw[0:32, :], ALU.add, [kr, krw], [s_])
                            DMA("pool", KRB[:, tsl], s_[0:32, :], s_.dsem, reads=[s_])

                            def cons_qn(j, Dp, mcols, tsl=tsl):
                                s_ = next_stg()
                                CP("dve", s_[:], Dp[:], [Dp], [s_])
                                DMAN("pool", [(QB[2 * j + hh_, 0:64, tsl], s_.t[hh_ * 64:(hh_ + 1) * 64, :]) for hh_ in range(2)],
                                     s_.dsem, reads=[s_])
                            proj("wuq", e_, 256, 1024, 0, 512, [cqn[0][:], cqn[1][:]], cqn, cons_qn)
                            for c in range(2):
                                z = next_tmp()
                                zw = next_tmp()
                                proj("wuq", e_, 256, 1024, 512 + c * 128, 128, [cqn[0][:], cqn[1][:]], cqn, evac_to(z))
                                proj("wuq", e_, 256, 1024, 768 + c * 128, 128, [cqn[0][:], cqn[1][:]], cqn, evac_to(zw))
                                TTo("dve", z[:], z[:], tab[2][:], ALU.mult, [z, tab[2]], [z])
                                TTo("dve", zw[:], zw[:], tab[3][:], ALU.mult, [zw, tab[3]], [zw])
                                s_ = next_stg()
                                TTo("dve", s_[:], z[:], zw[:], ALU.add, [z, zw], [s_])
                                DMAN("pool", [(QB[4 * c + hh_, 64:96, tsl], s_.t[hh_ * 32:(hh_ + 1) * 32, :]) for hh_ in range(4)],
                                     s_.dsem, reads=[s_])

                            def cons_kn(j, Dp, mcols, tsl=tsl):
                                s_ = next_stg()
                                CP("dve", s_[:], Dp[:], [Dp], [s_])
                                DMAN("pool", [(KB[2 * j + hh_, :, tsl], s_.t[hh_ * 64:(hh_ + 1) * 64, :]) for hh_ in range(2)],
                                     s_.dsem, reads=[s_])
                            proj("wukv", e_, 128, 1024, 0, 512, [ckvn[:]], [ckvn], cons_kn)
                            vs = vstg[cnt["vstg"] % 2]
                            cnt["vstg"] += 1

                            def consVB(s, Dp, vs=vs, t0=t0):
                                CP("dve", vs.t[:, 0:520].rearrange("p (g d) -> p g d", d=65)[:, :, 0:64],
                                   Dp.t[:, 0:512].rearrange("p (g d) -> p g d", d=64), [Dp], [vs])
                                DMA("pool", VB[t0 + s * 128:t0 + (s + 1) * 128, :], vs.t[:, 0:520], vs.dsem, reads=[vs])
                            proj_tok("wukv", e_, 128, 1024, 512, 512,
                                     lambda k, s: ckvn[:, s * 128:(s + 1) * 128], [ckvn], 1, consVB)
                        else:
                            o_ = li // 2
                            for which, dst in ((0, QC), (1024, KC)):
                                def cons_qk(j, Dp, mcols, dst=dst, tsl=tsl):
                                    s_ = next_stg()
                                    CP("dve", s_[:], Dp[:], [Dp], [s_])
                                    DMA("pool", dst[j * 128:(j + 1) * 128, tsl], s_[:], s_.dsem, reads=[s_])
                                proj("cin", o_, D, 3072, which, 1024, hn_aps, [hn], cons_qk)
                            for half in range(2):
                                def consVC(s, Dp, half=half, t0=t0):
                                    s_ = next_stg()
                                    CP("dve", s_[:], Dp[:], [Dp], [s_])
                                    DMA("pool", VC[t0 + s * 128:t0 + (s + 1) * 128, half * 512:(half + 1) * 512], s_[:],
                                        s_.dsem, reads=[s_])
                                proj_tok("cin", o_, D, 3072, 2048 + half * 512, 512,
                                         lambda k, s: hn[:, k, s * 128:(s + 1) * 128], [hn], 8, consVC)
                        DMA("pool", xs[:, :, tsl].rearrange("c p t -> p c t"), xb[:], xb.dsem, reads=[xb])
                    else:
                        rmsnorm_stats([xb[:, c, :] for c in range(8)], [xb] * 8, 1.0 / D)
                        gcol = GCOL["final"]
                        for c in range(8):
                            STT("dve", xb[:, c, :], xb[:, c, :], gall[:, gcol + c:gcol + c + 1], rstd[:], ALU.mult, ALU.mult,
                                [xb, rstd, gall], [xb])
                        for s in range(4):
                            xi = xin[cnt["xin"] % 2]
                            cnt["xin"] += 1
                            for c0 in range(0, 8, 4):
                                TRN([(pTp.t[:, cc * 128:(cc + 1) * 128], xb.t[:, c0 + cc, s * 128:(s + 1) * 128])
                                     for cc in range(4)], [xb], [pTp])
                                CP("dve", xi.t[:, c0 * 128:(c0 + 4) * 128], pTp[:], [pTp], [xi])
                            DMA("pool", y_out[t0 + s * 128:t0 + (s + 1) * 128, :], xi[:], xi.dsem, reads=[xi])
                P.barrier(bsem, bar_a, bar_b)
                P.release_dsems([b.dsem for b in ring + xbs + [otb, ptk] + tab + xin + stg + vstg])

        def ps2(stack, name):
            uid["n"] += 1
            return Buf(stack.enter_context(nc.psum_tensor("p%d_%s" % (uid["n"], name), [128, 1024], F32)))

        def MMS(specs, reads, writes):
            specs = list(specs)

            def fn(e):
                ins = None
                for (o_, a_, b_, s0_, s1_) in specs:
                    ins = e.matmul(o_, a_, b_, start=s0_, stop=s1_)
                return ins
            return P.add("pe", fn, reads, writes)

        def attn_even(e_):
            with ExitStack() as st:
                S2 = [ps2(st, "S2_%d" % i) for i in range(2)]
                O = [ps(st, "O%d" % i) for i in range(2)]
                Bp = ps(st, "Bp")
                Kt = [sb(st, "Kt%d" % i, [128, NT], BF16, dma=True) for i in range(2)]
                Qt = [sb(st, "Qt%d" % i, [128, NT], BF16, dma=True) for i in range(2)]
                Vt = sb(st, "Vt", [128, 64, 584], BF16, dma=True)
                maskb = sb(st, "maskb", [128, 1024], F32, dma=True)
                NP_ = 3
                Pt = [sb(st, "Pt%d" % i, [128, 2 * TT], BF16) for i in range(NP_)]
                rl = [sb(st, "rl%d" % i, [128, TT], F32) for i in range(2)]
                osb = [sb(st, "osb%d" % i, [128, TT], F32) for i in range(2)]
                ostg = [sb(st, "ostg%d" % i, [128, TT], BF16, dma=True) for i in range(4)]
                DMA("sp", maskb[:], maskb_in, maskb.dsem, writes=[maskb])
                for b_ in Kt + Qt + [Vt]:
                    MSET(b_[:], 0.0, [b_])
                qcount = 0
                ecount = 0
                for mixer in ("A", "B"):
                    if mixer == "A":
                        Kd, scale, vw, Vsrc = 128, 64 ** -0.5, 130, VA
                    else:
                        Kd, scale, vw, Vsrc = 96, 96 ** -0.5, 520, VB
                    vsrc = Vsrc.rearrange("(t p) c -> p t c", p=128)
                    DMAN("sp", [(Vt.t[:, 16 * i:16 * (i + 1), 0:vw], vsrc[:, 16 * i:16 * (i + 1), :]) for i in range(4)],
                         Vt.dsem, writes=[Vt])
                    qbase = qcount
                    qcount += 8

                    def Kbuf(h):
                        return Kt[(h // 4) % 2] if mixer == "A" else Kt[h % 2]

                    def Qbuf(h):
                        return Qt[(qbase + h) % 2]

                    def load_head(h):
                        q_ = Qbuf(h)
                        k_ = Kbuf(h)
                        if mixer == "A":
                            g = h // 4
                            if h % 4 == 0:
                                DMA("sp", k_.t[0:64, :], KA[g * 64:(g + 1) * 64, :], k_.dsem, writes=[k_])
                            DMA("sp", q_.t[0:64, :], QA[h * 64:(h + 1) * 64, :], q_.dsem, writes=[q_])
                        else:
                            DMAN("sp", [(k_.t[0:64, :], KB[h, :, :]), (k_.t[64:96, :], KRB[:, :])], k_.dsem, writes=[k_])
                            DMA("sp", q_.t[0:96, :], QB[h, :, :], q_.dsem, writes=[q_])

                    LA = 2
                    n = 8 * 8 * 64
                    load_head(0)
                    for i in range(n + LA):
                        if i % 512 == LA and (i // 512) + 1 < 8:
                            load_head(i // 512 + 1)
                        if i < n:
                            h, qp, t = i // 512, (i % 512) // 64, i % 64
                            K_, Q_ = Kbuf(h), Qbuf(h)
                            Sb = S2[i % 2]
                            kt_ = K_.t[0:Kd, t * 128:(t + 1) * 128]
                            MMS([(Sb.t[:, 0:TT], kt_, Q_.t[0:Kd, (2 * qp) * TT:(2 * qp + 1) * TT], True, True),
                                 (Sb.t[:, TT:2 * TT], kt_, Q_.t[0:Kd, (2 * qp + 1) * TT:(2 * qp + 2) * TT], True, True)],
                                [K_, Q_], [Sb])
                            pt = Pt[i % NP_]
                            blk = (2 * qp) * 64 + t
                            ACT(pt[:], Sb[:], AF.Exp, [Sb, maskb], [pt], bias=maskb[:, blk:blk + 1], scale=scale)
                        j = i - LA
                        if j >= 0:
                            h, qp, t = j // 512, (j % 512) // 64, j % 64
                            vcol = (h // 4) * 65 if mixer == "A" else h * 65
                            orow = (0 if mixer == "A" else 512) + h * 64
                            pt = Pt[j % NP_]
                            vt_ = Vt.t[:, t, vcol:vcol + 128]
                            MMS([(O[0].t[:, :], vt_, pt.t[:, 0:TT], t == 0, t == 63),
                                 (O[1].t[:, :], vt_, pt.t[:, TT:2 * TT], t == 0, t == 63)], [pt, Vt], [O[0], O[1]])
                            if t == 63:
                                issue_conv(2)
                                for k2 in range(2):
                                    CP("dve", osb[k2][0:65, :], O[k2].t[0:65, :], [O[k2]], [osb[k2]])
                                for k2 in range(2):
                                    RCP(rl[k2][64:65, :], osb[k2][64:65, :], [osb[k2]], [rl[k2]])
                                for k2 in range(2):
                                    MM(Bp.t[0:64, :], [(ones_f[64:65, 0:64], rl[k2][64:65, :])], [rl[k2], ones_f], [Bp])
                                    og = ostg[ecount % 4]
                                    ecount += 1
                                    TTo("dve", og[0:64, :], osb[k2][0:64, :], Bp.t[0:64, :], ALU.mult, [osb[k2], Bp], [og])
                                    qb = 2 * qp + k2
                                    DMA("pool", OT[orow:orow + 64, qb * TT:(qb + 1) * TT], og[0:64, :], og.dsem, reads=[og])
                issue_conv(len(conv_pending))
                P.barrier(bsem, bar_a, bar_b)
                P.release_dsems([b.dsem for b in Kt + Qt + [Vt, maskb] + ostg])

        ALIBI_THR = 40.0

        def attn_odd(o_):
            with ExitStack() as st:
                S = [ps(st, "S%d" % i) for i in range(4)]
                O1 = ps(st, "O1")
                O2 = ps(st, "O2")
                L1 = ps(st, "L1")
                L2 = ps(st, "L2")
                Kt = [sb(st, "Kt%d" % i, [128, NT], BF16, dma=True) for i in range(2)]
                Qt = [sb(st, "Qt%d" % i, [128, NT], BF16, dma=True) for i in range(2)]
                Vt = [sb(st, "Vt%d" % i, [128, 64, 128], BF16, dma=True) for i in range(2)]
                maskb = sb(st, "maskb", [128, 1024], F32, dma=True)
                cdist = sb(st, "cdist", [128, 1024], F32, dma=True)
                dtl = sb(st, "dtl", [128, 6, 512], F32, dma=True)
                biasC = [sb(st, "biasC%d" % i, [128, 1024], F32) for i in range(2)]
                NP_ = 3
                Pt = [sb(st, "Pt%d" % i, [128, 2 * TT], BF16) for i in range(NP_)]
                tm = [sb(st, "tm%d" % i, [128, 2 * TT], F32) for i in range(NP_)]
                r1 = sb(st, "r1", [128, TT], F32)
                r2 = sb(st, "r2", [128, TT], F32)
                o1 = sb(st, "o1", [128, TT], F32)
                o2 = sb(st, "o2", [128, TT], F32)
                sqo = sb(st, "sqo", [128, TT], BF16)
                rs = sb(st, "rs", [128, TT], F32)
                ostg = [sb(st, "ostg%d" % i, [128, TT], BF16, dma=True) for i in range(2)]
                DMA("sp", maskb[:], maskb_in, maskb.dsem, writes=[maskb])
                DMA("sp", cdist[:], cdist_in, cdist.dsem, writes=[cdist])
                DMA("sp", dtl[:], dtl_in, dtl.dsem, writes=[dtl])
                scale = 64 ** -0.5

                def load_head(h):
                    slope = 2.0 ** (-(h + 1))
                    K_, Q_, V_, bC = Kt[h % 2], Qt[h % 2], Vt[h % 2], biasC[h % 2]
                    DMA("sp", K_[:], KC[h * 128:(h + 1) * 128, :], K_.dsem, writes=[K_])
                    DMA("sp", Q_[:], QC[h * 128:(h + 1) * 128, :], Q_.dsem, writes=[Q_])
                    vsrc = VC[:, h * 128:(h + 1) * 128].rearrange("(t p) c -> p t c", p=128)
                    DMAN("sp", [(V_.t[:, 16 * i:16 * (i + 1), :], vsrc[:, 16 * i:16 * (i + 1), :]) for i in range(4)],
                         V_.dsem, writes=[V_])
                    STT("dve", bC[:], cdist[:], slope, maskb[:], ALU.mult, ALU.add, [cdist, maskb], [bC])

                items = []
                head_start = {}
                for h in range(8):
                    slope = 2.0 ** (-(h + 1))
                    head_start[len(items)] = h
                    for qb in range(16):
                        q_lo, q_hi = qb * TT, qb * TT + TT - 1
                        keep = []
                        for t in range(64):
                            s_lo, s_hi = t * 128, t * 128 + 127
                            mind = max(0, s_lo - q_hi, q_lo - s_hi)
                            if slope * mind < ALIBI_THR:
                                keep.append(t)
                        for t in keep:
                            items.append((h, qb, t, t == keep[0], t == keep[-1]))
                LA = 2
                n = len(items)
                load_head(0)
                ecount = 0
                for i in range(n + LA):
                    if (i - LA) in head_start and head_start[i - LA] + 1 < 8:
                        load_head(head_start[i - LA] + 1)
                    if i < n:
                        h, qb, t, fs, ls_ = items[i]
                        slope = 2.0 ** (-(h + 1))
                        K_, Q_, bC = Kt[h % 2], Qt[h % 2], biasC[h % 2]
                        Sa = S[(i % 2) * 2]
                        Sb = S[(i % 2) * 2 + 1]
                        MM(Sa[:], [(K_.t[0:64, t * 128:(t + 1) * 128], Q_.t[0:64, qb * TT:(qb + 1) * TT])], [K_, Q_], [Sa])
                        MM(Sb[:], [(K_.t[64:128, t * 128:(t + 1) * 128], Q_.t[64:128, qb * TT:(qb + 1) * TT])], [K_, Q_], [Sb])
                        if t < 4 * qb:
                            di = 0
                        elif t > 4 * qb + 3:
                            di = 1
                        else:
                            di = 2 + (t - 4 * qb)
                        tmb = tm[i % NP_]
                        fac = slope / scale
                        STT("dve", tmb[:, 0:TT], dtl[:, di, :], fac, Sa[:], ALU.mult, ALU.add, [Sa, dtl], [tmb])
                        STT("dve", tmb[:, TT:2 * TT], dtl[:, di, :], fac, Sb[:], ALU.mult, ALU.add, [Sb, dtl], [tmb])
                        pt = Pt[i % NP_]
                        blk = qb * 64 + t
                        ACT(pt[:], tmb[:], AF.Exp, [tmb, bC], [pt], bias=bC[:, blk:blk + 1], scale=scale)
                    j = i - LA
                    if j >= 0:
                        h, qb, t, fs, ls_ = items[j]
                        V_ = Vt[h % 2]
                        pt = Pt[j % NP_]
                        MMS([(O1.t[:, :], V_.t[:, t, :], pt.t[:, 0:TT], fs, ls_),
                             (L1.t[:, :], ones_bf.t[:, :], pt.t[:, 0:TT], fs, ls_),
                             (O2.t[:, :], V_.t[:, t, :], pt.t[:, TT:2 * TT], fs, ls_),
                             (L2.t[:, :], ones_bf.t[:, :], pt.t[:, TT:2 * TT], fs, ls_)],
                            [pt, V_, ones_bf], [O1, L1, O2, L2])
                        if ls_:
                            ACT(r1[:], L1[:], AF.Ln, [L1], [r1])
                            ACT(r2[:], L2[:], AF.Ln, [L2], [r2])
                            ACT(r1[:], r1[:], AF.Exp, [r1], [r1], scale=-1.0)
                            ACT(r2[:], r2[:], AF.Exp, [r2], [r2], scale=-1.0)
                            TTo("dve", o1[:], O1[:], r1[:], ALU.mult, [O1, r1], [o1])
                            TTo("dve", o2[:], O2[:], r2[:], ALU.mult, [O2, r2], [o2])
                            STT("dve", o1[:], o2[:], nlam[:, o_:o_ + 1], o1[:], ALU.mult, ALU.add, [o1, o2, nlam], [o1])
                            ACT(sqo[:], o1[:], AF.Square, [o1], [sqo])
                            MM(L1[:], [(ones_bf[:], sqo[:])], [sqo, ones_bf], [L1])
                            RSTD(rs, L1, 1.0 / 128)
                            og = ostg[ecount % 2]
                            ecount += 1
                            STT("dve", og[:], o1[:], subg[:, o_:o_ + 1], rs[:], ALU.mult, ALU.mult, [o1, subg, rs], [og])
                            DMA("pool", OT[h * 128:(h + 1) * 128, qb * TT:(qb + 1) * TT], og[:], og.dsem, reads=[og])
                P.barrier(bsem, bar_a, bar_b)
                P.release_dsems([b.dsem for b in Kt + Qt + Vt + [maskb, cdist, dtl] + ostg])

        for ipass in range(DBG["passes"]):
            rowlocal_pass(ipass)
            if ipass < DEPTH and DBG["attn"]:
                if ipass % 2 == 0:
                    attn_even(ipass // 2)
                else:
                    attn_odd(ipass // 2)
        P.add("sp", lambda e: e.nop(), extra=[P.last_barrier])
        P.emit(block)
    return nc


def _tables(seq_len):
    t = np.arange(NT)
    tl = t % seq_len
    row, col = tl // 64, tl % 64
    inv = (10000.0 ** (-np.arange(16, dtype=np.float32) * (2.0 / 32))).astype(np.float32)

    def cs(pos, d_idx):
        jj = d_idx % 32
        i = jj % 16
        ang = pos[None, :].astype(np.float32) * inv[i][:, None]
        c = np.cos(ang).astype(np.float32)
        s = np.sin(ang).astype(np.float32)
        s = np.where((jj < 16)[:, None], -s, s)
        return c, s
    d = np.arange(128)
    j = d % 64
    cA = np.zeros((128, NT), np.float32)
    sA = np.zeros((128, NT), np.float32)
    m_row = j < 32
    c1, s1 = cs(row, d)
    c2, s2 = cs(col, d)
    cA[m_row], sA[m_row] = c1[m_row], s1[m_row]
    cA[~m_row], sA[~m_row] = c2[~m_row], s2[~m_row]
    cB, sB = cs(tl, d)
    ropeA = np.stack([cA, sA]).astype(np.float32)
    ropeB = np.stack([cB, sB]).astype(np.float32)
    qb = np.arange(16)[:, None]
    tt = np.arange(64)[None, :]
    q0 = qb * 512
    s0 = tt * 128
    same = (q0 // seq_len) == (s0 // seq_len)
    maskb = np.where(same, 0.0, NEG).astype(np.float32).reshape(1, 1024)
    diag = (tt >= 4 * qb) & (tt <= 4 * qb + 3)
    cd = np.where(diag, 0.0, -np.abs(q0 - s0)).astype(np.float32).reshape(1, 1024)
    maskb = np.repeat(maskb, 128, 0)
    cd = np.repeat(cd, 128, 0)
    return ropeA, ropeB, maskb, cd


def _dtiles():
    p = np.arange(128)[:, None]
    j = np.arange(512)[None, :]
    tiles = [-(j - p), (j - p)] + [-np.abs(j - p - 128 * c) for c in range(4)]
    return np.ascontiguousarray(np.stack(tiles, 1).astype(np.float32))


_CACHE = {}


def kernel(**inp):
    f = lambda a: np.ascontiguousarray(np.asarray(a, dtype=np.float32))
    x_prompt, x_sample = f(inp["x_prompt"]), f(inp["x_sample"])
    p_prompt, p_sample = f(inp["p_prompt"]), f(inp["p_sample"])

    sw64 = np.concatenate([_swap32(64)])
    abin = f(inp["ab_w_in"])
    q_idx = np.arange(512)
    q_sw_idx = (q_idx // 64) * 64 + sw64[q_idx % 64]
    k_idx = 512 + np.arange(128)
    k_sw_idx = 512 + (np.arange(128) // 64) * 64 + sw64[np.arange(128) % 64]
    kr_idx = 1152 + np.arange(32)
    kr_sw_idx = 1152 + _swap32(32)
    cols = np.concatenate([q_idx, q_sw_idx, k_idx, k_sw_idx, 768 + np.arange(256), 1024 + np.arange(128),
                           kr_idx, kr_sw_idx, 640 + np.arange(128)])
    abin_x = np.ascontiguousarray(abin[:, :, cols])
    wuq = f(inp["b_w_uq"])
    hh = np.arange(8)[:, None]
    nope_idx = (hh * 96 + np.arange(64)[None, :]).reshape(-1)
    rope_idx = (hh * 96 + 64 + np.arange(32)[None, :]).reshape(-1)
    rope_sw_idx = (hh * 96 + 64 + _swap32(32)[None, :]).reshape(-1)
    wuq_x = np.ascontiguousarray(wuq[:, :, np.concatenate([nope_idx, rope_idx, rope_sw_idx])])
    wukv = f(inp["b_w_ukv"])
    kn_idx = (hh * 128 + np.arange(64)[None, :]).reshape(-1)
    v_idx = (hh * 128 + 64 + np.arange(64)[None, :]).reshape(-1)
    wukv_x = np.ascontiguousarray(wukv[:, :, np.concatenate([kn_idx, v_idx])])

    gall = np.zeros((128, NG), np.float32)

    def put(col, vec):
        v = np.asarray(vec, np.float32).reshape(-1, 128).T
        gall[:, col:col + v.shape[1]] = v
    for i in range(DEPTH):
        put(GCOL[("ffn1", i)], inp["ffn1_norm"][i])
        put(GCOL[("mix", i)], inp["mix_norm"][i])
        put(GCOL[("ffn2", i)], inp["ffn2_norm"][i])
        put(GCOL[("ple", i)], inp["ple_norm"][i])
    put(GCOL["final"], inp["final_norm"])
    for e in range(2):
        aq = np.asarray(inp["a_q_norm"][e], np.float32)
        ak = np.asarray(inp["a_k_norm"][e], np.float32)
        put(GCOL[("aq", e)], np.tile(aq, 2))
        put(GCOL[("aq_sw", e)], np.tile(aq[sw64], 2))
        put(GCOL[("ak", e)], np.tile(ak, 2))
        put(GCOL[("ak_sw", e)], np.tile(ak[sw64], 2))
        put(GCOL[("bq", e)], inp["b_q_norm"][e])
        put(GCOL[("bkv", e)], inp["b_kv_norm"][e])
    for o in range(2):
        put(GCOL[("sub", o)], inp["c_sub_norm"][o])
    lamv = np.stack([np.stack([f(inp["c_lambda_q1"])[o], f(inp["c_lambda_k1"])[o],
                               f(inp["c_lambda_q2"])[o], f(inp["c_lambda_k2"])[o]]) for o in range(2)])
    lamv = np.ascontiguousarray(lamv.reshape(1, 512))

    shared = {
        "f1g": f(inp["ffn1_wg"]), "f1u": f(inp["ffn1_wu"]), "f1d": f(inp["ffn1_wd"]),
        "f2g": f(inp["ffn2_wg"]), "f2u": f(inp["ffn2_wu"]), "f2d": f(inp["ffn2_wd"]),
        "abin": abin_x, "wuq": wuq_x, "wukv": wukv_x, "about": f(inp["ab_w_out"]),
        "cin": f(inp["c_w_in"]), "cout": f(inp["c_w_out"]),
        "pleg": f(inp["ple_w_gate"]), "plep": f(inp["ple_w_proj"]),
        "gall": gall, "lamv": lamv, "dtiles": _dtiles(),
    }
    tabs = {8192: _tables(8192), 2048: _tables(2048)}
    in_maps = []
    for core in range(8):
        u = core if core < N_UNITS else core - 2
        if u < 4:
            xu = x_prompt[u]
            pu = p_prompt[:, u]
            tb = tabs[8192]
        else:
            s = (u - 4) * 4
            xu = x_sample[s:s + 4].reshape(NT, D)
            pu = p_sample[:, s:s + 4].reshape(DEPTH, NT, 256)
            tb = tabs[2048]
        m = dict(shared)
        m["x"] = np.ascontiguousarray(xu)
        m["p"] = np.ascontiguousarray(pu)
        m["ropeA"], m["ropeB"], m["maskb"], m["cdist"] = tb
        in_maps.append(m)

    if "nc" not in _CACHE:
        _CACHE["nc"] = build_program()
    nc = _CACHE["nc"]
    res = run_bass_kernel_spmd(nc, in_maps, core_ids=list(range(8)))
    if DBG.get("dump"):
        _CACHE["res"] = res.results
    ys = [np.asarray(r["y"], dtype=np.float32) for r in res.results]
    y_prompt = np.stack(ys[0:4]).reshape(4, NT, D)
    y_sample = np.concatenate([ys[4].reshape(4, 2048, D), ys[5].reshape(4, 2048, D)], 0)
    return (y_prompt, y_sample)
```

```python
import math
from contextlib import ExitStack

import numpy as np
import concourse.bass as bass
import concourse.mybir as mybir
from concourse.bass_utils import run_bass_kernel_spmd

F32 = mybir.dt.float32
BF16 = mybir.dt.bfloat16
AF = mybir.ActivationFunctionType
ALU = mybir.AluOpType

DEPTH = 4
D = 1024
DFF = 2816
NT = 8192
TT = 512
NTILE = NT // TT
EPS = 1e-6
NEG = -30000.0
N_UNITS = 6

DBG = {"passes": 5, "attn": True}


class DSem:
    def __init__(self, handle, kind="hw"):
        self.handle = handle
        self.total = 0
        self.last = None
        self.kind = kind
        self.sw = None


class Op:
    __slots__ = ("eng", "fn", "deps", "is_dma", "sem", "val", "needs_inc", "n")

    def __init__(self, eng, fn, deps, is_dma=False, sem=None, n=1):
        self.eng = eng
        self.fn = fn
        self.deps = deps
        self.is_dma = is_dma
        self.sem = sem
        self.val = 0
        self.needs_inc = False
        self.n = n


class Buf:
    def __init__(self, t, dsem=None):
        self.t = t
        self.w = None
        self.r = []
        self.dsem = dsem

    def __getitem__(self, k):
        return self.t[k]


class Prog:
    ENGS = ("pe", "act", "dve", "pool", "sp")

    def __init__(self, nc, esems, dsem_handles):
        self.nc = nc
        self.ops = {e: [] for e in self.ENGS}
        self.esem = esems
        half = len(dsem_handles) // 2
        self.free = {"hw": [DSem(h, "hw") for h in dsem_handles[:half]],
                     "sw": [DSem(h, "sw") for h in dsem_handles[half:]]}
        self.all_dsems = self.free["hw"] + self.free["sw"]
        self.last_barrier = None

    def new_dsem(self, kind="hw"):
        return self.free[kind].pop()

    def release_dsems(self, ds):
        for d in ds:
            self.free[d.kind].append(d)
            if d.sw is not None:
                self.free["sw"].append(d.sw)
                d.sw = None

    def _deps(self, eng, reads, writes, extra):
        deps = []
        for b in reads:
            if b.w is not None:
                deps.append(b.w)
        strict = (eng == "pool")
        for b in writes:
            for r in b.r:
                if r.eng != eng or r.is_dma or strict:
                    deps.append(r)
            if b.w is not None and (b.w.eng != eng or b.w.is_dma or strict):
                deps.append(b.w)
        deps.extend(d for d in extra if d is not None)
        if self.last_barrier is not None:
            deps.append(self.last_barrier)
        return deps

    def add(self, eng, fn, reads=(), writes=(), extra=()):
        op = Op(eng, fn, self._deps(eng, reads, writes, extra))
        self.ops[eng].append(op)
        for b in reads:
            b.r.append(op)
        for b in writes:
            b.w = op
            b.r = []
        return op

    def dma(self, eng, fn, sem, reads=(), writes=(), extra=(), n=1):
        if eng == "pool" and sem.sw is not None:
            sem = sem.sw
        assert sem.kind == ("sw" if eng == "pool" else "hw"), (eng, sem.kind)
        deps = self._deps(eng, reads, writes, extra)
        if sem.last is not None:
            deps.append(sem.last)
        op = Op(eng, fn, deps, is_dma=True, sem=sem, n=n)
        sem.total += 16 * n
        op.val = sem.total
        sem.last = op
        self.ops[eng].append(op)
        for b in reads:
            b.r.append(op)
        for b in writes:
            b.w = op
            b.r = []
        return op

    def barrier(self, bsem, scratch_src, scratch_dst, skip=()):
        deps = []
        for e in self.ENGS:
            for op in reversed(self.ops[e]):
                if not op.is_dma:
                    deps.append(op)
                    break
        for ds in self.all_dsems:
            if ds.last is not None and ds not in skip:
                deps.append(ds.last)
        self.last_barrier = None
        op = self.dma("sp", lambda e: e.dma_start(out=scratch_dst, in_=scratch_src), bsem, extra=deps)
        self.last_barrier = op
        return op

    def emit(self, block):
        for e in self.ENGS:
            for op in self.ops[e]:
                for d in op.deps:
                    if not d.is_dma:
                        d.needs_inc = True
        for e in self.ENGS:
            c = 0
            for op in self.ops[e]:
                if not op.is_dma and op.needs_inc:
                    c += 1
                    op.val = c
        esem = self.esem

        def run(engname, eng):
            waited = {}
            for op in self.ops[engname]:
                need = {}
                for d in op.deps:
                    s = d.sem.handle if d.is_dma else esem[d.eng]
                    key = id(s)
                    if waited.get(key, 0) < d.val and need.get(key, (None, 0))[1] < d.val:
                        need[key] = (s, d.val)
                for key, (s, v) in need.items():
                    eng.wait_ge(s, v)
                    waited[key] = v
                ins = op.fn(eng)
                if op.is_dma:
                    if not isinstance(ins, (list, tuple)):
                        ins = [ins]
                    assert len(ins) == op.n, (len(ins), op.n)
                    for i_ in ins:
                        i_.then_inc(op.sem.handle, 16)
                elif op.needs_inc:
                    ins.then_inc(esem[engname], 1)

        block.tensor(lambda eng: run("pe", eng))
        block.scalar(lambda eng: run("act", eng))
        block.vector(lambda eng: run("dve", eng))
        block.gpsimd(lambda eng: run("pool", eng))
        block.sync(lambda eng: run("sp", eng))


def _gcols():
    cols = {}
    c = 0
    for i in range(DEPTH):
        for nm in ("ffn1", "mix", "ffn2", "ple"):
            cols[(nm, i)] = c
            c += 8
    cols["final"] = c
    c += 8
    for e in range(2):
        for nm, w in (("aq", 1), ("aq_sw", 1), ("ak", 1), ("ak_sw", 1), ("bq", 2), ("bkv", 1)):
            cols[(nm, e)] = c
            c += w
    for o in range(2):
        cols[("sub", o)] = c
        c += 1
    return cols, c


GCOL, NG = _gcols()

ABX = {"q": 0, "q_sw": 512, "k": 1024, "k_sw": 1152, "cq": 1280, "ckv": 1536, "kr": 1664, "kr_sw": 1696,
       "v": 1728}
ABX_N = 1856


def _swap32(n):
    idx = np.arange(n)
    j = idx % 32
    return np.where(j < 16, idx + 16, idx - 16)


def build_program():
    nc = bass.Bass("TRN2", target_bir_lowering=False)

    def dram(name, shape, dt, kind="Internal"):
        if name in DBG.get("dump", ()):
            kind = "ExternalOutput"
        return nc.dram_tensor(name, list(shape), dt, kind=kind).ap()

    x_in = dram("x", [NT, D], F32, "ExternalInput")
    p_in = dram("p", [DEPTH, NT, 256], F32, "ExternalInput")
    y_out = dram("y", [NT, D], F32, "ExternalOutput")
    wsrc = {}
    wshape = {
        "f1g": (4, D, DFF), "f1u": (4, D, DFF), "f1d": (4, DFF, D),
        "f2g": (4, D, DFF), "f2u": (4, D, DFF), "f2d": (4, DFF, D),
        "abin": (2, D, ABX_N), "wuq": (2, 256, 1024), "wukv": (2, 128, 1024), "about": (2, D, D),
        "cin": (2, D, 3072), "cout": (2, D, D), "pleg": (4, D, D), "plep": (4, 256, D),
    }
    for k, shp in wshape.items():
        wsrc[k] = dram(k, shp, F32, "ExternalInput")
    gall_in = dram("gall", [128, NG], F32, "ExternalInput")
    lamv_in = dram("lamv", [1, 2 * 4 * 64], F32, "ExternalInput")
    ropeA_in = dram("ropeA", [2, 128, NT], F32, "ExternalInput")
    ropeB_in = dram("ropeB", [2, 128, NT], F32, "ExternalInput")
    maskb_in = dram("maskb", [128, 1024], F32, "ExternalInput")
    cdist_in = dram("cdist", [128, 1024], F32, "ExternalInput")
    dtl_in = dram("dtiles", [128, 6, 512], F32, "ExternalInput")

    wt = {}
    for k, (L, K, N) in wshape.items():
        kp = min(K, 128)
        wt[k] = dram("wt_" + k, [L, kp, (K // kp) * N], BF16)
    xs = dram("xs", [8, 128, NT], F32)
    QA = dram("QA", [512, NT], BF16)
    KA = dram("KA", [128, NT], BF16)
    VA = dram("VA", [NT, 130], BF16)
    QB = dram("QB", [8, 96, NT], BF16)
    KB = dram("KB", [8, 64, NT], BF16)
    KRB = dram("KRB", [32, NT], BF16)
    VB = dram("VB", [NT, 520], BF16)
    QC = dram("QC", [1024, NT], BF16)
    KC = dram("KC", [1024, NT], BF16)
    VC = dram("VC", [NT, 1024], BF16)
    OT = dram("OT", [1024, NT], BF16)
    bar_a = dram("bar_a", [1, 64], F32)
    bar_b = dram("bar_b", [1, 64], F32)

    with ExitStack() as top:
        esems = {e: top.enter_context(nc.semaphore("es_" + e)) for e in Prog.ENGS}
        dhandles = [top.enter_context(nc.semaphore("ds%d" % i)) for i in range(60)]
        block = top.enter_context(nc.Block())
        P = Prog(nc, esems, dhandles)
        bsem = P.new_dsem()

        uid = {"n": 0}

        def sb(stack, name, shape, dt, dma=False):
            uid["n"] += 1
            t = stack.enter_context(nc.sbuf_tensor("s%d_%s" % (uid["n"], name), list(shape), dt))
            if not dma:
                return Buf(t, None)
            if dma == "sw":
                return Buf(t, P.new_dsem("sw"))
            d_ = P.new_dsem("hw")
            if dma == "both":
                d_.sw = P.new_dsem("sw")
            return Buf(t, d_)

        def ps(stack, name):
            uid["n"] += 1
            return Buf(stack.enter_context(nc.psum_tensor("p%d_%s" % (uid["n"], name), [128, 512], F32)))

        def ACT(out, in_, func, reads, writes, **kw):
            return P.add("act", lambda e: e.activation(out=out, in_=in_, func=func, **kw), reads, writes)

        def TTo(eng, out, a, b, op, reads, writes):
            return P.add(eng, lambda e: e.tensor_tensor(out, a, b, op), reads, writes)

        def STT(eng, out, in0, scalar, in1, op0, op1, reads, writes):
            return P.add(eng, lambda e: e.scalar_tensor_tensor(out, in0, scalar, in1, op0=op0, op1=op1), reads, writes)

        def TS(eng, out, in0, s1, s2, op0, op1, reads, writes):
            return P.add(eng, lambda e: e.tensor_scalar(out, in0, s1, s2, op0=op0, op1=op1), reads, writes)

        def TSS(eng, out, in0, s, op, reads, writes):
            return P.add(eng, lambda e: e.tensor_single_scalar(out, in0, s, op), reads, writes)

        def CP(eng, out, in_, reads, writes):
            return P.add(eng, lambda e: e.tensor_copy(out, in_), reads, writes)

        def RSTD(dst, src_ps, inv_n):
            ACT(dst[:], src_ps[:], AF.Ln, [src_ps, epsb], [dst], bias=epsb[:, 0:1], scale=inv_n)
            ACT(dst[:], dst[:], AF.Exp, [dst], [dst], scale=-0.5)

        def RCP(out, in_, reads, writes):
            return P.add("dve", lambda e: e.reciprocal(out, in_), reads, writes)

        def MSET(ap, val, writes):
            return P.add("pool", lambda e: e.memset(ap, val), (), writes)

        def DMA(eng, out, in_, sem, reads=(), writes=(), extra=()):
            return P.dma(eng, lambda e: e.dma_start(out=out, in_=in_), sem, reads, writes, extra)

        def DMAN(eng, pairs, sem, reads=(), writes=(), extra=()):
            pairs = list(pairs)
            return P.dma(eng, lambda e: [e.dma_start(out=a, in_=b) for a, b in pairs], sem, reads, writes, extra,
                         n=len(pairs))

        def MM(out_ap, pairs, reads, writes, first=True, last=True):
            pairs = list(pairs)

            def fn(e):
                ins = None
                n = len(pairs)
                for i, (a, b) in enumerate(pairs):
                    ins = e.matmul(out_ap, a, b, start=(first and i == 0), stop=(last and i == n - 1))
                return ins
            return P.add("pe", fn, reads, writes)

        def TRN(pairs, reads, writes):
            pairs = list(pairs)

            def fn(e):
                ins = None
                for o_, i_ in pairs:
                    ins = e.transpose(o_, i_, ident.t[:])
                return ins
            return P.add("pe", fn, list(reads) + [ident], writes)

        gall = sb(top, "gall", [128, NG], F32, dma=True)
        ones_bf = sb(top, "ones_bf", [128, 128], BF16)
        ones_f = sb(top, "ones_f", [128, 128], F32)
        bd64 = sb(top, "bd64", [128, 128], BF16)
        ident = sb(top, "ident", [128, 128], F32)
        nlam = sb(top, "nlam", [128, 2], F32)
        subg = sb(top, "subg", [128, 2], F32)

        epsb = sb(top, "epsb", [128, 1], F32)
        MSET(epsb[:], EPS, [epsb])
        DMA("sp", gall[:], gall_in, gall.dsem, writes=[gall])
        MSET(ones_bf[:], 1.0, [ones_bf])
        MSET(ones_f[:], 1.0, [ones_f])
        MSET(bd64[:], 0.0, [bd64])
        MSET(bd64[0:64, 0:64], 1.0, [bd64])
        MSET(bd64[64:128, 64:128], 1.0, [bd64])
        P.add("pool", lambda e: e.iota(ident[:], pattern=[[1, 128]], base=0, channel_multiplier=-1,
                                       allow_small_or_imprecise_dtypes=True), writes=[ident])
        TSS("dve", ident[:], ident[:], 0.0, ALU.is_equal, [ident], [ident])

        NWS = 8
        wsem = [P.new_dsem("sw") for _ in range(NWS)]
        wi = 0
        order = ["f1g", "f1u", "f1d", "abin", "wuq", "wukv", "about", "cin", "cout", "f2g", "f2u", "f2d",
                 "pleg", "plep"]
        conv_pending = []
        conv_ops = {}
        for l in range(4):
            for k in order:
                L, K, N = wshape[k]
                if l >= L:
                    continue
                kp = min(K, 128)
                src = wsrc[k][l].rearrange("(k p) n -> p k n", p=kp)
                dst = wt[k][l].rearrange("p (k n) -> p k n", n=N)
                nk = K // kp
                step = max(1, nk // 4) if nk >= 8 else nk
                conv_ops[(k, l)] = []
                for k0 in range(0, nk, step):
                    k1 = min(nk, k0 + step)
                    conv_pending.append(((k, l), dst[:, k0:k1, :], src[:, k0:k1, :]))
        conv_state = {"i": 0}

        def issue_conv(n_):
            while n_ > 0 and conv_pending:
                key_, d_, s_ = conv_pending.pop(0)
                op_ = DMA("pool", d_, s_, wsem[conv_state["i"] % NWS])
                conv_state["i"] += 1
                conv_ops[key_].append(op_)
                n_ -= 1

        n_first = 0
        for key_, _, _ in conv_pending:
            if key_[1] == 0 and key_[0] in ("f1g", "f1u", "f1d", "abin", "wuq", "wukv"):
                n_first += 1
            else:
                break
        issue_conv(n_first)

        with ExitStack() as st0:
            lv = sb(st0, "lv", [1, 512], F32, dma=True)
            lp = sb(st0, "lp", [1, 256], F32)
            ls = sb(st0, "ls", [1, 4], F32)
            le = sb(st0, "le", [1, 4], F32)
            ln2 = sb(st0, "ln2", [1, 2], F32)
            pst = ps(st0, "ps_pro")
            DMA("sp", lv[:], lamv_in, lv.dsem, writes=[lv])
            lvv = lv.t[:].rearrange("p (o f d) -> p o f d", o=2, f=4)
            lpv = lp.t[:].rearrange("p (o f d) -> p o f d", o=2, f=2)
            for o in range(2):
                for f in range(2):
                    TTo("dve", lpv[:, o, f, :], lvv[:, o, 2 * f, :], lvv[:, o, 2 * f + 1, :], ALU.mult, [lv], [lp])
            lp3 = lp.t[:].rearrange("p (g d) -> p g d", d=64)
            P.add("dve", lambda e: e.reduce_sum(ls[:, 0:4], lp3, axis=mybir.AxisListType.X), reads=[lp], writes=[ls])
            ACT(le[:], ls[:], AF.Exp, [ls], [le])
            lev = le.t[:].rearrange("p (o f) -> p o f", f=2)
            for o in range(2):
                li = 0.8 - 0.6 * math.exp(-0.3 * (2 * o + 1))
                STT("dve", ln2[:, o:o + 1], lev[:, o, 1:2], -li, lev[:, o, 0:1], ALU.add, ALU.subtract, [le], [ln2])
            MM(pst.t[:, 0:2], [(ones_f[0:1, :], ln2[0:1, :])], [ln2, ones_f], [pst])
            CP("dve", nlam[:], pst.t[:, 0:2], [pst], [nlam])
            for o in range(2):
                li = 0.8 - 0.6 * math.exp(-0.3 * (2 * o + 1))
                c0 = GCOL[("sub", o)]
                TSS("dve", subg[:, o:o + 1], gall[:, c0:c0 + 1], 1.0 - li, ALU.mult, [gall], [subg])
            P.barrier(bsem, gall_in[0:1, 0:64], bar_b, skip=wsem)

        def rowlocal_pass(ipass):
            with ExitStack() as st:
                RING = 5
                ring = [sb(st, "ring%d" % i, [128, 5632], BF16, dma=True) for i in range(RING)]
                xbs = [sb(st, "xb%d" % i, [128, 8, TT], F32, dma="both") for i in range(2)]
                xb = xbs[0]
                hn = sb(st, "hn", [128, 8, TT], BF16)
                act = sb(st, "act", [128, 22, TT], BF16)
                sq = sb(st, "sq", [128, 8, TT], BF16)
                rstd = sb(st, "rstd", [128, TT], F32)
                tmpf = [sb(st, "tmpf%d" % i, [128, TT], F32) for i in range(6)]
                tab = [sb(st, "tab%d" % i, [128, TT], F32, dma=True) for i in range(4)]
                otb = sb(st, "otb", [128, 8, TT], BF16, dma=True)
                xin = [sb(st, "xin%d" % i, [128, D], F32, dma="both") for i in range(2)]
                ptk = sb(st, "ptk", [128, 4, 256], F32, dma=True)
                pT = sb(st, "pT", [128, 2, TT], BF16)
                NSTG = 8
                stg = [sb(st, "stg%d" % i, [128, TT], BF16, dma="sw") for i in range(NSTG)]
                vstg = [sb(st, "vstg%d" % i, [128, 520], BF16, dma="sw") for i in range(2)]
                cqn_b = [sb(st, "cqn%d" % i, [128, TT], BF16) for i in range(2)]
                ckvn_b = sb(st, "ckvn", [128, TT], BF16)
                pG = [ps(st, "pG%d" % i) for i in range(2)]
                pU = [ps(st, "pU%d" % i) for i in range(2)]
                pD = [ps(st, "pD%d" % i) for i in range(2)]
                pN = ps(st, "pN")
                pTp = ps(st, "pTp")
                cnt = {"ring": 0, "stg": 0, "vstg": 0, "G": 0, "D": 0, "tmp": 0, "xin": 0}

                for v in vstg:
                    MSET(v[:], 1.0, [v])

                def slab(wname, l, K, N, c0, ncols):
                    b = ring[cnt["ring"] % RING]
                    cnt["ring"] += 1
                    kp = min(K, 128)
                    nk = K // kp
                    src = wt[wname][l].rearrange("p (k n) -> p k n", n=N)[:, :, c0:c0 + ncols]
                    view = b.t[0:kp, 0:nk * ncols].rearrange("p (k n) -> p k n", n=ncols)
                    assert conv_ops[(wname, l)], (wname, l)
                    DMA("sp", view, src, b.dsem, writes=[b], extra=conv_ops[(wname, l)])
                    return b, view

                def next_tmp():
                    t = tmpf[cnt["tmp"] % len(tmpf)]
                    cnt["tmp"] += 1
                    return t

                def next_stg():
                    s_ = stg[cnt["stg"] % NSTG]
                    cnt["stg"] += 1
                    return s_

                def next_D():
                    d_ = pD[cnt["D"] % 2]
                    cnt["D"] += 1
                    return d_

                def rstd_from(ps_buf, inv_n):
                    RSTD(rstd, ps_buf, inv_n)

                def rmsnorm_stats(src_aps, src_bufs, inv_n):
                    n = len(src_aps)
                    for c, (ap, bb) in enumerate(zip(src_aps, src_bufs)):
                        if c % 2 == 0:
                            ACT(sq[:, c, :], ap, AF.Square, [bb], [sq])
                        else:
                            TTo("dve", sq[:, c, :], ap, ap, ALU.mult, [bb], [sq])
                    MM(pN[:], [(ones_bf[:], sq[:, c, :]) for c in range(n)], [sq, ones_bf], [pN])
                    rstd_from(pN, inv_n)

                def norm_x(gcol):
                    rmsnorm_stats([xb[:, c, :] for c in range(8)], [xb] * 8, 1.0 / D)
                    for c in range(8):
                        STT("dve", hn[:, c, :], xb[:, c, :], gall[:, gcol + c:gcol + c + 1], rstd[:], ALU.mult, ALU.mult,
                            [xb, rstd, gall], [hn])

                def ffn(wg, wu, wd, l, gcol):
                    norm_x(gcol)
                    for s0 in range(0, DFF, 512):
                        ncols = min(512, DFF - s0)
                        bg, vg = slab(wg, l, D, DFF, s0, ncols)
                        bu, vu = slab(wu, l, D, DFF, s0, ncols)
                        for j in range(ncols // 128):
                            f = (s0 // 128) + j
                            G = pG[cnt["G"] % 2]
                            U = pU[cnt["G"] % 2]
                            cnt["G"] += 1
                            MM(G[:], [(vg[:, k, j * 128:(j + 1) * 128], hn[:, k, :]) for k in range(8)], [bg, hn], [G])
                            MM(U[:], [(vu[:, k, j * 128:(j + 1) * 128], hn[:, k, :]) for k in range(8)], [bu, hn], [U])
                            t = next_tmp()
                            ACT(t[:], G[:], AF.Silu, [G], [t])
                            TTo("dve", act[:, f, :], t[:], U[:], ALU.mult, [t, U], [act])
                    for m0 in range(0, D, 256):
                        bd_, vd = slab(wd, l, DFF, D, m0, 256)
                        for j in range(2):
                            m = m0 // 128 + j
                            Dp = next_D()
                            MM(Dp[:], [(vd[:, k, j * 128:(j + 1) * 128], act[:, k, :]) for k in range(22)], [bd_, act], [Dp])
                            STT("dve", xb[:, m, :], Dp[:], 0.5, xb[:, m, :], ALU.mult, ALU.add, [Dp, xb], [xb])

                def proj(wname, l, K, N, c0, ncols, rhs_aps, rhs_bufs, consume):
                    nk = len(rhs_aps)
                    done = 0
                    while done < ncols:
                        sc = min(512, ncols - done)
                        b, v = slab(wname, l, K, N, c0 + done, sc)
                        for j in range((sc + 127) // 128):
                            mcols = min(128, sc - j * 128)
                            Dp = next_D()
                            MM(Dp.t[0:mcols, :], [(v[:, k, j * 128:j * 128 + mcols], rhs_aps[k]) for k in range(nk)],
                               [b] + list(rhs_bufs), [Dp])
                            consume((done // 128) + j, Dp, mcols)
                        done += sc

                def proj_tok(wname, l, K, N, c0, ncols, lhs_fn, lhs_bufs, nk, consume):
                    b, v = slab(wname, l, K, N, c0, ncols)
                    for s in range(4):
                        Dp = next_D()
                        MM(Dp.t[:, 0:ncols], [(lhs_fn(k, s), v[:, k, :]) for k in range(nk)], [b] + list(lhs_bufs), [Dp])
                        consume(s, Dp)

                def evac_to(dst_buf, rows=128):
                    def cons(j, Dp, mcols):
                        ACT(dst_buf.t[0:rows, :], Dp.t[0:rows, :], AF.Identity, [Dp], [dst_buf])
                    return cons

                hn_aps = [hn[:, k, :] for k in range(8)]

                for ti in range(DBG.get("ntile", NTILE)):
                    t0 = ti * TT
                    tsl = slice(t0, t0 + TT)
                    xb = xbs[ti % 2]
                    if ipass == 0:
                        for s in range(4):
                            xi = xin[cnt["xin"] % 2]
                            cnt["xin"] += 1
                            DMA("sp", xi[:], x_in[t0 + s * 128:t0 + (s + 1) * 128, :], xi.dsem, writes=[xi])
                            for c0 in range(0, 8, 4):
                                TRN([(pTp.t[:, cc * 128:(cc + 1) * 128], xi.t[:, (c0 + cc) * 128:(c0 + cc + 1) * 128])
                                     for cc in range(4)], [xi], [pTp])
                                CP("dve", xb.t[:, c0:c0 + 4, s * 128:(s + 1) * 128],
                                   pTp.t[:].rearrange("p (c t) -> p c t", t=128), [pTp], [xb])
                    else:
                        DMA("sp", xb[:], xs[:, :, tsl].rearrange("c p t -> p c t"), xb.dsem, writes=[xb])

                    if ipass > 0:
                        lp_ = ipass - 1
                        DMA("sp", otb[:], OT[:, tsl].rearrange("(c p) t -> p c t", p=128), otb.dsem, writes=[otb])
                        wo = "about" if lp_ % 2 == 0 else "cout"

                        def cons_out(m, Dp, mcols):
                            TTo("dve", xb[:, m, :], Dp[:], xb[:, m, :], ALU.add, [Dp, xb], [xb])
                        proj(wo, lp_ // 2, D, D, 0, D, [otb[:, k, :] for k in range(8)], [otb], cons_out)
                        ffn("f2g", "f2u", "f2d", lp_, GCOL[("ffn2", lp_)])
                        DMA("sp", ptk[:], p_in[lp_, t0:t0 + TT, :].rearrange("(s p) d -> p s d", p=128), ptk.dsem,
                            writes=[ptk])
                        for dc in range(2):
                            TRN([(pTp.t[:, s * 128:(s + 1) * 128], ptk.t[:, s, dc * 128:(dc + 1) * 128]) for s in range(4)],
                                [ptk], [pTp])
                            CP("dve", pT[:, dc, :], pTp[:], [pTp], [pT])
                        norm_x(GCOL[("ple", lp_)])
                        for m0 in range(0, D, 512):
                            bg_, vg_ = slab("pleg", lp_, D, D, m0, 512)
                            bp_, vp_ = slab("plep", lp_, 256, D, m0, 512)
                            for j in range(4):
                                m = m0 // 128 + j
                                G = pG[cnt["G"] % 2]
                                U = pU[cnt["G"] % 2]
                                cnt["G"] += 1
                                MM(G[:], [(vg_[:, k, j * 128:(j + 1) * 128], hn[:, k, :]) for k in range(8)], [bg_, hn], [G])
                                MM(U[:], [(vp_[:, k, j * 128:(j + 1) * 128], pT[:, k, :]) for k in range(2)], [bp_, pT], [U])
                                t = next_tmp()
                                ACT(t[:], G[:], AF.Sigmoid, [G], [t])
                                t2 = next_tmp()
                                TTo("dve", t2[:], t[:], U[:], ALU.mult, [t, U], [t2])
                                TTo("dve", xb[:, m, :], t2[:], xb[:, m, :], ALU.add, [t2, xb], [xb])

                    if ipass < DEPTH:
                        li = ipass
                        ffn("f1g", "f1u", "f1d", li, GCOL[("ffn1", li)])
                        norm_x(GCOL[("mix", li)])
                        if li % 2 == 0:
                            e_ = li // 2
                            for i_, (src_, row) in enumerate(((ropeA_in, 0), (ropeA_in, 1), (ropeB_in, 0), (ropeB_in, 1))):
                                DMA("sp", tab[i_][:], src_[row, :, tsl], tab[i_].dsem, writes=[tab[i_]])
                            for (nm, nch, gq, dst) in (("q", 4, "aq", QA), ("k", 1, "ak", KA)):
                                for c in range(nch):
                                    z = next_tmp()
                                    zw = next_tmp()
                                    proj("abin", e_, D, ABX_N, ABX[nm] + c * 128, 128, hn_aps, [hn], evac_to(z))
                                    proj("abin", e_, D, ABX_N, ABX[nm + "_sw"] + c * 128, 128, hn_aps, [hn], evac_to(zw))
                                    ACT(sq[:, 0, :], z[:], AF.Square, [z], [sq])
                                    MM(pN[:], [(bd64[:], sq[:, 0, :])], [sq, bd64], [pN])
                                    rstd_from(pN, 1.0 / 64)
                                    g0 = GCOL[(gq, e_)]
                                    g1 = GCOL[(gq + "_sw", e_)]
                                    STT("dve", z[:], z[:], gall[:, g0:g0 + 1], tab[0][:], ALU.mult, ALU.mult,
                                        [z, gall, tab[0]], [z])
                                    STT("dve", zw[:], zw[:], gall[:, g1:g1 + 1], tab[1][:], ALU.mult, ALU.mult,
                                        [zw, gall, tab[1]], [zw])
                                    TTo("dve", z[:], z[:], zw[:], ALU.add, [z, zw], [z])
                                    s_ = next_stg()
                                    TTo("dve", s_[:], z[:], rstd[:], ALU.mult, [z, rstd], [s_])
                                    DMA("pool", dst[c * 128:(c + 1) * 128, tsl], s_[:], s_.dsem, reads=[s_])
                            vs = vstg[cnt["vstg"] % 2]
                            cnt["vstg"] += 1

                            def consVA(s, Dp, vs=vs, t0=t0):
                                CP("dve", vs.t[:, 0:130].rearrange("p (g d) -> p g d", d=65)[:, :, 0:64],
                                   Dp.t[:, 0:128].rearrange("p (g d) -> p g d", d=64), [Dp], [vs])
                                DMA("pool", VA[t0 + s * 128:t0 + (s + 1) * 128, :], vs.t[:, 0:130], vs.dsem, reads=[vs])
                            proj_tok("abin", e_, D, ABX_N, ABX["v"], 128,
                                     lambda k, s: hn[:, k, s * 128:(s + 1) * 128], [hn], 8, consVA)
                            cq = [next_tmp(), next_tmp()]
                            for c in range(2):
                                proj("abin", e_, D, ABX_N, ABX["cq"] + c * 128, 128, hn_aps, [hn], evac_to(cq[c]))
                            rmsnorm_stats([cq[0][:], cq[1][:]], cq, 1.0 / 256)
                            cqn = cqn_b
                            for c in range(2):
                                gq_ = GCOL[("bq", e_)] + c
                                STT("dve", cqn[c][:], cq[c][:], gall[:, gq_:gq_ + 1], rstd[:], ALU.mult, ALU.mult,
                                    [cq[c], gall, rstd], [cqn[c]])
                            ckv = next_tmp()
                            proj("abin", e_, D, ABX_N, ABX["ckv"], 128, hn_aps, [hn], evac_to(ckv))
                            rmsnorm_stats([ckv[:]], [ckv], 1.0 / 128)
                            ckvn = ckvn_b
                            gk_ = GCOL[("bkv", e_)]
                            STT("dve", ckvn[:], ckv[:], gall[:, gk_:gk_ + 1], rstd[:], ALU.mult, ALU.mult,
                                [ckv, gall, rstd], [ckvn])
                            kr = next_tmp()
                            krw = next_tmp()
                            proj("abin", e_, D, ABX_N, ABX["kr"], 32, hn_aps, [hn], evac_to(kr, 32))
                            proj("abin", e_, D, ABX_N, ABX["kr_sw"], 32, hn_aps, [hn], evac_to(krw, 32))
                            TTo("dve", kr[0:32, :], kr[0:32, :], tab[2][0:32, :], ALU.mult, [kr, tab[2]], [kr])
                            TTo("dve", krw[0:32, :], krw[0:32, :], tab[3][0:32, :], ALU.mult, [krw, tab[3]], [krw])
                            s_ = next_stg()
                            TTo("dve", s_[0:32, :], kr[0:32, :], krw[0:32, :], ALU.add, [kr, krw], [s_])
                            DMA("pool", KRB[:, tsl], s_[0:32, :], s_.dsem, reads=[s_])

                            def cons_qn(j, Dp, mcols, tsl=tsl):
                                s_ = next_stg()
                                CP("dve", s_[:], Dp[:], [Dp], [s_])
                                DMAN("pool", [(QB[2 * j + hh_, 0:64, tsl], s_.t[hh_ * 64:(hh_ + 1) * 64, :]) for hh_ in range(2)],
                                     s_.dsem, reads=[s_])
                            proj("wuq", e_, 256, 1024, 0, 512, [cqn[0][:], cqn[1][:]], cqn, cons_qn)
                            for c in range(2):
                                z = next_tmp()
                                zw = next_tmp()
                                proj("wuq", e_, 256, 1024, 512 + c * 128, 128, [cqn[0][:], cqn[1][:]], cqn, evac_to(z))
                                proj("wuq", e_, 256, 1024, 768 + c * 128, 128, [cqn[0][:], cqn[1][:]], cqn, evac_to(zw))
                                TTo("dve", z[:], z[:], tab[2][:], ALU.mult, [z, tab[2]], [z])
                                TTo("dve", zw[:], zw[:], tab[3][:], ALU.mult, [zw, tab[3]], [zw])
                                s_ = next_stg()
                                TTo("dve", s_[:], z[:], zw[:], ALU.add, [z, zw], [s_])
                                DMAN("pool", [(QB[4 * c + hh_, 64:96, tsl], s_.t[hh_ * 32:(hh_ + 1) * 32, :]) for hh_ in range(4)],
                                     s_.dsem, reads=[s_])

                            def cons_kn(j, Dp, mcols, tsl=tsl):
                                s_ = next_stg()
                                CP("dve", s_[:], Dp[:], [Dp], [s_])
                                DMAN("pool", [(KB[2 * j + hh_, :, tsl], s_.t[hh_ * 64:(hh_ + 1) * 64, :]) for hh_ in range(2)],
                                     s_.dsem, reads=[s_])
                            proj("wukv", e_, 128, 1024, 0, 512, [ckvn[:]], [ckvn], cons_kn)
                            vs = vstg[cnt["vstg"] % 2]
                            cnt["vstg"] += 1

                            def consVB(s, Dp, vs=vs, t0=t0):
                                CP("dve", vs.t[:, 0:520].rearrange("p (g d) -> p g d", d=65)[:, :, 0:64],
                                   Dp.t[:, 0:512].rearrange("p (g d) -> p g d", d=64), [Dp], [vs])
                                DMA("pool", VB[t0 + s * 128:t0 + (s + 1) * 128, :], vs.t[:, 0:520], vs.dsem, reads=[vs])
                            proj_tok("wukv", e_, 128, 1024, 512, 512,
                                     lambda k, s: ckvn[:, s * 128:(s + 1) * 128], [ckvn], 1, consVB)
                        else:
                            o_ = li // 2
                            for which, dst in ((0, QC), (1024, KC)):
                                def cons_qk(j, Dp, mcols, dst=dst, tsl=tsl):
                                    s_ = next_stg()
                                    CP("dve", s_[:], Dp[:], [Dp], [s_])
                                    DMA("pool", dst[j * 128:(j + 1) * 128, tsl], s_[:], s_.dsem, reads=[s_])
                                proj("cin", o_, D, 3072, which, 1024, hn_aps, [hn], cons_qk)
                            for half in range(2):
                                def consVC(s, Dp, half=half, t0=t0):
                                    s_ = next_stg()
                                    CP("dve", s_[:], Dp[:], [Dp], [s_])
                                    DMA("pool", VC[t0 + s * 128:t0 + (s + 1) * 128, half * 512:(half + 1) * 512], s_[:],
                                        s_.dsem, reads=[s_])
                                proj_tok("cin", o_, D, 3072, 2048 + half * 512, 512,
                                         lambda k, s: hn[:, k, s * 128:(s + 1) * 128], [hn], 8, consVC)
                        DMA("pool", xs[:, :, tsl].rearrange("c p t -> p c t"), xb[:], xb.dsem, reads=[xb])
                    else:
                        rmsnorm_stats([xb[:, c, :] for c in range(8)], [xb] * 8, 1.0 / D)
                        gcol = GCOL["final"]
                        for c in range(8):
                            STT("dve", xb[:, c, :], xb[:, c, :], gall[:, gcol + c:gcol + c + 1], rstd[:], ALU.mult, ALU.mult,
                                [xb, rstd, gall], [xb])
                        for s in range(4):
                            xi = xin[cnt["xin"] % 2]
                            cnt["xin"] += 1
                            for c0 in range(0, 8, 4):
                                TRN([(pTp.t[:, cc * 128:(cc + 1) * 128], xb.t[:, c0 + cc, s * 128:(s + 1) * 128])
                                     for cc in range(4)], [xb], [pTp])
                                CP("dve", xi.t[:, c0 * 128:(c0 + 4) * 128], pTp[:], [pTp], [xi])
                            DMA("pool", y_out[t0 + s * 128:t0 + (s + 1) * 128, :], xi[:], xi.dsem, reads=[xi])
                P.barrier(bsem, gall_in[0:1, 0:64], bar_b)
                P.release_dsems([b.dsem for b in ring + xbs + [otb, ptk] + tab + xin + stg + vstg])

        def ps2(stack, name):
            uid["n"] += 1
            return Buf(stack.enter_context(nc.psum_tensor("p%d_%s" % (uid["n"], name), [128, 1024], F32)))

        def MMS(specs, reads, writes):
            specs = list(specs)

            def fn(e):
                ins = None
                for (o_, a_, b_, s0_, s1_) in specs:
                    ins = e.matmul(o_, a_, b_, start=s0_, stop=s1_)
                return ins
            return P.add("pe", fn, reads, writes)

        def attn_even(e_):
            with ExitStack() as st:
                S2 = [ps2(st, "S2_%d" % i) for i in range(2)]
                O = [ps(st, "O%d" % i) for i in range(2)]
                Bp = ps(st, "Bp")
                Kt = [sb(st, "Kt%d" % i, [128, NT], BF16, dma=True) for i in range(2)]
                Qt = [sb(st, "Qt%d" % i, [128, NT], BF16, dma=True) for i in range(2)]
                Vt = sb(st, "Vt", [128, 64, 584], BF16, dma=True)
                maskb = sb(st, "maskb", [128, 1024], F32, dma=True)
                NP_ = 3
                Pt = [sb(st, "Pt%d" % i, [128, 2 * TT], BF16) for i in range(NP_)]
                rl = [sb(st, "rl%d" % i, [128, TT], F32) for i in range(2)]
                osb = [sb(st, "osb%d" % i, [128, TT], F32) for i in range(2)]
                ostg = [sb(st, "ostg%d" % i, [128, TT], BF16, dma="sw") for i in range(4)]
                DMA("sp", maskb[:], maskb_in, maskb.dsem, writes=[maskb])
                for b_ in Kt + Qt + [Vt]:
                    MSET(b_[:], 0.0, [b_])
                qcount = 0
                ecount = 0
                for mixer in ("A", "B"):
                    if mixer == "A":
                        Kd, scale, vw, Vsrc = 128, 64 ** -0.5, 130, VA
                    else:
                        Kd, scale, vw, Vsrc = 96, 96 ** -0.5, 520, VB
                    vsrc = Vsrc.rearrange("(t p) c -> p t c", p=128)
                    DMAN("sp", [(Vt.t[:, 16 * i:16 * (i + 1), 0:vw], vsrc[:, 16 * i:16 * (i + 1), :]) for i in range(4)],
                         Vt.dsem, writes=[Vt])
                    qbase = qcount
                    qcount += 8

                    def Kbuf(h):
                        return Kt[(h // 4) % 2] if mixer == "A" else Kt[h % 2]

                    def Qbuf(h):
                        return Qt[(qbase + h) % 2]

                    def load_head(h):
                        q_ = Qbuf(h)
                        k_ = Kbuf(h)
                        if mixer == "A":
                            g = h // 4
                            if h % 4 == 0:
                                DMA("sp", k_.t[0:64, :], KA[g * 64:(g + 1) * 64, :], k_.dsem, writes=[k_])
                            DMA("sp", q_.t[0:64, :], QA[h * 64:(h + 1) * 64, :], q_.dsem, writes=[q_])
                        else:
                            DMAN("sp", [(k_.t[0:64, :], KB[h, :, :]), (k_.t[64:96, :], KRB[:, :])], k_.dsem, writes=[k_])
                            DMA("sp", q_.t[0:96, :], QB[h, :, :], q_.dsem, writes=[q_])

                    LA = 2
                    n = 8 * 8 * 64
                    load_head(0)
                    for i in range(n + LA):
                        if i % 512 == LA and (i // 512) + 1 < 8:
                            load_head(i // 512 + 1)
                        if i < n:
                            h, qp, t = i // 512, (i % 512) // 64, i % 64
                            K_, Q_ = Kbuf(h), Qbuf(h)
                            Sb = S2[i % 2]
                            kt_ = K_.t[0:Kd, t * 128:(t + 1) * 128]
                            MMS([(Sb.t[:, 0:TT], kt_, Q_.t[0:Kd, (2 * qp) * TT:(2 * qp + 1) * TT], True, True),
                                 (Sb.t[:, TT:2 * TT], kt_, Q_.t[0:Kd, (2 * qp + 1) * TT:(2 * qp + 2) * TT], True, True)],
                                [K_, Q_], [Sb])
                            pt = Pt[i % NP_]
                            blk = (2 * qp) * 64 + t
                            ACT(pt[:], Sb[:], AF.Exp, [Sb, maskb], [pt], bias=maskb[:, blk:blk + 1], scale=scale)
                        j = i - LA
                        if j >= 0:
                            h, qp, t = j // 512, (j % 512) // 64, j % 64
                            vcol = (h // 4) * 65 if mixer == "A" else h * 65
                            orow = (0 if mixer == "A" else 512) + h * 64
                            pt = Pt[j % NP_]
                            vt_ = Vt.t[:, t, vcol:vcol + 128]
                            MMS([(O[0].t[:, :], vt_, pt.t[:, 0:TT], t == 0, t == 63),
                                 (O[1].t[:, :], vt_, pt.t[:, TT:2 * TT], t == 0, t == 63)], [pt, Vt], [O[0], O[1]])
                            if t == 63:
                                issue_conv(2)
                                for k2 in range(2):
                                    CP("dve", osb[k2][0:65, :], O[k2].t[0:65, :], [O[k2]], [osb[k2]])
                                for k2 in range(2):
                                    RCP(rl[k2][64:65, :], osb[k2][64:65, :], [osb[k2]], [rl[k2]])
                                for k2 in range(2):
                                    MM(Bp.t[0:64, :], [(ones_f[64:65, 0:64], rl[k2][64:65, :])], [rl[k2], ones_f], [Bp])
                                    og = ostg[ecount % 4]
                                    ecount += 1
                                    TTo("dve", og[0:64, :], osb[k2][0:64, :], Bp.t[0:64, :], ALU.mult, [osb[k2], Bp], [og])
                                    qb = 2 * qp + k2
                                    DMA("pool", OT[orow:orow + 64, qb * TT:(qb + 1) * TT], og[0:64, :], og.dsem, reads=[og])
                issue_conv(len(conv_pending))
                P.barrier(bsem, gall_in[0:1, 0:64], bar_b)
                P.release_dsems([b.dsem for b in Kt + Qt + [Vt, maskb] + ostg])

        ALIBI_THR = 40.0

        def attn_odd(o_):
            with ExitStack() as st:
                S = [ps(st, "S%d" % i) for i in range(4)]
                O1 = ps(st, "O1")
                O2 = ps(st, "O2")
                L1 = ps(st, "L1")
                L2 = ps(st, "L2")
                Kt = [sb(st, "Kt%d" % i, [128, NT], BF16, dma=True) for i in range(2)]
                Qt = [sb(st, "Qt%d" % i, [128, NT], BF16, dma=True) for i in range(2)]
                Vt = [sb(st, "Vt%d" % i, [128, 64, 128], BF16, dma=True) for i in range(2)]
                maskb = sb(st, "maskb", [128, 1024], F32, dma=True)
                cdist = sb(st, "cdist", [128, 1024], F32, dma=True)
                dtl = sb(st, "dtl", [128, 6, 512], F32, dma=True)
                biasC = [sb(st, "biasC%d" % i, [128, 1024], F32) for i in range(2)]
                NP_ = 3
                Pt = [sb(st, "Pt%d" % i, [128, 2 * TT], BF16) for i in range(NP_)]
                tm = [sb(st, "tm%d" % i, [128, 2 * TT], F32) for i in range(NP_)]
                r1 = sb(st, "r1", [128, TT], F32)
                r2 = sb(st, "r2", [128, TT], F32)
                o1 = sb(st, "o1", [128, TT], F32)
                o2 = sb(st, "o2", [128, TT], F32)
                sqo = sb(st, "sqo", [128, TT], BF16)
                rs = sb(st, "rs", [128, TT], F32)
                ostg = [sb(st, "ostg%d" % i, [128, TT], BF16, dma="sw") for i in range(2)]
                DMA("sp", maskb[:], maskb_in, maskb.dsem, writes=[maskb])
                DMA("sp", cdist[:], cdist_in, cdist.dsem, writes=[cdist])
                DMA("sp", dtl[:], dtl_in, dtl.dsem, writes=[dtl])
                scale = 64 ** -0.5

                def load_head(h):
                    slope = 2.0 ** (-(h + 1))
                    K_, Q_, V_, bC = Kt[h % 2], Qt[h % 2], Vt[h % 2], biasC[h % 2]
                    DMA("sp", K_[:], KC[h * 128:(h + 1) * 128, :], K_.dsem, writes=[K_])
                    DMA("sp", Q_[:], QC[h * 128:(h + 1) * 128, :], Q_.dsem, writes=[Q_])
                    vsrc = VC[:, h * 128:(h + 1) * 128].rearrange("(t p) c -> p t c", p=128)
                    DMAN("sp", [(V_.t[:, 16 * i:16 * (i + 1), :], vsrc[:, 16 * i:16 * (i + 1), :]) for i in range(4)],
                         V_.dsem, writes=[V_])
                    STT("dve", bC[:], cdist[:], slope, maskb[:], ALU.mult, ALU.add, [cdist, maskb], [bC])

                items = []
                head_start = {}
                for h in range(8):
                    slope = 2.0 ** (-(h + 1))
                    head_start[len(items)] = h
                    for qb in range(16):
                        q_lo, q_hi = qb * TT, qb * TT + TT - 1
                        keep = []
                        for t in range(64):
                            s_lo, s_hi = t * 128, t * 128 + 127
                            mind = max(0, s_lo - q_hi, q_lo - s_hi)
                            if slope * mind < ALIBI_THR:
                                keep.append(t)
                        for t in keep:
                            items.append((h, qb, t, t == keep[0], t == keep[-1]))
                LA = 2
                n = len(items)
                load_head(0)
                ecount = 0
                for i in range(n + LA):
                    if (i - LA) in head_start and head_start[i - LA] + 1 < 8:
                        load_head(head_start[i - LA] + 1)
                    if i < n:
                        h, qb, t, fs, ls_ = items[i]
                        slope = 2.0 ** (-(h + 1))
                        K_, Q_, bC = Kt[h % 2], Qt[h % 2], biasC[h % 2]
                        Sa = S[(i % 2) * 2]
                        Sb = S[(i % 2) * 2 + 1]
                        MM(Sa[:], [(K_.t[0:64, t * 128:(t + 1) * 128], Q_.t[0:64, qb * TT:(qb + 1) * TT])], [K_, Q_], [Sa])
                        MM(Sb[:], [(K_.t[64:128, t * 128:(t + 1) * 128], Q_.t[64:128, qb * TT:(qb + 1) * TT])], [K_, Q_], [Sb])
                        if t < 4 * qb:
                            di = 0
                        elif t > 4 * qb + 3:
                            di = 1
                        else:
                            di = 2 + (t - 4 * qb)
                        tmb = tm[i % NP_]
                        fac = slope / scale
                        STT("dve", tmb[:, 0:TT], dtl[:, di, :], fac, Sa[:], ALU.mult, ALU.add, [Sa, dtl], [tmb])
                        STT("dve", tmb[:, TT:2 * TT], dtl[:, di, :], fac, Sb[:], ALU.mult, ALU.add, [Sb, dtl], [tmb])
                        pt = Pt[i % NP_]
                        blk = qb * 64 + t
                        ACT(pt[:], tmb[:], AF.Exp, [tmb, bC], [pt], bias=bC[:, blk:blk + 1], scale=scale)
                    j = i - LA
                    if j >= 0:
                        h, qb, t, fs, ls_ = items[j]
                        V_ = Vt[h % 2]
                        pt = Pt[j % NP_]
                        MMS([(O1.t[:, :], V_.t[:, t, :], pt.t[:, 0:TT], fs, ls_),
                             (L1.t[:, :], ones_bf.t[:, :], pt.t[:, 0:TT], fs, ls_),
                             (O2.t[:, :], V_.t[:, t, :], pt.t[:, TT:2 * TT], fs, ls_),
                             (L2.t[:, :], ones_bf.t[:, :], pt.t[:, TT:2 * TT], fs, ls_)],
                            [pt, V_, ones_bf], [O1, L1, O2, L2])
                        if ls_:
                            ACT(r1[:], L1[:], AF.Ln, [L1], [r1])
                            ACT(r2[:], L2[:], AF.Ln, [L2], [r2])
                            ACT(r1[:], r1[:], AF.Exp, [r1], [r1], scale=-1.0)
                            ACT(r2[:], r2[:], AF.Exp, [r2], [r2], scale=-1.0)
                            TTo("dve", o1[:], O1[:], r1[:], ALU.mult, [O1, r1], [o1])
                            TTo("dve", o2[:], O2[:], r2[:], ALU.mult, [O2, r2], [o2])
                            STT("dve", o1[:], o2[:], nlam[:, o_:o_ + 1], o1[:], ALU.mult, ALU.add, [o1, o2, nlam], [o1])
                            ACT(sqo[:], o1[:], AF.Square, [o1], [sqo])
                            MM(L1[:], [(ones_bf[:], sqo[:])], [sqo, ones_bf], [L1])
                            RSTD(rs, L1, 1.0 / 128)
                            og = ostg[ecount % 2]
                            ecount += 1
                            STT("dve", og[:], o1[:], subg[:, o_:o_ + 1], rs[:], ALU.mult, ALU.mult, [o1, subg, rs], [og])
                            DMA("pool", OT[h * 128:(h + 1) * 128, qb * TT:(qb + 1) * TT], og[:], og.dsem, reads=[og])
                P.barrier(bsem, gall_in[0:1, 0:64], bar_b)
                P.release_dsems([b.dsem for b in Kt + Qt + Vt + [maskb, cdist, dtl] + ostg])

        for ipass in range(DBG["passes"]):
            rowlocal_pass(ipass)
            if ipass < DEPTH and DBG["attn"]:
                if ipass % 2 == 0:
                    attn_even(ipass // 2)
                else:
                    attn_odd(ipass // 2)
        P.add("sp", lambda e: e.nop(), extra=[P.last_barrier])
        P.emit(block)
    return nc


def _tables(seq_len):
    t = np.arange(NT)
    tl = t % seq_len
    row, col = tl // 64, tl % 64
    inv = (10000.0 ** (-np.arange(16, dtype=np.float32) * (2.0 / 32))).astype(np.float32)

    def cs(pos, d_idx):
        jj = d_idx % 32
        i = jj % 16
        ang = pos[None, :].astype(np.float32) * inv[i][:, None]
        c = np.cos(ang).astype(np.float32)
        s = np.sin(ang).astype(np.float32)
        s = np.where((jj < 16)[:, None], -s, s)
        return c, s
    d = np.arange(128)
    j = d % 64
    cA = np.zeros((128, NT), np.float32)
    sA = np.zeros((128, NT), np.float32)
    m_row = j < 32
    c1, s1 = cs(row, d)
    c2, s2 = cs(col, d)
    cA[m_row], sA[m_row] = c1[m_row], s1[m_row]
    cA[~m_row], sA[~m_row] = c2[~m_row], s2[~m_row]
    cB, sB = cs(tl, d)
    ropeA = np.stack([cA, sA]).astype(np.float32)
    ropeB = np.stack([cB, sB]).astype(np.float32)
    qb = np.arange(16)[:, None]
    tt = np.arange(64)[None, :]
    q0 = qb * 512
    s0 = tt * 128
    same = (q0 // seq_len) == (s0 // seq_len)
    maskb = np.where(same, 0.0, NEG).astype(np.float32).reshape(1, 1024)
    diag = (tt >= 4 * qb) & (tt <= 4 * qb + 3)
    cd = np.where(diag, 0.0, -np.abs(q0 - s0)).astype(np.float32).reshape(1, 1024)
    maskb = np.repeat(maskb, 128, 0)
    cd = np.repeat(cd, 128, 0)
    return ropeA, ropeB, maskb, cd


def _dtiles():
    p = np.arange(128)[:, None]
    j = np.arange(512)[None, :]
    tiles = [-(j - p), (j - p)] + [-np.abs(j - p - 128 * c) for c in range(4)]
    return np.ascontiguousarray(np.stack(tiles, 1).astype(np.float32))


_CACHE = {}


def kernel(**inp):
    f = lambda a: np.ascontiguousarray(np.asarray(a, dtype=np.float32))
    x_prompt, x_sample = f(inp["x_prompt"]), f(inp["x_sample"])
    p_prompt, p_sample = f(inp["p_prompt"]), f(inp["p_sample"])

    sw64 = np.concatenate([_swap32(64)])
    abin = f(inp["ab_w_in"])
    q_idx = np.arange(512)
    q_sw_idx = (q_idx // 64) * 64 + sw64[q_idx % 64]
    k_idx = 512 + np.arange(128)
    k_sw_idx = 512 + (np.arange(128) // 64) * 64 + sw64[np.arange(128) % 64]
    kr_idx = 1152 + np.arange(32)
    kr_sw_idx = 1152 + _swap32(32)
    cols = np.concatenate([q_idx, q_sw_idx, k_idx, k_sw_idx, 768 + np.arange(256), 1024 + np.arange(128),
                           kr_idx, kr_sw_idx, 640 + np.arange(128)])
    abin_x = np.ascontiguousarray(abin[:, :, cols])
    wuq = f(inp["b_w_uq"])
    hh = np.arange(8)[:, None]
    nope_idx = (hh * 96 + np.arange(64)[None, :]).reshape(-1)
    rope_idx = (hh * 96 + 64 + np.arange(32)[None, :]).reshape(-1)
    rope_sw_idx = (hh * 96 + 64 + _swap32(32)[None, :]).reshape(-1)
    wuq_x = np.ascontiguousarray(wuq[:, :, np.concatenate([nope_idx, rope_idx, rope_sw_idx])])
    wukv = f(inp["b_w_ukv"])
    kn_idx = (hh * 128 + np.arange(64)[None, :]).reshape(-1)
    v_idx = (hh * 128 + 64 + np.arange(64)[None, :]).reshape(-1)
    wukv_x = np.ascontiguousarray(wukv[:, :, np.concatenate([kn_idx, v_idx])])

    gall = np.zeros((128, NG), np.float32)

    def put(col, vec):
        v = np.asarray(vec, np.float32).reshape(-1, 128).T
        gall[:, col:col + v.shape[1]] = v
    for i in range(DEPTH):
        put(GCOL[("ffn1", i)], inp["ffn1_norm"][i])
        put(GCOL[("mix", i)], inp["mix_norm"][i])
        put(GCOL[("ffn2", i)], inp["ffn2_norm"][i])
        put(GCOL[("ple", i)], inp["ple_norm"][i])
    put(GCOL["final"], inp["final_norm"])
    for e in range(2):
        aq = np.asarray(inp["a_q_norm"][e], np.float32)
        ak = np.asarray(inp["a_k_norm"][e], np.float32)
        put(GCOL[("aq", e)], np.tile(aq, 2))
        put(GCOL[("aq_sw", e)], np.tile(aq[sw64], 2))
        put(GCOL[("ak", e)], np.tile(ak, 2))
        put(GCOL[("ak_sw", e)], np.tile(ak[sw64], 2))
        put(GCOL[("bq", e)], inp["b_q_norm"][e])
        put(GCOL[("bkv", e)], inp["b_kv_norm"][e])
    for o in range(2):
        put(GCOL[("sub", o)], inp["c_sub_norm"][o])
    lamv = np.stack([np.stack([f(inp["c_lambda_q1"])[o], f(inp["c_lambda_k1"])[o],
                               f(inp["c_lambda_q2"])[o], f(inp["c_lambda_k2"])[o]]) for o in range(2)])
    lamv = np.ascontiguousarray(lamv.reshape(1, 512))

    shared = {
        "f1g": f(inp["ffn1_wg"]), "f1u": f(inp["ffn1_wu"]), "f1d": f(inp["ffn1_wd"]),
        "f2g": f(inp["ffn2_wg"]), "f2u": f(inp["ffn2_wu"]), "f2d": f(inp["ffn2_wd"]),
        "abin": abin_x, "wuq": wuq_x, "wukv": wukv_x, "about": f(inp["ab_w_out"]),
        "cin": f(inp["c_w_in"]), "cout": f(inp["c_w_out"]),
        "pleg": f(inp["ple_w_gate"]), "plep": f(inp["ple_w_proj"]),
        "gall": gall, "lamv": lamv, "dtiles": _dtiles(),
    }
    tabs = {8192: _tables(8192), 2048: _tables(2048)}
    in_maps = []
    for core in range(8):
        u = core if core < N_UNITS else core - 2
        if u < 4:
            xu = x_prompt[u]
            pu = p_prompt[:, u]
            tb = tabs[8192]
        else:
            s = (u - 4) * 4
            xu = x_sample[s:s + 4].reshape(NT, D)
            pu = p_sample[:, s:s + 4].reshape(DEPTH, NT, 256)
            tb = tabs[2048]
        m = dict(shared)
        m["x"] = np.ascontiguousarray(xu)
        m["p"] = np.ascontiguousarray(pu)
        m["ropeA"], m["ropeB"], m["maskb"], m["cdist"] = tb
        in_maps.append(m)

    if "nc" not in _CACHE:
        _CACHE["nc"] = build_program()
    nc = _CACHE["nc"]
    res = run_bass_kernel_spmd(nc, in_maps, core_ids=list(range(8)))
    if DBG.get("dump"):
        _CACHE["res"] = res.results
    ys = [np.asarray(r["y"], dtype=np.float32) for r in res.results]
    y_prompt = np.stack(ys[0:4]).reshape(4, NT, D)
    y_sample = np.concatenate([ys[4].reshape(4, 2048, D), ys[5].reshape(4, 2048, D)], 0)
    return (y_prompt, y_sample)
```

```python
import math
from contextlib import ExitStack

import numpy as np
import concourse.bass as bass
import concourse.mybir as mybir
from concourse.bass_utils import run_bass_kernel_spmd

F32 = mybir.dt.float32
BF16 = mybir.dt.bfloat16
AF = mybir.ActivationFunctionType
ALU = mybir.AluOpType

DEPTH = 4
D = 1024
DFF = 2816
NT = 8192
TT = 512
NTILE = NT // TT
EPS = 1e-6
NEG = -30000.0
N_UNITS = 6

DBG = {"passes": 5, "attn": True}


class DSem:
    def __init__(self, handle, kind="hw"):
        self.handle = handle
        self.total = 0
        self.last = None
        self.kind = kind
        self.sw = None


class Op:
    __slots__ = ("eng", "fn", "deps", "is_dma", "sem", "val", "needs_inc", "n")

    def __init__(self, eng, fn, deps, is_dma=False, sem=None, n=1):
        self.eng = eng
        self.fn = fn
        self.deps = deps
        self.is_dma = is_dma
        self.sem = sem
        self.val = 0
        self.needs_inc = False
        self.n = n


class Buf:
    def __init__(self, t, dsem=None):
        self.t = t
        self.w = None
        self.r = []
        self.dsem = dsem

    def __getitem__(self, k):
        return self.t[k]


class Prog:
    ENGS = ("pe", "act", "dve", "pool", "sp")

    def __init__(self, nc, esems, dsem_handles):
        self.nc = nc
        self.ops = {e: [] for e in self.ENGS}
        self.esem = esems
        half = len(dsem_handles) // 2
        self.free = {"hw": [DSem(h, "hw") for h in dsem_handles[:half]],
                     "sw": [DSem(h, "sw") for h in dsem_handles[half:]]}
        self.all_dsems = self.free["hw"] + self.free["sw"]
        self.last_barrier = None

    def new_dsem(self, kind="hw"):
        return self.free[kind].pop()

    def release_dsems(self, ds):
        for d in ds:
            self.free[d.kind].append(d)
            if d.sw is not None:
                self.free["sw"].append(d.sw)
                d.sw = None

    def _deps(self, eng, reads, writes, extra):
        deps = []
        for b in reads:
            if b.w is not None:
                deps.append(b.w)
        strict = (eng == "pool")
        for b in writes:
            for r in b.r:
                if r.eng != eng or r.is_dma or strict:
                    deps.append(r)
            if b.w is not None and (b.w.eng != eng or b.w.is_dma or strict):
                deps.append(b.w)
        deps.extend(d for d in extra if d is not None)
        if self.last_barrier is not None:
            deps.append(self.last_barrier)
        return deps

    def add(self, eng, fn, reads=(), writes=(), extra=()):
        op = Op(eng, fn, self._deps(eng, reads, writes, extra))
        self.ops[eng].append(op)
        for b in reads:
            b.r.append(op)
        for b in writes:
            b.w = op
            b.r = []
        return op

    def dma(self, eng, fn, sem, reads=(), writes=(), extra=(), n=1):
        if eng == "pool" and sem.sw is not None:
            sem = sem.sw
        assert sem.kind == ("sw" if eng == "pool" else "hw"), (eng, sem.kind)
        deps = self._deps(eng, reads, writes, extra)
        if sem.last is not None:
            deps.append(sem.last)
        op = Op(eng, fn, deps, is_dma=True, sem=sem, n=n)
        sem.total += 16 * n
        op.val = sem.total
        sem.last = op
        self.ops[eng].append(op)
        for b in reads:
            b.r.append(op)
        for b in writes:
            b.w = op
            b.r = []
        return op

    def barrier(self, bsem, scratch_src, scratch_dst, skip=()):
        deps = []
        for e in self.ENGS:
            for op in reversed(self.ops[e]):
                if not op.is_dma:
                    deps.append(op)
                    break
        for ds in self.all_dsems:
            if ds.last is not None and ds not in skip:
                deps.append(ds.last)
        self.last_barrier = None
        op = self.dma("sp", lambda e: e.dma_start(out=scratch_dst, in_=scratch_src), bsem, extra=deps)
        self.last_barrier = op
        return op

    def emit(self, block):
        for e in self.ENGS:
            for op in self.ops[e]:
                for d in op.deps:
                    if not d.is_dma:
                        d.needs_inc = True
        for e in self.ENGS:
            c = 0
            for op in self.ops[e]:
                if not op.is_dma and op.needs_inc:
                    c += 1
                    op.val = c
        esem = self.esem

        def run(engname, eng):
            waited = {}
            for op in self.ops[engname]:
                need = {}
                for d in op.deps:
                    s = d.sem.handle if d.is_dma else esem[d.eng]
                    key = id(s)
                    if waited.get(key, 0) < d.val and need.get(key, (None, 0))[1] < d.val:
                        need[key] = (s, d.val)
                for key, (s, v) in need.items():
                    eng.wait_ge(s, v)
                    waited[key] = v
                ins = op.fn(eng)
                if op.is_dma:
                    if not isinstance(ins, (list, tuple)):
                        ins = [ins]
                    assert len(ins) == op.n, (len(ins), op.n)
                    for i_ in ins:
                        i_.then_inc(op.sem.handle, 16)
                elif op.needs_inc:
                    ins.then_inc(esem[engname], 1)

        block.tensor(lambda eng: run("pe", eng))
        block.scalar(lambda eng: run("act", eng))
        block.vector(lambda eng: run("dve", eng))
        block.gpsimd(lambda eng: run("pool", eng))
        block.sync(lambda eng: run("sp", eng))


def _gcols():
    cols = {}
    c = 0
    for i in range(DEPTH):
        for nm in ("ffn1", "mix", "ffn2", "ple"):
            cols[(nm, i)] = c
            c += 8
    cols["final"] = c
    c += 8
    for e in range(2):
        for nm, w in (("aq", 1), ("aq_sw", 1), ("ak", 1), ("ak_sw", 1), ("bq", 2), ("bkv", 1)):
            cols[(nm, e)] = c
            c += w
    for o in range(2):
        cols[("sub", o)] = c
        c += 1
    return cols, c


GCOL, NG = _gcols()

ABX = {"q": 0, "q_sw": 512, "k": 1024, "k_sw": 1152, "cq": 1280, "ckv": 1536, "kr": 1664, "kr_sw": 1696,
       "v": 1728}
ABX_N = 1856


def _swap32(n):
    idx = np.arange(n)
    j = idx % 32
    return np.where(j < 16, idx + 16, idx - 16)


def build_program():
    nc = bass.Bass("TRN2", target_bir_lowering=False)

    def dram(name, shape, dt, kind="Internal"):
        if name in DBG.get("dump", ()):
            kind = "ExternalOutput"
        return nc.dram_tensor(name, list(shape), dt, kind=kind).ap()

    x_in = dram("x", [NT, D], F32, "ExternalInput")
    p_in = dram("p", [DEPTH, NT, 256], F32, "ExternalInput")
    y_out = dram("y", [NT, D], F32, "ExternalOutput")
    wsrc = {}
    wshape = {
        "f1g": (4, D, DFF), "f1u": (4, D, DFF), "f1d": (4, DFF, D),
        "f2g": (4, D, DFF), "f2u": (4, D, DFF), "f2d": (4, DFF, D),
        "abin": (2, D, ABX_N), "wuq": (2, 256, 1024), "wukv": (2, 128, 1024), "about": (2, D, D),
        "cin": (2, D, 3072), "cout": (2, D, D), "pleg": (4, D, D), "plep": (4, 256, D),
    }
    for k, shp in wshape.items():
        wsrc[k] = dram(k, shp, F32, "ExternalInput")
    gall_in = dram("gall", [128, NG], F32, "ExternalInput")
    lamv_in = dram("lamv", [1, 2 * 4 * 64], F32, "ExternalInput")
    ropeA_in = dram("ropeA", [2, 128, NT], F32, "ExternalInput")
    ropeB_in = dram("ropeB", [2, 128, NT], F32, "ExternalInput")
    maskb_in = dram("maskb", [128, 1024], F32, "ExternalInput")
    cdist_in = dram("cdist", [128, 1024], F32, "ExternalInput")
    dtl_in = dram("dtiles", [128, 6, 512], F32, "ExternalInput")

    wt = {}
    for k, (L, K, N) in wshape.items():
        kp = min(K, 128)
        wt[k] = dram("wt_" + k, [L, kp, (K // kp) * N], BF16)
    xs = dram("xs", [8, 128, NT], F32)
    QA = dram("QA", [512, NT], BF16)
    KA = dram("KA", [128, NT], BF16)
    VA = dram("VA", [NT, 130], BF16)
    QB = dram("QB", [8, 96, NT], BF16)
    KB = dram("KB", [8, 64, NT], BF16)
    KRB = dram("KRB", [32, NT], BF16)
    VB = dram("VB", [NT, 520], BF16)
    QC = dram("QC", [1024, NT], BF16)
    KC = dram("KC", [1024, NT], BF16)
    VC = dram("VC", [NT, 1024], BF16)
    OT = dram("OT", [1024, NT], BF16)
    bar_a = dram("bar_a", [1, 64], F32)
    bar_b = dram("bar_b", [1, 64], F32)

    with ExitStack() as top:
        esems = {e: top.enter_context(nc.semaphore("es_" + e)) for e in Prog.ENGS}
        dhandles = [top.enter_context(nc.semaphore("ds%d" % i)) for i in range(60)]
        block = top.enter_context(nc.Block())
        P = Prog(nc, esems, dhandles)
        bsem = P.new_dsem()

        uid = {"n": 0}

        def sb(stack, name, shape, dt, dma=False):
            uid["n"] += 1
            t = stack.enter_context(nc.sbuf_tensor("s%d_%s" % (uid["n"], name), list(shape), dt))
            if not dma:
                return Buf(t, None)
            if dma == "sw":
                return Buf(t, P.new_dsem("sw"))
            d_ = P.new_dsem("hw")
            if dma == "both":
                d_.sw = P.new_dsem("sw")
            return Buf(t, d_)

        def ps(stack, name):
            uid["n"] += 1
            return Buf(stack.enter_context(nc.psum_tensor("p%d_%s" % (uid["n"], name), [128, 512], F32)))

        def ACT(out, in_, func, reads, writes, **kw):
            return P.add("act", lambda e: e.activation(out=out, in_=in_, func=func, **kw), reads, writes)

        def TTo(eng, out, a, b, op, reads, writes):
            return P.add(eng, lambda e: e.tensor_tensor(out, a, b, op), reads, writes)

        def STT(eng, out, in0, scalar, in1, op0, op1, reads, writes):
            return P.add(eng, lambda e: e.scalar_tensor_tensor(out, in0, scalar, in1, op0=op0, op1=op1), reads, writes)

        def TS(eng, out, in0, s1, s2, op0, op1, reads, writes):
            return P.add(eng, lambda e: e.tensor_scalar(out, in0, s1, s2, op0=op0, op1=op1), reads, writes)

        def TSS(eng, out, in0, s, op, reads, writes):
            return P.add(eng, lambda e: e.tensor_single_scalar(out, in0, s, op), reads, writes)

        def CP(eng, out, in_, reads, writes):
            return P.add(eng, lambda e: e.tensor_copy(out, in_), reads, writes)

        def RSTD(dst, src_ps, inv_n):
            ACT(dst[:], src_ps[:], AF.Ln, [src_ps, epsb], [dst], bias=epsb[:, 0:1], scale=inv_n)
            ACT(dst[:], dst[:], AF.Exp, [dst], [dst], scale=-0.5)

        def RCP(out, in_, reads, writes):
            return P.add("dve", lambda e: e.reciprocal(out, in_), reads, writes)

        def MSET(ap, val, writes):
            return P.add("pool", lambda e: e.memset(ap, val), (), writes)

        def DMA(eng, out, in_, sem, reads=(), writes=(), extra=()):
            return P.dma(eng, lambda e: e.dma_start(out=out, in_=in_), sem, reads, writes, extra)

        def DMAN(eng, pairs, sem, reads=(), writes=(), extra=()):
            pairs = list(pairs)
            return P.dma(eng, lambda e: [e.dma_start(out=a, in_=b) for a, b in pairs], sem, reads, writes, extra,
                         n=len(pairs))

        def MM(out_ap, pairs, reads, writes, first=True, last=True):
            pairs = list(pairs)

            def fn(e):
                ins = None
                n = len(pairs)
                for i, (a, b) in enumerate(pairs):
                    ins = e.matmul(out_ap, a, b, start=(first and i == 0), stop=(last and i == n - 1))
                return ins
            return P.add("pe", fn, reads, writes)

        def TRN(pairs, reads, writes):
            pairs = list(pairs)

            def fn(e):
                ins = None
                for o_, i_ in pairs:
                    ins = e.transpose(o_, i_, ident.t[:])
                return ins
            return P.add("pe", fn, list(reads) + [ident], writes)

        gall = sb(top, "gall", [128, NG], F32, dma=True)
        ones_bf = sb(top, "ones_bf", [128, 128], BF16)
        ones_f = sb(top, "ones_f", [128, 128], F32)
        bd64 = sb(top, "bd64", [128, 128], BF16)
        ident = sb(top, "ident", [128, 128], F32)
        nlam = sb(top, "nlam", [128, 2], F32)
        subg = sb(top, "subg", [128, 2], F32)

        epsb = sb(top, "epsb", [128, 1], F32)
        MSET(epsb[:], EPS, [epsb])
        DMA("sp", gall[:], gall_in, gall.dsem, writes=[gall])
        MSET(ones_bf[:], 1.0, [ones_bf])
        MSET(ones_f[:], 1.0, [ones_f])
        MSET(bd64[:], 0.0, [bd64])
        MSET(bd64[0:64, 0:64], 1.0, [bd64])
        MSET(bd64[64:128, 64:128], 1.0, [bd64])
        P.add("pool", lambda e: e.iota(ident[:], pattern=[[1, 128]], base=0, channel_multiplier=-1,
                                       allow_small_or_imprecise_dtypes=True), writes=[ident])
        TSS("dve", ident[:], ident[:], 0.0, ALU.is_equal, [ident], [ident])

        NWS = 8
        wsem = [P.new_dsem("sw") for _ in range(NWS)]
        wi = 0
        order = ["f1g", "f1u", "f1d", "abin", "wuq", "wukv", "about", "cin", "cout", "f2g", "f2u", "f2d",
                 "pleg", "plep"]
        conv_pending = []
        conv_ops = {}
        for l in range(4):
            for k in order:
                L, K, N = wshape[k]
                if l >= L:
                    continue
                kp = min(K, 128)
                src = wsrc[k][l].rearrange("(k p) n -> p k n", p=kp)
                dst = wt[k][l].rearrange("p (k n) -> p k n", n=N)
                nk = K // kp
                step = max(1, nk // 4) if nk >= 8 else nk
                conv_ops[(k, l)] = []
                for k0 in range(0, nk, step):
                    k1 = min(nk, k0 + step)
                    conv_pending.append(((k, l), dst[:, k0:k1, :], src[:, k0:k1, :]))
        conv_state = {"i": 0}

        def issue_conv(n_):
            while n_ > 0 and conv_pending:
                key_, d_, s_ = conv_pending.pop(0)
                op_ = DMA("pool", d_, s_, wsem[conv_state["i"] % NWS])
                conv_state["i"] += 1
                conv_ops[key_].append(op_)
                n_ -= 1

        n_first = 0
        for key_, _, _ in conv_pending:
            if key_[1] == 0 and key_[0] in ("f1g", "f1u", "f1d", "abin", "wuq", "wukv"):
                n_first += 1
            else:
                break
        issue_conv(n_first)

        with ExitStack() as st0:
            lv = sb(st0, "lv", [1, 512], F32, dma=True)
            lp = sb(st0, "lp", [1, 256], F32)
            ls = sb(st0, "ls", [1, 4], F32)
            le = sb(st0, "le", [1, 4], F32)
            ln2 = sb(st0, "ln2", [1, 2], F32)
            pst = ps(st0, "ps_pro")
            DMA("sp", lv[:], lamv_in, lv.dsem, writes=[lv])
            lvv = lv.t[:].rearrange("p (o f d) -> p o f d", o=2, f=4)
            lpv = lp.t[:].rearrange("p (o f d) -> p o f d", o=2, f=2)
            for o in range(2):
                for f in range(2):
                    TTo("dve", lpv[:, o, f, :], lvv[:, o, 2 * f, :], lvv[:, o, 2 * f + 1, :], ALU.mult, [lv], [lp])
            lp3 = lp.t[:].rearrange("p (g d) -> p g d", d=64)
            P.add("dve", lambda e: e.reduce_sum(ls[:, 0:4], lp3, axis=mybir.AxisListType.X), reads=[lp], writes=[ls])
            ACT(le[:], ls[:], AF.Exp, [ls], [le])
            lev = le.t[:].rearrange("p (o f) -> p o f", f=2)
            for o in range(2):
                li = 0.8 - 0.6 * math.exp(-0.3 * (2 * o + 1))
                STT("dve", ln2[:, o:o + 1], lev[:, o, 1:2], -li, lev[:, o, 0:1], ALU.add, ALU.subtract, [le], [ln2])
            MM(pst.t[:, 0:2], [(ones_f[0:1, :], ln2[0:1, :])], [ln2, ones_f], [pst])
            CP("dve", nlam[:], pst.t[:, 0:2], [pst], [nlam])
            for o in range(2):
                li = 0.8 - 0.6 * math.exp(-0.3 * (2 * o + 1))
                c0 = GCOL[("sub", o)]
                TSS("dve", subg[:, o:o + 1], gall[:, c0:c0 + 1], 1.0 - li, ALU.mult, [gall], [subg])
            P.barrier(bsem, gall_in[0:1, 0:64], bar_b, skip=wsem)

        def rowlocal_pass(ipass):
            with ExitStack() as st:
                RING = 5
                ring = [sb(st, "ring%d" % i, [128, 5632], BF16, dma=True) for i in range(RING)]
                xbs = [sb(st, "xb%d" % i, [128, 8, TT], F32, dma="both") for i in range(2)]
                xb = xbs[0]
                hn = sb(st, "hn", [128, 8, TT], BF16)
                act = sb(st, "act", [128, 22, TT], BF16)
                sq = sb(st, "sq", [128, 8, TT], BF16)
                rstd = sb(st, "rstd", [128, TT], F32)
                tmpf = [sb(st, "tmpf%d" % i, [128, TT], F32) for i in range(6)]
                tab = [sb(st, "tab%d" % i, [128, TT], F32, dma=True) for i in range(4)]
                otb = sb(st, "otb", [128, 8, TT], BF16, dma=True)
                xin = [sb(st, "xin%d" % i, [128, D], F32, dma="both") for i in range(2)]
                ptk = sb(st, "ptk", [128, 4, 256], F32, dma=True)
                pT = sb(st, "pT", [128, 2, TT], BF16)
                NSTG = 8
                stg = [sb(st, "stg%d" % i, [128, TT], BF16, dma="sw") for i in range(NSTG)]
                vstg = [sb(st, "vstg%d" % i, [128, 520], BF16, dma="sw") for i in range(2)]
                cqn_b = [sb(st, "cqn%d" % i, [128, TT], BF16) for i in range(2)]
                ckvn_b = sb(st, "ckvn", [128, TT], BF16)
                pG = [ps(st, "pG%d" % i) for i in range(2)]
                pU = [ps(st, "pU%d" % i) for i in range(2)]
                pD = [ps(st, "pD%d" % i) for i in range(2)]
                pN = ps(st, "pN")
                pTp = ps(st, "pTp")
                cnt = {"ring": 0, "stg": 0, "vstg": 0, "G": 0, "D": 0, "tmp": 0, "xin": 0}

                for v in vstg:
                    MSET(v[:], 1.0, [v])

                def slab(wname, l, K, N, c0, ncols):
                    b = ring[cnt["ring"] % RING]
                    cnt["ring"] += 1
                    kp = min(K, 128)
                    nk = K // kp
                    src = wt[wname][l].rearrange("p (k n) -> p k n", n=N)[:, :, c0:c0 + ncols]
                    view = b.t[0:kp, 0:nk * ncols].rearrange("p (k n) -> p k n", n=ncols)
                    assert conv_ops[(wname, l)], (wname, l)
                    DMA("sp", view, src, b.dsem, writes=[b], extra=conv_ops[(wname, l)])
                    return b, view

                def next_tmp():
                    t = tmpf[cnt["tmp"] % len(tmpf)]
                    cnt["tmp"] += 1
                    return t

                def next_stg():
                    s_ = stg[cnt["stg"] % NSTG]
                    cnt["stg"] += 1
                    return s_

                def next_D():
                    d_ = pD[cnt["D"] % 2]
                    cnt["D"] += 1
                    return d_

                def rstd_from(ps_buf, inv_n):
                    RSTD(rstd, ps_buf, inv_n)

                def rmsnorm_stats(src_aps, src_bufs, inv_n):
                    n = len(src_aps)
                    for c, (ap, bb) in enumerate(zip(src_aps, src_bufs)):
                        if c % 2 == 0:
                            ACT(sq[:, c, :], ap, AF.Square, [bb], [sq])
                        else:
                            TTo("dve", sq[:, c, :], ap, ap, ALU.mult, [bb], [sq])
                    MM(pN[:], [(ones_bf[:], sq[:, c, :]) for c in range(n)], [sq, ones_bf], [pN])
                    rstd_from(pN, inv_n)

                def norm_x(gcol):
                    rmsnorm_stats([xb[:, c, :] for c in range(8)], [xb] * 8, 1.0 / D)
                    for c in range(8):
                        STT("dve", hn[:, c, :], xb[:, c, :], gall[:, gcol + c:gcol + c + 1], rstd[:], ALU.mult, ALU.mult,
                            [xb, rstd, gall], [hn])

                def ffn(wg, wu, wd, l, gcol):
                    norm_x(gcol)
                    for s0 in range(0, DFF, 512):
                        ncols = min(512, DFF - s0)
                        bg, vg = slab(wg, l, D, DFF, s0, ncols)
                        bu, vu = slab(wu, l, D, DFF, s0, ncols)
                        for j in range(ncols // 128):
                            f = (s0 // 128) + j
                            G = pG[cnt["G"] % 2]
                            U = pU[cnt["G"] % 2]
                            cnt["G"] += 1
                            MM(G[:], [(vg[:, k, j * 128:(j + 1) * 128], hn[:, k, :]) for k in range(8)], [bg, hn], [G])
                            MM(U[:], [(vu[:, k, j * 128:(j + 1) * 128], hn[:, k, :]) for k in range(8)], [bu, hn], [U])
                            t = next_tmp()
                            ACT(t[:], G[:], AF.Silu, [G], [t])
                            TTo("dve", act[:, f, :], t[:], U[:], ALU.mult, [t, U], [act])
                    for m0 in range(0, D, 256):
                        bd_, vd = slab(wd, l, DFF, D, m0, 256)
                        for j in range(2):
                            m = m0 // 128 + j
                            Dp = next_D()
                            MM(Dp[:], [(vd[:, k, j * 128:(j + 1) * 128], act[:, k, :]) for k in range(22)], [bd_, act], [Dp])
                            STT("dve", xb[:, m, :], Dp[:], 0.5, xb[:, m, :], ALU.mult, ALU.add, [Dp, xb], [xb])

                def proj(wname, l, K, N, c0, ncols, rhs_aps, rhs_bufs, consume):
                    nk = len(rhs_aps)
                    done = 0
                    while done < ncols:
                        sc = min(512, ncols - done)
                        b, v = slab(wname, l, K, N, c0 + done, sc)
                        for j in range((sc + 127) // 128):
                            mcols = min(128, sc - j * 128)
                            Dp = next_D()
                            MM(Dp.t[0:mcols, :], [(v[:, k, j * 128:j * 128 + mcols], rhs_aps[k]) for k in range(nk)],
                               [b] + list(rhs_bufs), [Dp])
                            consume((done // 128) + j, Dp, mcols)
                        done += sc

                def proj_tok(wname, l, K, N, c0, ncols, lhs_fn, lhs_bufs, nk, consume):
                    b, v = slab(wname, l, K, N, c0, ncols)
                    for s in range(4):
                        Dp = next_D()
                        MM(Dp.t[:, 0:ncols], [(lhs_fn(k, s), v[:, k, :]) for k in range(nk)], [b] + list(lhs_bufs), [Dp])
                        consume(s, Dp)

                def evac_to(dst_buf, rows=128):
                    def cons(j, Dp, mcols):
                        ACT(dst_buf.t[0:rows, :], Dp.t[0:rows, :], AF.Identity, [Dp], [dst_buf])
                    return cons

                hn_aps = [hn[:, k, :] for k in range(8)]

                for ti in range(DBG.get("ntile", NTILE)):
                    t0 = ti * TT
                    tsl = slice(t0, t0 + TT)
                    xb = xbs[ti % 2]
                    if ipass == 0:
                        for s in range(4):
                            xi = xin[cnt["xin"] % 2]
                            cnt["xin"] += 1
                            DMA("sp", xi[:], x_in[t0 + s * 128:t0 + (s + 1) * 128, :], xi.dsem, writes=[xi])
                            for c0 in range(0, 8, 4):
                                TRN([(pTp.t[:, cc * 128:(cc + 1) * 128], xi.t[:, (c0 + cc) * 128:(c0 + cc + 1) * 128])
                                     for cc in range(4)], [xi], [pTp])
                                CP("dve", xb.t[:, c0:c0 + 4, s * 128:(s + 1) * 128],
                                   pTp.t[:].rearrange("p (c t) -> p c t", t=128), [pTp], [xb])
                    else:
                        DMA("sp", xb[:], xs[:, :, tsl].rearrange("c p t -> p c t"), xb.dsem, writes=[xb])

                    if ipass > 0:
                        lp_ = ipass - 1
                        DMA("sp", otb[:], OT[:, tsl].rearrange("(c p) t -> p c t", p=128), otb.dsem, writes=[otb])
                        wo = "about" if lp_ % 2 == 0 else "cout"

                        def cons_out(m, Dp, mcols):
                            TTo("dve", xb[:, m, :], Dp[:], xb[:, m, :], ALU.add, [Dp, xb], [xb])
                        proj(wo, lp_ // 2, D, D, 0, D, [otb[:, k, :] for k in range(8)], [otb], cons_out)
                        ffn("f2g", "f2u", "f2d", lp_, GCOL[("ffn2", lp_)])
                        DMA("sp", ptk[:], p_in[lp_, t0:t0 + TT, :].rearrange("(s p) d -> p s d", p=128), ptk.dsem,
                            writes=[ptk])
                        for dc in range(2):
                            TRN([(pTp.t[:, s * 128:(s + 1) * 128], ptk.t[:, s, dc * 128:(dc + 1) * 128]) for s in range(4)],
                                [ptk], [pTp])
                            CP("dve", pT[:, dc, :], pTp[:], [pTp], [pT])
                        norm_x(GCOL[("ple", lp_)])
                        for m0 in range(0, D, 512):
                            bg_, vg_ = slab("pleg", lp_, D, D, m0, 512)
                            bp_, vp_ = slab("plep", lp_, 256, D, m0, 512)
                            for j in range(4):
                                m = m0 // 128 + j
                                G = pG[cnt["G"] % 2]
                                U = pU[cnt["G"] % 2]
                                cnt["G"] += 1
                                MM(G[:], [(vg_[:, k, j * 128:(j + 1) * 128], hn[:, k, :]) for k in range(8)], [bg_, hn], [G])
                                MM(U[:], [(vp_[:, k, j * 128:(j + 1) * 128], pT[:, k, :]) for k in range(2)], [bp_, pT], [U])
                                t = next_tmp()
                                ACT(t[:], G[:], AF.Sigmoid, [G], [t])
                                t2 = next_tmp()
                                TTo("dve", t2[:], t[:], U[:], ALU.mult, [t, U], [t2])
                                TTo("dve", xb[:, m, :], t2[:], xb[:, m, :], ALU.add, [t2, xb], [xb])

                    if ipass < DEPTH:
                        li = ipass
                        ffn("f1g", "f1u", "f1d", li, GCOL[("ffn1", li)])
                        norm_x(GCOL[("mix", li)])
                        if li % 2 == 0:
                            e_ = li // 2
                            for i_, (src_, row) in enumerate(((ropeA_in, 0), (ropeA_in, 1), (ropeB_in, 0), (ropeB_in, 1))):
                                DMA("sp", tab[i_][:], src_[row, :, tsl], tab[i_].dsem, writes=[tab[i_]])
                            for (nm, nch, gq, dst) in (("q", 4, "aq", QA), ("k", 1, "ak", KA)):
                                for c in range(nch):
                                    z = next_tmp()
                                    zw = next_tmp()
                                    proj("abin", e_, D, ABX_N, ABX[nm] + c * 128, 128, hn_aps, [hn], evac_to(z))
                                    proj("abin", e_, D, ABX_N, ABX[nm + "_sw"] + c * 128, 128, hn_aps, [hn], evac_to(zw))
                                    ACT(sq[:, 0, :], z[:], AF.Square, [z], [sq])
                                    MM(pN[:], [(bd64[:], sq[:, 0, :])], [sq, bd64], [pN])
                                    rstd_from(pN, 1.0 / 64)
                                    g0 = GCOL[(gq, e_)]
                                    g1 = GCOL[(gq + "_sw", e_)]
                                    STT("dve", z[:], z[:], gall[:, g0:g0 + 1], tab[0][:], ALU.mult, ALU.mult,
                                        [z, gall, tab[0]], [z])
                                    STT("dve", zw[:], zw[:], gall[:, g1:g1 + 1], tab[1][:], ALU.mult, ALU.mult,
                                        [zw, gall, tab[1]], [zw])
                                    TTo("dve", z[:], z[:], zw[:], ALU.add, [z, zw], [z])
                                    s_ = next_stg()
                                    TTo("dve", s_[:], z[:], rstd[:], ALU.mult, [z, rstd], [s_])
                                    DMA("pool", dst[c * 128:(c + 1) * 128, tsl], s_[:], s_.dsem, reads=[s_])
                            vs = vstg[cnt["vstg"] % 2]
                            cnt["vstg"] += 1

                            def consVA(s, Dp, vs=vs, t0=t0):
                                CP("dve", vs.t[:, 0:130].rearrange("p (g d) -> p g d", d=65)[:, :, 0:64],
                                   Dp.t[:, 0:128].rearrange("p (g d) -> p g d", d=64), [Dp], [vs])
                                DMA("pool", VA[t0 + s * 128:t0 + (s + 1) * 128, :], vs.t[:, 0:130], vs.dsem, reads=[vs])
                            proj_tok("abin", e_, D, ABX_N, ABX["v"], 128,
                                     lambda k, s: hn[:, k, s * 128:(s + 1) * 128], [hn], 8, consVA)
                            cq = [next_tmp(), next_tmp()]
                            for c in range(2):
                                proj("abin", e_, D, ABX_N, ABX["cq"] + c * 128, 128, hn_aps, [hn], evac_to(cq[c]))
                            rmsnorm_stats([cq[0][:], cq[1][:]], cq, 1.0 / 256)
                            cqn = cqn_b
                            for c in range(2):
                                gq_ = GCOL[("bq", e_)] + c
                                STT("dve", cqn[c][:], cq[c][:], gall[:, gq_:gq_ + 1], rstd[:], ALU.mult, ALU.mult,
                                    [cq[c], gall, rstd], [cqn[c]])
                            ckv = next_tmp()
                            proj("abin", e_, D, ABX_N, ABX["ckv"], 128, hn_aps, [hn], evac_to(ckv))
                            rmsnorm_stats([ckv[:]], [ckv], 1.0 / 128)
                            ckvn = ckvn_b
                            gk_ = GCOL[("bkv", e_)]
                            STT("dve", ckvn[:], ckv[:], gall[:, gk_:gk_ + 1], rstd[:], ALU.mult, ALU.mult,
                                [ckv, gall, rstd], [ckvn])
                            kr = next_tmp()
                            krw = next_tmp()
                            proj("abin", e_, D, ABX_N, ABX["kr"], 32, hn_aps, [hn], evac_to(kr, 32))
                            proj("abin", e_, D, ABX_N, ABX["kr_sw"], 32, hn_aps, [hn], evac_to(krw, 32))
                            TTo("dve", kr[0:32, :], kr[0:32, :], tab[2][0:32, :], ALU.mult, [kr, tab[2]], [kr])
                            TTo("dve", krw[0:32, :], krw[0:32, :], tab[3][0:32, :], ALU.mult, [krw, tab[3]], [krw])
                            s_ = next_stg()
                            TTo("dve", s_[0:32, :], kr[0:32, :], krw[0:32, :], ALU.add, [kr, krw], [s_])
                            DMA("pool", KRB[:, tsl], s_[0:32, :], s_.dsem, reads=[s_])

                            def cons_qn(j, Dp, mcols, tsl=tsl):
                                s_ = next_stg()
                                CP("dve", s_[:], Dp[:], [Dp], [s_])
                                DMAN("pool", [(QB[2 * j + hh_, 0:64, tsl], s_.t[hh_ * 64:(hh_ + 1) * 64, :]) for hh_ in range(2)],
                                     s_.dsem, reads=[s_])
                            proj("wuq", e_, 256, 1024, 0, 512, [cqn[0][:], cqn[1][:]], cqn, cons_qn)
                            for c in range(2):
                                z = next_tmp()
                                zw = next_tmp()
                                proj("wuq", e_, 256, 1024, 512 + c * 128, 128, [cqn[0][:], cqn[1][:]], cqn, evac_to(z))
                                proj("wuq", e_, 256, 1024, 768 + c * 128, 128, [cqn[0][:], cqn[1][:]], cqn, evac_to(zw))
                                TTo("dve", z[:], z[:], tab[2][:], ALU.mult, [z, tab[2]], [z])
                                TTo("dve", zw[:], zw[:], tab[3][:], ALU.mult, [zw, tab[3]], [zw])
                                s_ = next_stg()
                                TTo("dve", s_[:], z[:], zw[:], ALU.add, [z, zw], [s_])
                                DMAN("pool", [(QB[4 * c + hh_, 64:96, tsl], s_.t[hh_ * 32:(hh_ + 1) * 32, :]) for hh_ in range(4)],
                                     s_.dsem, reads=[s_])

                            def cons_kn(j, Dp, mcols, tsl=tsl):
                                s_ = next_stg()
                                CP("dve", s_[:], Dp[:], [Dp], [s_])
                                DMAN("pool", [(KB[2 * j + hh_, :, tsl], s_.t[hh_ * 64:(hh_ + 1) * 64, :]) for hh_ in range(2)],
                                     s_.dsem, reads=[s_])
                            proj("wukv", e_, 128, 1024, 0, 512, [ckvn[:]], [ckvn], cons_kn)
                            vs = vstg[cnt["vstg"] % 2]
                            cnt["vstg"] += 1

                            def consVB(s, Dp, vs=vs, t0=t0):
                                CP("dve", vs.t[:, 0:520].rearrange("p (g d) -> p g d", d=65)[:, :, 0:64],
                                   Dp.t[:, 0:512].rearrange("p (g d) -> p g d", d=64), [Dp], [vs])
                                DMA("pool", VB[t0 + s * 128:t0 + (s + 1) * 128, :], vs.t[:, 0:520], vs.dsem, reads=[vs])
                            proj_tok("wukv", e_, 128, 1024, 512, 512,
                                     lambda k, s: ckvn[:, s * 128:(s + 1) * 128], [ckvn], 1, consVB)
                        else:
                            o_ = li // 2
                            for which, dst in ((0, QC), (1024, KC)):
                                def cons_qk(j, Dp, mcols, dst=dst, tsl=tsl):
                                    s_ = next_stg()
                                    CP("dve", s_[:], Dp[:], [Dp], [s_])
                                    DMA("pool", dst[j * 128:(j + 1) * 128, tsl], s_[:], s_.dsem, reads=[s_])
                                proj("cin", o_, D, 3072, which, 1024, hn_aps, [hn], cons_qk)
                            for half in range(2):
                                def consVC(s, Dp, half=half, t0=t0):
                                    s_ = next_stg()
                                    CP("dve", s_[:], Dp[:], [Dp], [s_])
                                    DMA("pool", VC[t0 + s * 128:t0 + (s + 1) * 128, half * 512:(half + 1) * 512], s_[:],
                                        s_.dsem, reads=[s_])
                                proj_tok("cin", o_, D, 3072, 2048 + half * 512, 512,
                                         lambda k, s: hn[:, k, s * 128:(s + 1) * 128], [hn], 8, consVC)
                        DMA("pool", xs[:, :, tsl].rearrange("c p t -> p c t"), xb[:], xb.dsem, reads=[xb])
                    else:
                        rmsnorm_stats([xb[:, c, :] for c in range(8)], [xb] * 8, 1.0 / D)
                        gcol = GCOL["final"]
                        for c in range(8):
                            STT("dve", xb[:, c, :], xb[:, c, :], gall[:, gcol + c:gcol + c + 1], rstd[:], ALU.mult, ALU.mult,
                                [xb, rstd, gall], [xb])
                        for s in range(4):
                            xi = xin[cnt["xin"] % 2]
                            cnt["xin"] += 1
                            for c0 in range(0, 8, 4):
                                TRN([(pTp.t[:, cc * 128:(cc + 1) * 128], xb.t[:, c0 + cc, s * 128:(s + 1) * 128])
                                     for cc in range(4)], [xb], [pTp])
                                CP("dve", xi.t[:, c0 * 128:(c0 + 4) * 128], pTp[:], [pTp], [xi])
                            DMA("pool", y_out[t0 + s * 128:t0 + (s + 1) * 128, :], xi[:], xi.dsem, reads=[xi])
                P.barrier(bsem, gall_in[0:1, 0:64], bar_b)
                P.release_dsems([b.dsem for b in ring + xbs + [otb, ptk] + tab + xin + stg + vstg])

        def ps2(stack, name):
            uid["n"] += 1
            return Buf(stack.enter_context(nc.psum_tensor("p%d_%s" % (uid["n"], name), [128, 1024], F32)))

        def MMS(specs, reads, writes):
            specs = list(specs)

            def fn(e):
                ins = None
                for (o_, a_, b_, s0_, s1_) in specs:
                    ins = e.matmul(o_, a_, b_, start=s0_, stop=s1_)
                return ins
            return P.add("pe", fn, reads, writes)

        def attn_even(e_):
            with ExitStack() as st:
                S2 = [ps2(st, "S2_%d" % i) for i in range(2)]
                O = [ps(st, "O%d" % i) for i in range(2)]
                Bp = ps(st, "Bp")
                Kt = [sb(st, "Kt%d" % i, [128, NT], BF16, dma=True) for i in range(2)]
                Qt = [sb(st, "Qt%d" % i, [128, NT], BF16, dma=True) for i in range(2)]
                Vt = sb(st, "Vt", [128, 64, 584], BF16, dma=True)
                maskb = sb(st, "maskb", [128, 1024], F32, dma=True)
                NP_ = 3
                Pt = [sb(st, "Pt%d" % i, [128, 2 * TT], BF16) for i in range(NP_)]
                rl = [sb(st, "rl%d" % i, [128, TT], F32) for i in range(2)]
                osb = [sb(st, "osb%d" % i, [128, TT], F32) for i in range(2)]
                ostg = [sb(st, "ostg%d" % i, [128, TT], BF16, dma="sw") for i in range(4)]
                DMA("sp", maskb[:], maskb_in, maskb.dsem, writes=[maskb])
                for b_ in Kt + Qt + [Vt]:
                    MSET(b_[:], 0.0, [b_])
                qcount = 0
                ecount = 0
                for mixer in ("A", "B"):
                    if mixer == "A":
                        Kd, scale, vw, Vsrc = 128, 64 ** -0.5, 130, VA
                    else:
                        Kd, scale, vw, Vsrc = 96, 96 ** -0.5, 520, VB
                    vsrc = Vsrc.rearrange("(t p) c -> p t c", p=128)
                    DMAN("sp", [(Vt.t[:, 16 * i:16 * (i + 1), 0:vw], vsrc[:, 16 * i:16 * (i + 1), :]) for i in range(4)],
                         Vt.dsem, writes=[Vt])
                    qbase = qcount
                    qcount += 8

                    def Kbuf(h):
                        return Kt[(h // 4) % 2] if mixer == "A" else Kt[h % 2]

                    def Qbuf(h):
                        return Qt[(qbase + h) % 2]

                    def load_head(h):
                        q_ = Qbuf(h)
                        k_ = Kbuf(h)
                        if mixer == "A":
                            g = h // 4
                            if h % 4 == 0:
                                DMA("sp", k_.t[0:64, :], KA[g * 64:(g + 1) * 64, :], k_.dsem, writes=[k_])
                            DMA("sp", q_.t[0:64, :], QA[h * 64:(h + 1) * 64, :], q_.dsem, writes=[q_])
                        else:
                            DMAN("sp", [(k_.t[0:64, :], KB[h, :, :]), (k_.t[64:96, :], KRB[:, :])], k_.dsem, writes=[k_])
                            DMA("sp", q_.t[0:96, :], QB[h, :, :], q_.dsem, writes=[q_])

                    LA = 2
                    n = 8 * 8 * 64
                    load_head(0)
                    for i in range(n + LA):
                        if i % 512 == LA and (i // 512) + 1 < 8:
                            load_head(i // 512 + 1)
                        if i < n:
                            h, qp, t = i // 512, (i % 512) // 64, i % 64
                            K_, Q_ = Kbuf(h), Qbuf(h)
                            Sb = S2[i % 2]
                            kt_ = K_.t[0:Kd, t * 128:(t + 1) * 128]
                            MMS([(Sb.t[:, 0:TT], kt_, Q_.t[0:Kd, (2 * qp) * TT:(2 * qp + 1) * TT], True, True),
                                 (Sb.t[:, TT:2 * TT], kt_, Q_.t[0:Kd, (2 * qp + 1) * TT:(2 * qp + 2) * TT], True, True)],
                                [K_, Q_], [Sb])
                            pt = Pt[i % NP_]
                            blk = (2 * qp) * 64 + t
                            ACT(pt[:], Sb[:], AF.Exp, [Sb, maskb], [pt], bias=maskb[:, blk:blk + 1], scale=scale)
                        j = i - LA
                        if j >= 0:
                            h, qp, t = j // 512, (j % 512) // 64, j % 64
                            vcol = (h // 4) * 65 if mixer == "A" else h * 65
                            orow = (0 if mixer == "A" else 512) + h * 64
                            pt = Pt[j % NP_]
                            vt_ = Vt.t[:, t, vcol:vcol + 128]
                            MMS([(O[0].t[:, :], vt_, pt.t[:, 0:TT], t == 0, t == 63),
                                 (O[1].t[:, :], vt_, pt.t[:, TT:2 * TT], t == 0, t == 63)], [pt, Vt], [O[0], O[1]])
                            if t == 63:
                                issue_conv(2)
                                for k2 in range(2):
                                    CP("dve", osb[k2][0:65, :], O[k2].t[0:65, :], [O[k2]], [osb[k2]])
                                for k2 in range(2):
                                    RCP(rl[k2][64:65, :], osb[k2][64:65, :], [osb[k2]], [rl[k2]])
                                for k2 in range(2):
                                    MM(Bp.t[0:64, :], [(ones_f[64:65, 0:64], rl[k2][64:65, :])], [rl[k2], ones_f], [Bp])
                                    og = ostg[ecount % 4]
                                    ecount += 1
                                    TTo("dve", og[0:64, :], osb[k2][0:64, :], Bp.t[0:64, :], ALU.mult, [osb[k2], Bp], [og])
                                    qb = 2 * qp + k2
                                    DMA("pool", OT[orow:orow + 64, qb * TT:(qb + 1) * TT], og[0:64, :], og.dsem, reads=[og])
                issue_conv(len(conv_pending))
                P.barrier(bsem, gall_in[0:1, 0:64], bar_b)
                P.release_dsems([b.dsem for b in Kt + Qt + [Vt, maskb] + ostg])

        ALIBI_THR = 36.0

        def attn_odd(o_):
            with ExitStack() as st:
                S = [ps(st, "S%d" % i) for i in range(4)]
                O1 = ps(st, "O1")
                O2 = ps(st, "O2")
                L1 = ps(st, "L1")
                L2 = ps(st, "L2")
                Kt = [sb(st, "Kt%d" % i, [128, NT], BF16, dma=True) for i in range(2)]
                Qt = [sb(st, "Qt%d" % i, [128, NT], BF16, dma=True) for i in range(2)]
                Vt = [sb(st, "Vt%d" % i, [128, 64, 128], BF16, dma=True) for i in range(2)]
                maskb = sb(st, "maskb", [128, 1024], F32, dma=True)
                cdist = sb(st, "cdist", [128, 1024], F32, dma=True)
                dtl = sb(st, "dtl", [128, 6, 512], F32, dma=True)
                biasC = [sb(st, "biasC%d" % i, [128, 1024], F32) for i in range(2)]
                NP_ = 4
                Pt = [sb(st, "Pt%d" % i, [128, 2 * TT], BF16) for i in range(NP_)]
                tm = [sb(st, "tm%d" % i, [128, 2 * TT], F32) for i in range(NP_)]
                r1 = sb(st, "r1", [128, TT], F32)
                r2 = sb(st, "r2", [128, TT], F32)
                o1 = sb(st, "o1", [128, TT], F32)
                o2 = sb(st, "o2", [128, TT], F32)
                sqo = sb(st, "sqo", [128, TT], BF16)
                rs = sb(st, "rs", [128, TT], F32)
                ostg = [sb(st, "ostg%d" % i, [128, TT], BF16, dma="sw") for i in range(2)]
                DMA("sp", maskb[:], maskb_in, maskb.dsem, writes=[maskb])
                DMA("sp", cdist[:], cdist_in, cdist.dsem, writes=[cdist])
                DMA("sp", dtl[:], dtl_in, dtl.dsem, writes=[dtl])
                scale = 64 ** -0.5

                def load_head(h):
                    slope = 2.0 ** (-(h + 1))
                    K_, Q_, V_, bC = Kt[h % 2], Qt[h % 2], Vt[h % 2], biasC[h % 2]
                    DMA("sp", K_[:], KC[h * 128:(h + 1) * 128, :], K_.dsem, writes=[K_])
                    DMA("sp", Q_[:], QC[h * 128:(h + 1) * 128, :], Q_.dsem, writes=[Q_])
                    vsrc = VC[:, h * 128:(h + 1) * 128].rearrange("(t p) c -> p t c", p=128)
                    DMAN("sp", [(V_.t[:, 16 * i:16 * (i + 1), :], vsrc[:, 16 * i:16 * (i + 1), :]) for i in range(4)],
                         V_.dsem, writes=[V_])
                    STT("dve", bC[:], cdist[:], slope, maskb[:], ALU.mult, ALU.add, [cdist, maskb], [bC])

                items = []
                head_start = {}
                for h in range(8):
                    slope = 2.0 ** (-(h + 1))
                    head_start[len(items)] = h
                    for qb in range(16):
                        q_lo, q_hi = qb * TT, qb * TT + TT - 1
                        keep = []
                        for t in range(64):
                            s_lo, s_hi = t * 128, t * 128 + 127
                            mind = max(0, s_lo - q_hi, q_lo - s_hi)
                            if slope * mind < ALIBI_THR:
                                keep.append(t)
                        for t in keep:
                            items.append((h, qb, t, t == keep[0], t == keep[-1]))
                LA = 2
                n = len(items)
                load_head(0)
                ecount = 0
                for i in range(n + LA):
                    if (i - LA) in head_start and head_start[i - LA] + 1 < 8:
                        load_head(head_start[i - LA] + 1)
                    if i < n:
                        h, qb, t, fs, ls_ = items[i]
                        slope = 2.0 ** (-(h + 1))
                        K_, Q_, bC = Kt[h % 2], Qt[h % 2], biasC[h % 2]
                        Sa = S[(i % 2) * 2]
                        Sb = S[(i % 2) * 2 + 1]
                        MM(Sa[:], [(K_.t[0:64, t * 128:(t + 1) * 128], Q_.t[0:64, qb * TT:(qb + 1) * TT])], [K_, Q_], [Sa])
                        MM(Sb[:], [(K_.t[64:128, t * 128:(t + 1) * 128], Q_.t[64:128, qb * TT:(qb + 1) * TT])], [K_, Q_], [Sb])
                        if t < 4 * qb:
                            di = 0
                        elif t > 4 * qb + 3:
                            di = 1
                        else:
                            di = 2 + (t - 4 * qb)
                        tmb = tm[i % NP_]
                        fac = slope / scale
                        STT("dve", tmb[:, 0:TT], dtl[:, di, :], fac, Sa[:], ALU.mult, ALU.add, [Sa, dtl], [tmb])
                        STT("dve", tmb[:, TT:2 * TT], dtl[:, di, :], fac, Sb[:], ALU.mult, ALU.add, [Sb, dtl], [tmb])
                        pt = Pt[i % NP_]
                        blk = qb * 64 + t
                        ACT(pt[:], tmb[:], AF.Exp, [tmb, bC], [pt], bias=bC[:, blk:blk + 1], scale=scale)
                    j = i - LA
                    if j >= 0:
                        h, qb, t, fs, ls_ = items[j]
                        V_ = Vt[h % 2]
                        pt = Pt[j % NP_]
                        MMS([(O1.t[:, :], V_.t[:, t, :], pt.t[:, 0:TT], fs, ls_),
                             (L1.t[:, :], ones_bf.t[:, :], pt.t[:, 0:TT], fs, ls_),
                             (O2.t[:, :], V_.t[:, t, :], pt.t[:, TT:2 * TT], fs, ls_),
                             (L2.t[:, :], ones_bf.t[:, :], pt.t[:, TT:2 * TT], fs, ls_)],
                            [pt, V_, ones_bf], [O1, L1, O2, L2])
                        if ls_:
                            ACT(r1[:], L1[:], AF.Ln, [L1], [r1])
                            ACT(r2[:], L2[:], AF.Ln, [L2], [r2])
                            ACT(r1[:], r1[:], AF.Exp, [r1], [r1], scale=-1.0)
                            ACT(r2[:], r2[:], AF.Exp, [r2], [r2], scale=-1.0)
                            TTo("dve", o1[:], O1[:], r1[:], ALU.mult, [O1, r1], [o1])
                            TTo("dve", o2[:], O2[:], r2[:], ALU.mult, [O2, r2], [o2])
                            STT("dve", o1[:], o2[:], nlam[:, o_:o_ + 1], o1[:], ALU.mult, ALU.add, [o1, o2, nlam], [o1])
                            ACT(sqo[:], o1[:], AF.Square, [o1], [sqo])
                            MM(L1[:], [(ones_bf[:], sqo[:])], [sqo, ones_bf], [L1])
                            RSTD(rs, L1, 1.0 / 128)
                            og = ostg[ecount % 2]
                            ecount += 1
                            STT("dve", og[:], o1[:], subg[:, o_:o_ + 1], rs[:], ALU.mult, ALU.mult, [o1, subg, rs], [og])
                            DMA("pool", OT[h * 128:(h + 1) * 128, qb * TT:(qb + 1) * TT], og[:], og.dsem, reads=[og])
                P.barrier(bsem, gall_in[0:1, 0:64], bar_b)
                P.release_dsems([b.dsem for b in Kt + Qt + Vt + [maskb, cdist, dtl] + ostg])

        for ipass in range(DBG["passes"]):
            rowlocal_pass(ipass)
            if ipass < DEPTH and DBG["attn"]:
                if ipass % 2 == 0:
                    attn_even(ipass // 2)
                else:
                    attn_odd(ipass // 2)
        P.add("sp", lambda e: e.nop(), extra=[P.last_barrier])
        P.emit(block)
    return nc


def _tables(seq_len):
    t = np.arange(NT)
    tl = t % seq_len
    row, col = tl // 64, tl % 64
    inv = (10000.0 ** (-np.arange(16, dtype=np.float32) * (2.0 / 32))).astype(np.float32)

    def cs(pos, d_idx):
        jj = d_idx % 32
        i = jj % 16
        ang = pos[None, :].astype(np.float32) * inv[i][:, None]
        c = np.cos(ang).astype(np.float32)
        s = np.sin(ang).astype(np.float32)
        s = np.where((jj < 16)[:, None], -s, s)
        return c, s
    d = np.arange(128)
    j = d % 64
    cA = np.zeros((128, NT), np.float32)
    sA = np.zeros((128, NT), np.float32)
    m_row = j < 32
    c1, s1 = cs(row, d)
    c2, s2 = cs(col, d)
    cA[m_row], sA[m_row] = c1[m_row], s1[m_row]
    cA[~m_row], sA[~m_row] = c2[~m_row], s2[~m_row]
    cB, sB = cs(tl, d)
    ropeA = np.stack([cA, sA]).astype(np.float32)
    ropeB = np.stack([cB, sB]).astype(np.float32)
    qb = np.arange(16)[:, None]
    tt = np.arange(64)[None, :]
    q0 = qb * 512
    s0 = tt * 128
    same = (q0 // seq_len) == (s0 // seq_len)
    maskb = np.where(same, 0.0, NEG).astype(np.float32).reshape(1, 1024)
    diag = (tt >= 4 * qb) & (tt <= 4 * qb + 3)
    cd = np.where(diag, 0.0, -np.abs(q0 - s0)).astype(np.float32).reshape(1, 1024)
    maskb = np.repeat(maskb, 128, 0)
    cd = np.repeat(cd, 128, 0)
    return ropeA, ropeB, maskb, cd


def _dtiles():
    p = np.arange(128)[:, None]
    j = np.arange(512)[None, :]
    tiles = [-(j - p), (j - p)] + [-np.abs(j - p - 128 * c) for c in range(4)]
    return np.ascontiguousarray(np.stack(tiles, 1).astype(np.float32))


_CACHE = {}


def kernel(**inp):
    f = lambda a: np.ascontiguousarray(np.asarray(a, dtype=np.float32))
    x_prompt, x_sample = f(inp["x_prompt"]), f(inp["x_sample"])
    p_prompt, p_sample = f(inp["p_prompt"]), f(inp["p_sample"])

    sw64 = np.concatenate([_swap32(64)])
    abin = f(inp["ab_w_in"])
    q_idx = np.arange(512)
    q_sw_idx = (q_idx // 64) * 64 + sw64[q_idx % 64]
    k_idx = 512 + np.arange(128)
    k_sw_idx = 512 + (np.arange(128) // 64) * 64 + sw64[np.arange(128) % 64]
    kr_idx = 1152 + np.arange(32)
    kr_sw_idx = 1152 + _swap32(32)
    cols = np.concatenate([q_idx, q_sw_idx, k_idx, k_sw_idx, 768 + np.arange(256), 1024 + np.arange(128),
                           kr_idx, kr_sw_idx, 640 + np.arange(128)])
    abin_x = np.ascontiguousarray(abin[:, :, cols])
    wuq = f(inp["b_w_uq"])
    hh = np.arange(8)[:, None]
    nope_idx = (hh * 96 + np.arange(64)[None, :]).reshape(-1)
    rope_idx = (hh * 96 + 64 + np.arange(32)[None, :]).reshape(-1)
    rope_sw_idx = (hh * 96 + 64 + _swap32(32)[None, :]).reshape(-1)
    wuq_x = np.ascontiguousarray(wuq[:, :, np.concatenate([nope_idx, rope_idx, rope_sw_idx])])
    wukv = f(inp["b_w_ukv"])
    kn_idx = (hh * 128 + np.arange(64)[None, :]).reshape(-1)
    v_idx = (hh * 128 + 64 + np.arange(64)[None, :]).reshape(-1)
    wukv_x = np.ascontiguousarray(wukv[:, :, np.concatenate([kn_idx, v_idx])])

    gall = np.zeros((128, NG), np.float32)

    def put(col, vec):
        v = np.asarray(vec, np.float32).reshape(-1, 128).T
        gall[:, col:col + v.shape[1]] = v
    for i in range(DEPTH):
        put(GCOL[("ffn1", i)], inp["ffn1_norm"][i])
        put(GCOL[("mix", i)], inp["mix_norm"][i])
        put(GCOL[("ffn2", i)], inp["ffn2_norm"][i])
        put(GCOL[("ple", i)], inp["ple_norm"][i])
    put(GCOL["final"], inp["final_norm"])
    for e in range(2):
        aq = np.asarray(inp["a_q_norm"][e], np.float32)
        ak = np.asarray(inp["a_k_norm"][e], np.float32)
        put(GCOL[("aq", e)], np.tile(aq, 2))
        put(GCOL[("aq_sw", e)], np.tile(aq[sw64], 2))
        put(GCOL[("ak", e)], np.tile(ak, 2))
        put(GCOL[("ak_sw", e)], np.tile(ak[sw64], 2))
        put(GCOL[("bq", e)], inp["b_q_norm"][e])
        put(GCOL[("bkv", e)], inp["b_kv_norm"][e])
    for o in range(2):
        put(GCOL[("sub", o)], inp["c_sub_norm"][o])
    lamv = np.stack([np.stack([f(inp["c_lambda_q1"])[o], f(inp["c_lambda_k1"])[o],
                               f(inp["c_lambda_q2"])[o], f(inp["c_lambda_k2"])[o]]) for o in range(2)])
    lamv = np.ascontiguousarray(lamv.reshape(1, 512))

    shared = {
        "f1g": f(inp["ffn1_wg"]), "f1u": f(inp["ffn1_wu"]), "f1d": f(inp["ffn1_wd"]),
        "f2g": f(inp["ffn2_wg"]), "f2u": f(inp["ffn2_wu"]), "f2d": f(inp["ffn2_wd"]),
        "abin": abin_x, "wuq": wuq_x, "wukv": wukv_x, "about": f(inp["ab_w_out"]),
        "cin": f(inp["c_w_in"]), "cout": f(inp["c_w_out"]),
        "pleg": f(inp["ple_w_gate"]), "plep": f(inp["ple_w_proj"]),
        "gall": gall, "lamv": lamv, "dtiles": _dtiles(),
    }
    tabs = {8192: _tables(8192), 2048: _tables(2048)}
    in_maps = []
    for core in range(8):
        u = core if core < N_UNITS else core - 2
        if u < 4:
            xu = x_prompt[u]
            pu = p_prompt[:, u]
            tb = tabs[8192]
        else:
            s = (u - 4) * 4
            xu = x_sample[s:s + 4].reshape(NT, D)
            pu = p_sample[:, s:s + 4].reshape(DEPTH, NT, 256)
            tb = tabs[2048]
        m = dict(shared)
        m["x"] = np.ascontiguousarray(xu)
        m["p"] = np.ascontiguousarray(pu)
        m["ropeA"], m["ropeB"], m["maskb"], m["cdist"] = tb
        in_maps.append(m)

    if "nc" not in _CACHE:
        _CACHE["nc"] = build_program()
    nc = _CACHE["nc"]
    res = run_bass_kernel_spmd(nc, in_maps, core_ids=list(range(8)))
    if DBG.get("dump"):
        _CACHE["res"] = res.results
    ys = [np.asarray(r["y"], dtype=np.float32) for r in res.results]
    y_prompt = np.stack(ys[0:4]).reshape(4, NT, D)
    y_sample = np.concatenate([ys[4].reshape(4, 2048, D), ys[5].reshape(4, 2048, D)], 0)
    return (y_prompt, y_sample)
```
